# Optimizing a Trainium2 kernel written in Bass

```python
import jax, jax.numpy as jnp
from jax import lax
import numpy as np

D_MODEL = 1024
BATCH = 8
SEQ = 4096
DEPTH = 2
DEC_BATCH = 32
DEC_SEQ = 1
PAST_LEN = 16384
PAGE_SIZE = 128

N_A = DEPTH // 2
N_B = DEPTH - N_A
CONV_EXPAND = 2
D_CONV = CONV_EXPAND * D_MODEL
CONV_WIDTH = 3
N_HEADS = 16
HEAD_DIM = D_MODEL // N_HEADS
KV_HEADS = 4
GROUP = N_HEADS // KV_HEADS
D_ATTN = N_HEADS * HEAD_DIM
N_BRANCH = 3
CMP_STRIDE = 16
CMP_LEN = 2 * CMP_STRIDE
CMP_HIDDEN = HEAD_DIM
SEL_BLOCK = 64
SEL_TOPK = 16
WINDOW = 512
Q_BLOCK = 128
EPS = 1e-6
NEG = -1e30
FORCE = 1e9
SCALE = HEAD_DIM ** -0.5

kernel_name = 'yoco_shortconv_nsa_step'


def rmsnorm(x, g):
    xf = x.astype(jnp.float32)
    y = xf * lax.rsqrt(jnp.mean(xf * xf, axis=-1, keepdims=True) + EPS)
    return y.astype(x.dtype) * g


def masked_softmax(s, mask):
    s = jnp.where(mask, s, NEG)
    e = jnp.exp(s - jnp.max(s, axis=-1, keepdims=True)) * mask
    return e / jnp.maximum(jnp.sum(e, axis=-1, keepdims=True), 1e-30)


def short_conv_layer(xn, buf, w_in, conv_w, w_out):
    b_gate, c_gate, h, z = jnp.split(xn @ w_in, 4, axis=-1)
    u = c_gate * h
    ext = jnp.concatenate([buf.astype(u.dtype), u], axis=1)
    T = u.shape[1]
    v = sum(conv_w[j] * ext[:, j:j + T] for j in range(CONV_WIDTH))
    y = (jax.nn.silu(z) * b_gate * v) @ w_out
    return y, ext[:, T:]


def shared_kv(h, norm_kv, w_kv):
    kv = rmsnorm(h, norm_kv) @ w_kv
    return kv.reshape(h.shape[0], h.shape[1], N_BRANCH, 2, KV_HEADS, HEAD_DIM)


def query_side(xn, w_qg):
    B, T = xn.shape[:2]
    proj = xn @ w_qg
    q = proj[..., :D_ATTN].reshape(B, T, N_HEADS, HEAD_DIM)
    gate = jax.nn.sigmoid(proj[..., D_ATTN:D_ATTN + N_BRANCH * N_HEADS]).reshape(B, T, N_HEADS, N_BRANCH)
    z = proj[..., D_ATTN + N_BRANCH * N_HEADS:]
    return q, gate, z


def compress_rows(rows, cmp_pe, cmp_w1, cmp_w2):
    n = rows.shape[0] // CMP_STRIDE
    ch = rows[:n * CMP_STRIDE].reshape(n, CMP_STRIDE, 2, KV_HEADS, HEAD_DIM)
    blk = jnp.concatenate([ch[:-1], ch[1:]], axis=1)
    blk = blk + jnp.transpose(cmp_pe, (1, 0, 2))[None, :, :, None, :]
    flat = jnp.transpose(blk, (0, 2, 3, 1, 4)).reshape(n - 1, 2, KV_HEADS, CMP_LEN * HEAD_DIM)
    hid = jax.nn.silu(jnp.einsum('nckf,cfe->ncke', flat, cmp_w1))
    return jnp.einsum('ncke,ced->nckd', hid, cmp_w2)


def compress_sample(new_cmp, page_table, pool_cmp, cmp_pe, cmp_w1, cmp_w2):
    past_len = page_table.shape[1] * PAGE_SIZE

    def one(args):
        pages, new_b = args
        past = pool_cmp[pages].reshape(past_len, 2, KV_HEADS, HEAD_DIM)
        return compress_rows(jnp.concatenate([past, new_b], axis=0), cmp_pe, cmp_w1, cmp_w2)

    return lax.map(one, (page_table, new_cmp))


def scores(qg, k, spec):
    return jnp.einsum(spec, qg, k).astype(jnp.float32) * SCALE


def nsa_attend(q, gate, pos, cmp_kv, fetch_sel, n_blocks, win_kv, win_pos):
    nq = q.shape[0]
    n_c = cmp_kv.shape[0]
    qg = q.reshape(nq, KV_HEADS, GROUP, HEAD_DIM)
    c_end = CMP_STRIDE * jnp.arange(n_c) + (CMP_LEN - 1)
    p_cmp = masked_softmax(scores(qg, cmp_kv[:, 0], 'qhgd,nhd->qhgn'),
                           (c_end[None, :] <= pos[:, None])[:, None, None, :])
    o_cmp = jnp.einsum('qhgn,nhd->qhgd', p_cmp.astype(q.dtype), cmp_kv[:, 1])
    ratio = SEL_BLOCK // CMP_STRIDE
    ci = jnp.arange(n_c)[:, None]
    bj = jnp.arange(n_blocks)[None, :]
    member = ((ci >= ratio * bj - (CMP_LEN // CMP_STRIDE - 1)) & (ci <= ratio * bj + ratio - 1)).astype(jnp.float32)
    imp = jnp.einsum('qhn,nj->qhj', p_cmp.sum(axis=2), member)
    valid = bj * SEL_BLOCK <= pos[:, None]
    cur = (pos // SEL_BLOCK)[:, None]
    forced = (bj == 0) | (bj == cur) | (bj == cur - 1)
    score = jnp.where((valid & forced)[:, None], FORCE, jnp.where(valid[:, None], imp, -FORCE))
    top_s, idx = lax.top_k(score, min(SEL_TOPK, n_blocks))
    key_pos = idx[..., None] * SEL_BLOCK + jnp.arange(SEL_BLOCK)
    sel = fetch_sel(key_pos).reshape(nq, KV_HEADS, -1, 2, HEAD_DIM)
    sel_mask = ((top_s > -0.5 * FORCE)[..., None] & (key_pos <= pos[:, None, None, None])).reshape(nq, KV_HEADS, 1, -1)
    p_sel = masked_softmax(scores(qg, sel[..., 0, :], 'qhgd,qhnd->qhgn'), sel_mask)
    o_sel = jnp.einsum('qhgn,qhnd->qhgd', p_sel.astype(q.dtype), sel[..., 1, :])
    d_pos = pos[:, None] - win_pos[None, :]
    w_mask = (win_pos[None, :] >= 0) & (d_pos >= 0) & (d_pos <= WINDOW)
    p_win = masked_softmax(scores(qg, win_kv[:, 0], 'qhgd,nhd->qhgn'), w_mask[:, None, None, :])
    o_win = jnp.einsum('qhgn,nhd->qhgd', p_win.astype(q.dtype), win_kv[:, 1])
    g = gate.reshape(nq, KV_HEADS, GROUP, N_BRANCH)
    o = g[..., 0:1] * o_cmp + g[..., 1:2] * o_sel + g[..., 2:3] * o_win
    return o.reshape(nq, D_ATTN)


def nsa_prompt(q, gate, cmp_tok, sel_kv, win_kv):
    B, S = q.shape[:2]
    n_chunks = S // Q_BLOCK
    n_blocks = -(-S // SEL_BLOCK)
    hidx = jnp.arange(KV_HEADS)[None, :, None, None]
    win_pad = jnp.pad(win_kv, ((0, 0), (WINDOW, 0), (0, 0), (0, 0), (0, 0)))

    def one(bc):
        b, c = bc
        c0 = c * Q_BLOCK
        pos = c0 + jnp.arange(Q_BLOCK)
        q_c = lax.dynamic_slice_in_dim(q[b], c0, Q_BLOCK, axis=0)
        g_c = lax.dynamic_slice_in_dim(gate[b], c0, Q_BLOCK, axis=0)
        sel_b = sel_kv[b]

        def fetch(kp):
            return sel_b[kp, :, hidx, :]

        win_c = lax.dynamic_slice_in_dim(win_pad[b], c0, WINDOW + Q_BLOCK, axis=0)
        win_pos = c0 - WINDOW + jnp.arange(WINDOW + Q_BLOCK)
        return nsa_attend(q_c, g_c, pos, cmp_tok[b], fetch, n_blocks, win_c, win_pos)

    bi = jnp.repeat(jnp.arange(B), n_chunks)
    ci = jnp.tile(jnp.arange(n_chunks), B)
    return lax.map(one, (bi, ci)).reshape(B, S, D_ATTN)


def nsa_sample(q, gate, cmp_tok, new_sel, win_all, page_table, pool_sel):
    T = q.shape[1]
    n_pages = page_table.shape[1]
    past_len = n_pages * PAGE_SIZE
    n_blocks = -(-(past_len + T) // SEL_BLOCK)
    pos = past_len + jnp.arange(T)
    n_w = win_all.shape[1]
    win_pos = past_len + T - n_w + jnp.arange(n_w)
    hidx = jnp.arange(KV_HEADS)[None, :, None, None]

    def one(q_b, g_b, cmp_b, new_b, win_b, pages):
        def fetch(kp):
            in_past = kp < past_len
            page = pages[jnp.clip(kp // PAGE_SIZE, 0, n_pages - 1)]
            old = pool_sel[page, kp % PAGE_SIZE, :, hidx, :]
            new = new_b[jnp.clip(kp - past_len, 0, T - 1), :, hidx, :]
            return jnp.where(in_past[..., None, None], old, new)

        return nsa_attend(q_b, g_b, pos, cmp_b, fetch, n_blocks, win_b, win_pos)

    return jax.vmap(one)(q, gate, cmp_tok, new_sel, win_all, page_table)


def setup_inputs(seed: int = 0) -> dict:
    key = jax.random.key(seed)
    ks = jax.random.split(key, 20)
    n_pages = PAST_LEN // PAGE_SIZE
    n_used = DEC_BATCH * n_pages
    n_pool = n_used + max(1, n_used // 4)
    win_keep = min(WINDOW, PAST_LEN)

    def nrm(k, shape, scale):
        return jax.random.normal(k, shape, jnp.float32) * scale

    page_table = jax.random.permutation(ks[0], n_pool)[:n_used].reshape(DEC_BATCH, n_pages).astype(jnp.int32)
    return {
        'x_prompt': nrm(ks[1], (BATCH, SEQ, D_MODEL), 1.0),
        'x_sample': nrm(ks[2], (DEC_BATCH, DEC_SEQ, D_MODEL), 1.0),
        'cache_conv': nrm(ks[3], (N_A, DEC_BATCH, CONV_WIDTH - 1, D_CONV), 1.0),
        'cache_cmp_kv': nrm(ks[4], (n_pool, PAGE_SIZE, 2, KV_HEADS, HEAD_DIM), 1.0),
        'cache_sel_kv': nrm(ks[5], (n_pool, PAGE_SIZE, 2, KV_HEADS, HEAD_DIM), 1.0),
        'cache_win_kv': nrm(ks[6], (DEC_BATCH, win_keep, 2, KV_HEADS, HEAD_DIM), 1.0),
        'page_table': page_table,
        'norm_a': 1.0 + nrm(ks[7], (N_A, D_MODEL), 0.01),
        'conv_w_in': nrm(ks[8], (N_A, D_MODEL, 4 * D_CONV), D_MODEL ** -0.5),
        'conv_w': nrm(ks[9], (N_A, CONV_WIDTH, D_CONV), CONV_WIDTH ** -0.5),
        'conv_w_out': nrm(ks[10], (N_A, D_CONV, D_MODEL), D_CONV ** -0.5),
        'norm_kv': 1.0 + nrm(ks[11], (D_MODEL,), 0.01),
        'w_kv': nrm(ks[12], (D_MODEL, N_BRANCH * 2 * KV_HEADS * HEAD_DIM), D_MODEL ** -0.5),
        'cmp_pe': nrm(ks[13], (2, CMP_LEN, HEAD_DIM), 0.1),
        'cmp_w1': nrm(ks[14], (2, CMP_LEN * HEAD_DIM, CMP_HIDDEN), (CMP_LEN * HEAD_DIM) ** -0.5),
        'cmp_w2': nrm(ks[15], (2, CMP_HIDDEN, HEAD_DIM), CMP_HIDDEN ** -0.5),
        'norm_b': 1.0 + nrm(ks[16], (N_B, D_MODEL), 0.01),
        'w_qg': nrm(ks[17], (N_B, D_MODEL, 2 * D_ATTN + N_BRANCH * N_HEADS), D_MODEL ** -0.5),
        'w_o': nrm(ks[18], (N_B, D_ATTN, D_MODEL), D_ATTN ** -0.5),
        'norm_f': 1.0 + nrm(ks[19], (D_MODEL,), 0.01),
    }


def reference(x_prompt, x_sample, cache_conv, cache_cmp_kv, cache_sel_kv, cache_win_kv, page_table,
              norm_a, conv_w_in, conv_w, conv_w_out, norm_kv, w_kv, cmp_pe, cmp_w1, cmp_w2,
              norm_b, w_qg, w_o, norm_f):
    h_p, h_s = x_prompt, x_sample
    conv_p, conv_s = [], []
    for layer in range(DEPTH):
        if layer < N_A:
            buf0 = jnp.zeros((h_p.shape[0], CONV_WIDTH - 1, D_CONV), h_p.dtype)
            y_p, st_p = short_conv_layer(rmsnorm(h_p, norm_a[layer]), buf0,
                                         conv_w_in[layer], conv_w[layer], conv_w_out[layer])
            y_s, st_s = short_conv_layer(rmsnorm(h_s, norm_a[layer]), cache_conv[layer],
                                         conv_w_in[layer], conv_w[layer], conv_w_out[layer])
            h_p = h_p + y_p
            h_s = h_s + y_s
            conv_p.append(st_p)
            conv_s.append(st_s)
            if layer == N_A - 1:
                kv_p = shared_kv(h_p, norm_kv, w_kv)
                kv_s = shared_kv(h_s, norm_kv, w_kv)
                cmp_tok_p = lax.map(lambda rows: compress_rows(rows, cmp_pe, cmp_w1, cmp_w2), kv_p[:, :, 0])
                cmp_tok_s = compress_sample(kv_s[:, :, 0], page_table, cache_cmp_kv, cmp_pe, cmp_w1, cmp_w2)
                win_all_s = jnp.concatenate([cache_win_kv, kv_s[:, :, 2]], axis=1)
        else:
            i = layer - N_A
            q_p, g_p, z_p = query_side(rmsnorm(h_p, norm_b[i]), w_qg[i])
            o_p = nsa_prompt(q_p, g_p, cmp_tok_p, kv_p[:, :, 1], kv_p[:, :, 2])
            h_p = h_p + (o_p * jax.nn.silu(z_p)) @ w_o[i]
            q_s, g_s, z_s = query_side(rmsnorm(h_s, norm_b[i]), w_qg[i])
            o_s = nsa_sample(q_s, g_s, cmp_tok_s, kv_s[:, :, 1], win_all_s, page_table, cache_sel_kv)
            h_s = h_s + (o_s * jax.nn.silu(z_s)) @ w_o[i]
    y_prompt = rmsnorm(h_p, norm_f)
    y_sample = rmsnorm(h_s, norm_f)
    new_conv_p = jnp.stack(conv_p)
    new_conv_s = jnp.stack(conv_s)
    keep_p = min(WINDOW, x_prompt.shape[1])
    keep_s = min(WINDOW, win_all_s.shape[1])
    return (y_prompt, y_sample, new_conv_p, new_conv_s, kv_p[:, :, 0], kv_s[:, :, 0], kv_p[:, :, 1], kv_s[:, :, 1], kv_p[:, -keep_p:, 2], win_all_s[:, -keep_s:])
```

```python
import numpy as np
import concourse.bass as bass
import concourse.mybir as mybir
from concourse.bass_utils import run_bass_kernel_spmd

F32 = mybir.dt.float32
BF16 = mybir.dt.bfloat16
I32 = mybir.dt.int32
AF = mybir.ActivationFunctionType
ALU = mybir.AluOpType

NCORES = 8
D = 1024
SEQ = 4096
ST = 512
NST = SEQ // ST
DC = 2048
NCC = DC // 128
KVW = 1536
QGW = 2096
EPS = 1e-6
DEC_B = 32
SB = DEC_B // NCORES


class _Op:
    __slots__ = ("eng", "fn", "deps", "is_dma", "sem", "cum", "needs_inc", "count", "name", "inc")


class Sched:
    ENGS = ("pe", "act", "dve", "pool", "sp")

    def __init__(self, nc):
        self.nc = nc
        self.ops = {e: [] for e in self.ENGS}
        self.last_w = {}
        self.readers = {}
        self.dma_cum = {}
        self.out_dmas = []

    def op(self, eng, fn, reads=(), writes=(), dma=None, is_out=False, inc=16):
        o = _Op()
        o.eng = eng
        o.fn = fn
        o.is_dma = dma is not None
        o.sem = dma
        o.count = None
        deps = {}
        def add(d):
            if d is o:
                return
            if (not d.is_dma) and d.eng == "pe" and eng == "pe" and dma is None:
                return
            deps[id(d)] = d
        for t in reads:
            w = self.last_w.get(t)
            if w is not None:
                add(w)
        for t in writes:
            w = self.last_w.get(t)
            if w is not None:
                add(w)
            for r in self.readers.get(t, ()):
                add(r)
        o.deps = list(deps.values())
        for t in reads:
            self.readers.setdefault(t, []).append(o)
        for t in writes:
            self.last_w[t] = o
            self.readers[t] = []
        if o.is_dma:
            self.dma_cum[dma] = self.dma_cum.get(dma, 0) + inc
            o.inc = inc
            o.cum = self.dma_cum[dma]
            if is_out:
                self.out_dmas.append(o)
        o.needs_inc = o.is_dma
        for d in o.deps:
            d.needs_inc = True
        self.ops[eng].append(o)
        return o

    def barrier(self):
        lasts = []
        for e in self.ENGS:
            for o in reversed(self.ops[e]):
                if (not o.is_dma) and o.fn is not None:
                    o.needs_inc = True
                    lasts.append(o)
                    break
        seen = set()
        for e in self.ENGS:
            for o in reversed(self.ops[e]):
                if o.is_dma and o.sem not in seen:
                    seen.add(o.sem)
                    lasts.append(o)
        for e in self.ENGS:
            b = _Op()
            b.eng = e
            b.fn = None
            b.is_dma = False
            b.sem = None
            b.count = None
            b.needs_inc = False
            b.deps = list(lasts)
            self.ops[e].append(b)

    def emit(self):
        nc = self.nc
        eng_sem = {e: nc.alloc_semaphore("sem_" + e) for e in ("pe", "act", "dve", "pool")}
        dma_sem = {k: nc.alloc_semaphore("dsem_" + str(k)) for k in self.dma_cum}
        for e in self.ENGS:
            c = 0
            for o in self.ops[e]:
                if o.needs_inc and not o.is_dma:
                    c += 1
                    o.count = c
        final_waits = {}
        for o in self.out_dmas:
            final_waits[o.sem] = max(final_waits.get(o.sem, 0), self.dma_cum[o.sem])

        def run(e, engine):
            known = {}
            for o in self.ops[e]:
                need = {}
                for d in o.deps:
                    if d.is_dma:
                        key = ("d", d.sem)
                        val = d.cum
                    else:
                        key = ("e", d.eng)
                        val = d.count
                    if need.get(key, 0) < val:
                        need[key] = val
                for key, val in need.items():
                    if known.get(key, 0) >= val:
                        continue
                    known[key] = val
                    sem = dma_sem[key[1]] if key[0] == "d" else eng_sem[key[1]]
                    engine.wait_ge(sem, val)
                if o.fn is None:
                    continue
                ins = o.fn(engine)
                if o.needs_inc:
                    if o.is_dma:
                        ins.then_inc(dma_sem[o.sem], o.inc)
                    else:
                        ins.then_inc(eng_sem[o.eng], 1)
            if e == "sp":
                for k, v in final_waits.items():
                    engine.wait_ge(dma_sem[k], v)

        with nc.Block() as block:
            @block.tensor
            def _(eng):
                run("pe", eng)

            @block.scalar
            def _(eng):
                run("act", eng)

            @block.vector
            def _(eng):
                run("dve", eng)

            @block.gpsimd
            def _(eng):
                run("pool", eng)

            @block.sync
            def _(eng):
                run("sp", eng)


BIG = 30000.0
SCALE = 0.125
C_ID, C_TRIB, C_TRIU, C_MUL, C_ADD, C_VAL, C_MEM, C_STAIR, C_ZSEL, C_EB = 0, 128, 256, 384, 512, 640, 768, 896, 1024, 1288
C_MCOL = C_EB + 4096
C_IOTAR = C_MCOL + 4
C_W = C_IOTAR + 128
SBC = 16
NMEM = 8 * 257


def make_consts():
    c = np.zeros((128, C_W), np.float32)
    c[:, C_ID:C_ID + 128] = np.eye(128)
    kk = np.arange(128)[:, None]
    ql = np.arange(128)[None, :]
    c[:, C_TRIB:C_TRIB + 128] = np.where(kk <= ql, 0.0, -BIG)
    c[:, C_TRIU:C_TRIU + 128] = np.where(kk >= ql, 0.0, -BIG)
    qq = np.arange(128)[:, None]
    jrel = np.arange(128)[None, :] - 64
    lo = qq < 64
    mul = np.zeros((128, 128), np.float32)
    add = np.zeros((128, 128), np.float32)
    mul[:] = np.where(jrel <= -2, 1.0, 0.0)
    mul += np.where((jrel == -1) & ~lo, 1.0, 0.0)
    add += np.where(jrel >= 2, -1e9, 0.0)
    add += np.where((jrel == -1) & lo, 1e9, 0.0)
    add += np.where((jrel == 0) & lo, 2e9, 0.0)
    add += np.where((jrel == 0) & ~lo, 1e9, 0.0)
    add += np.where((jrel == 1) & lo, -1e9, 0.0)
    add += np.where((jrel == 1) & ~lo, 2e9, 0.0)
    c[:, C_MUL:C_MUL + 128] = mul
    c[:, C_ADD:C_ADD + 128] = add
    c[:, C_VAL:C_VAL + 128] = (add > -0.5e9).astype(np.float32)
    for ch in range(2):
        n = ch * 128 + np.arange(128)[:, None]
        j = np.arange(64)[None, :]
        m = ((n >= 4 * j - 1) & (n <= 4 * j + 3) & (n < 255)).astype(np.float32)
        c[:, C_MEM + ch * 64:C_MEM + (ch + 1) * 64] = m
    for k in range(8):
        rel = k - 1
        c[k, C_STAIR:C_STAIR + 128] = np.where(np.arange(128) >= 16 * rel + 31, 0.0, -BIG)
        c[k, C_ZSEL + 128 + k] = 1.0
    for j in range(64):
        c[j, C_EB + 64 * j:C_EB + 64 * j + 64] = 1.0
        c[64 + j, C_EB + 64 * j:C_EB + 64 * j + 64] = 1.0
    p = np.arange(128)
    c[:, C_MCOL + 0] = (p <= 64)
    c[:, C_MCOL + 1] = (p == 0)
    c[:, C_MCOL + 2] = p % 64
    c[0, C_IOTAR:C_IOTAR + 128] = np.arange(128)
    return c


def make_mems():
    m = np.zeros((128, 8, 257), np.float32)
    for ch in range(8):
        n = ch * 128 + np.arange(128)[:, None]
        j = np.arange(257)[None, :]
        m[:, ch, :] = ((n >= 4 * j - 1) & (n <= 4 * j + 3) & (n < 1023))
    return m.reshape(128, NMEM)


def build_program(nst=NST, with_b=True, with_s=True, use_cc=True):
    import os
    DBG_S = os.environ.get("KDBG", "")
    from contextlib import ExitStack
    nc = bass.Bass("TRN2", target_bir_lowering=False)
    S = Sched(nc)

    def din(name, shape, dt=F32):
        return nc.dram_tensor(name, list(shape), dt, kind="ExternalInput").ap()

    def dout(name, shape, dt=F32):
        return nc.dram_tensor(name, list(shape), dt, kind="ExternalOutput").ap()

    def dscr(name, shape, dt):
        return nc.dram_tensor(name, list(shape), dt).ap()

    x_p = din("x_p", [SEQ, D])
    norm_a = din("norm_a", [D])
    w_in = din("w_in", [D, 4 * DC])
    conv_w = din("conv_w", [3, DC])
    w_out = din("w_out", [DC, D])
    norm_kv = din("norm_kv", [D])
    w_kv = din("w_kv", [D, KVW])
    cmp_pe = din("cmp_pe", [2, 32, 64])
    cmp_w1 = din("cmp_w1", [2, 2048, 64])
    cmp_w2 = din("cmp_w2", [2, 64, 64])
    norm_b = din("norm_b", [D])
    w_qg = din("w_qg", [D, QGW])
    w_o = din("w_o", [D, D])
    norm_f = din("norm_f", [D])
    consts = din("consts", [128, C_W])
    if with_s:
        NPOOL = 8 if "tinypool" in DBG_S else 5120
        if "ccdbg" in DBG_S:
            o_dpre = dout("o_dpre", [SBC, D])
            o_dpost = dout("o_dpost", [SBC, D])
        x_s = din("x_s", [SBC, D])
        cconv_s = din("cconv_s", [SBC, 2, DC])
        pool_cmp = din("pool_cmp", [NPOOL * 8, 2048])
        pool_sel = din("pool_sel", [NPOOL * 128, 128])
        win_s = din("win_s", [SBC, 512, 128])
        ptab_in = din("ptab_in", [SBC, 128], I32)
        wqs_in = din("wqs_in", [D, 524])
        wkvs_in = din("wkvs_in", [D, 384])
        wos_in = din("wos_in", [256, D])
        mems_in = din("mems_in", [128, NMEM])
        o_y_s = dout("o_y_s", [SBC, D])
        o_conv_s = dout("o_conv_s", [SBC, 2, DC])
        o_kv_s = dout("o_kv_s", [SBC, 384])
        o_win_s = dout("o_win_s", [SBC, 512, 128])
        if not use_cc:
            o_part = dout("o_part", [SBC, D])
        s_wqs = dscr("s_wqs", [128, 8, 524], BF16)
        s_wkvs = dscr("s_wkvs", [128, 8, 384], BF16)
        s_wos = dscr("s_wos", [128, 2, D], BF16)
        s_mems = dscr("s_mems", [128, NMEM], BF16)
        s_o3 = dscr("s_o3", [SBC, 4, 3, 65], F32)
        cc_in = dscr("cc_in", [SBC, D], F32)
        s_gbi = dscr("s_gbi", [SBC, 16], I32)
        cc_out = dscr("cc_out", [SBC, D], F32)

    o_y_p = dout("o_y_p", [SEQ, D])
    o_conv_p = dout("o_conv_p", [2, DC])
    o_cmp_p = dout("o_cmp_p", [SEQ, 512])
    o_sel_p = dout("o_sel_p", [SEQ, 512])
    o_win_p = dout("o_win_p", [512, 512])

    s_h = dscr("s_h", [SEQ, D], F32)
    s_hnT = dscr("s_hnT", [NST, 128, 8 * ST], BF16)
    s_win = dscr("s_win", [NCC, 128, 8, 512], BF16)
    s_wkv = dscr("s_wkv", [3, 128, 8, 512], BF16)
    s_wout = dscr("s_wout", [128, NCC, D], BF16)
    s_wq = dscr("s_wq", [128, 8, 1024], BF16)
    s_wgz = dscr("s_wgz", [128, 8, 1072], BF16)
    s_wo = dscr("s_wo", [128, 8, 1024], BF16)
    s_w1 = dscr("s_w1", [2, 128, 2048], BF16)
    s_eb = dscr("s_eb", [128, 4096], BF16)

    def sbp(name, shape, dt=F32):
        return nc.alloc_sbuf_tensor(name, list(shape), dt)

    cb = sbp("cb", [128, C_EB], BF16)
    eps_sb = sbp("eps_sb", [128, 1])
    stat = sbp("stat", [128, 32])
    ident = cb[:, C_ID:C_ID + 128]

    psn = {}
    for i in (0, 1, 2, 3, 5, 6, 7):
        psn[i] = nc.alloc_psum_tensor("ps%d" % i, [128, 512], F32)
    ps_tr = nc.alloc_psum_tensor("ps_tr", [128, 1024], BF16)

    def dma(eng, out, in_, reads, writes, sem, is_out=False, **kw):
        return S.op(eng, lambda e: e.dma_start(out=out, in_=in_, **kw), reads, writes, dma=sem, is_out=is_out)

    def mm(out, lhsT, rhs, start, stop, reads, writes):
        return S.op("pe", lambda e: e.matmul(out, lhsT, rhs, start=start, stop=stop), reads, writes)

    def tr(out, in_, reads, writes):
        kk = in_.shape[0]
        return S.op("pe", lambda e: e.transpose(out, in_, cb[0:kk, C_ID:C_ID + kk]), list(reads) + ["cb"], writes)

    trf_ident = [None]

    def trf(out, in_, reads, writes):
        kk = in_.shape[0]
        idf = trf_ident[0]
        return S.op("pe", lambda e: e.transpose(out, in_, idf[0:kk, 0:kk]), list(reads) + ["identf"], writes)

    def act(out, in_, func, reads, writes, **kw):
        return S.op("act", lambda e: e.activation(out, in_, func, **kw), reads, writes)

    def tt(eng, out, in0, in1, op, reads, writes):
        return S.op(eng, lambda e: e.tensor_tensor(out, in0, in1, op), reads, writes)

    def ts(eng, out, in0, s1, s2, op0, op1, reads, writes):
        if s2 is None:
            return S.op(eng, lambda e: e.tensor_scalar(out, in0, s1, None, op0), reads, writes)
        return S.op(eng, lambda e: e.tensor_scalar(out, in0, s1, s2, op0, op1), reads, writes)

    def stt(eng, out, in0, sc, in1, op0, op1, reads, writes):
        return S.op(eng, lambda e: e.scalar_tensor_tensor(out, in0, sc, in1, op0, op1), reads, writes)

    def cp(eng, out, in_, reads, writes):
        if eng == "act":
            return act(out, in_, AF.Copy, reads, writes)
        return S.op(eng, lambda e: e.tensor_copy(out, in_), reads, writes)

    def memset(eng, ap, val, writes):
        return S.op(eng, lambda e: e.memset(ap, val), [], writes)

    with ExitStack() as es:
        def sb(name, shape, dt=F32):
            return es.enter_context(nc.sbuf_tensor(name, list(shape), dt))
        cst = sb("cst", [128, C_W])
        stage = [sb("stage%d" % i, [128, 2304]) for i in range(2)]
        stageb = [sb("stageb%d" % i, [128, 2304], BF16) for i in range(2)]
        na_sb = sb("na_sb", [128, 8])
        nkv_sb = sb("nkv_sb", [128, 8])
        nb_sb = sb("nb_sb", [128, 8])

        dma("sp", cst[:], consts, [], ["cst"], "c_cst")
        cp("dve", cb[:], cst[:, 0:C_EB], ["cst"], ["cb"])
        ebst = sb("ebst", [128, 4096], BF16)
        cp("act", ebst[:], cst[:, C_EB:C_EB + 4096], ["cst"], ["ebst"])
        dma("pool", s_eb, ebst[:], ["ebst"], ["s_w"], "c_ebst")
        dma("sp", na_sb[:], norm_a.rearrange("(k p) -> p k", p=128), [], ["na"], "c_na", allow_slow_non_contiguous=True)
        dma("sp", nkv_sb[:], norm_kv.rearrange("(k p) -> p k", p=128), [], ["nkv"], "c_nkv", allow_slow_non_contiguous=True)
        dma("sp", nb_sb[:], norm_b.rearrange("(k p) -> p k", p=128), [], ["nb"], "c_nb", allow_slow_non_contiguous=True)
        memset("pool", eps_sb[:], EPS, ["eps"])

        prep_i = [0]

        def prep(src_ap, width, scale_ap, scale_tok, dst_ap, dst_tok, src_view=None, permute_q=False):
            i = prep_i[0] % 2
            n = prep_i[0]
            prep_i[0] += 1
            st_t, sb_t = "stage%d" % i, "stageb%d" % i
            dma("sp", stage[i][:, 0:width], src_ap, [], [st_t], "prep_ld%d" % i)
            rd = [st_t] + ([scale_tok] if scale_tok else [])
            o_, i_ = stageb[i][:, 0:width], stage[i][:, 0:width]
            if permute_q:
                for a in range(2):
                    ov = stageb[i][:, a * 512:(a + 1) * 512].rearrange("p (g h d) -> p g h d", g=4, h=2, d=64)
                    iv = stage[i][:, a * 512:(a + 1) * 512].rearrange("p (h g d) -> p g h d", g=4, h=2, d=64)
                    ts("dve" if a == 0 else "pool", ov, iv, scale_ap, None, ALU.mult, None, rd, [sb_t])
            elif scale_ap is None:
                cp("dve" if n % 2 == 0 else "pool", o_, i_, rd, [sb_t])
            elif n % 4 < 2:
                ts("dve", o_, i_, scale_ap, None, ALU.mult, None, rd, [sb_t])
            else:
                act(o_, i_, AF.Copy, rd, [sb_t], scale=scale_ap)
            src = o_ if src_view is None else src_view(o_)
            dma("pool", dst_ap, src, [sb_t], [dst_tok], "prep_st%d" % i)

        for kc in range(8):
            for part in range(4):
                prep(w_in[kc * 128:(kc + 1) * 128, part * DC:(part + 1) * DC], DC, na_sb[:, kc:kc + 1], "na",
                     s_win[:, :, kc, part * 128:(part + 1) * 128].rearrange("c p j -> p c j"), "s_w",
                     src_view=lambda a: a.rearrange("p (c j) -> p c j", j=128))
        for cc in range(NCC):
            prep(w_out[cc * 128:(cc + 1) * 128, :], D, None, None, s_wout[:, cc, :], "s_w")
        for kc in range(8):
            prep(w_kv[kc * 128:(kc + 1) * 128, :], KVW, nkv_sb[:, kc:kc + 1], "nkv",
                 s_wkv[:, :, kc, :].rearrange("b p j -> p b j"), "s_w",
                 src_view=lambda a: a.rearrange("p (b j) -> p b j", j=512))
        for kc in range(8):
            prep(w_qg[kc * 128:(kc + 1) * 128, 0:1024], 1024, nb_sb[:, kc:kc + 1], "nb", s_wq[:, kc, :], "s_w", permute_q=True)
            prep(w_qg[kc * 128:(kc + 1) * 128, 1024:QGW], 1072, nb_sb[:, kc:kc + 1], "nb", s_wgz[:, kc, :], "s_w")
            prep(w_o[kc * 128:(kc + 1) * 128, :], D, None, None, s_wo[:, kc, :], "s_w")
        if with_s:
            for kc in range(8):
                prep(wqs_in[kc * 128:(kc + 1) * 128, :], 524, nb_sb[:, kc:kc + 1], "nb", s_wqs[:, kc, :], "s_w")
                prep(wkvs_in[kc * 128:(kc + 1) * 128, :], 384, nkv_sb[:, kc:kc + 1], "nkv", s_wkvs[:, kc, :], "s_w")
            for kc in range(2):
                prep(wos_in[kc * 128:(kc + 1) * 128, :], D, None, None, s_wos[:, kc, :], "s_w")
            prep(mems_in, NMEM, None, None, s_mems, "s_w")
        for c in range(2):
            i = prep_i[0] % 2
            prep_i[0] += 1
            for hf in range(2):
                dma("sp", stage[i][hf * 64:(hf + 1) * 64, 0:2048].rearrange("d (j e) -> d j e", e=64),
                    cmp_w1[c].rearrange("(j d) e -> d j e", d=64), [], ["stage%d" % i], "prep_ld%d" % i)
            cp("dve", stageb[i][:, 0:2048], stage[i][:, 0:2048], ["stage%d" % i], ["stageb%d" % i])
            dma("pool", s_w1[c], stageb[i][:, 0:2048], ["stageb%d" % i], ["s_w"], "prep_st%d" % i)
        S.barrier()

    if with_s:
      with ExitStack() as es:
        def sb(name, shape, dt=F32):
            return es.enter_context(nc.sbuf_tensor(name, list(shape), dt))
        NB = SBC
        xs_sb = sb("xs_sb", [NB, D])
        hs_sb = sb("hs_sb", [NB, D])
        xsn = sb("xsn", [NB, D], BF16)
        xsT = sb("xsT", [128, 8, NB], BF16)
        hsT = sb("hsT", [128, 8, NB], BF16)
        wch = [sb("wch%d" % i, [128, 8, 512], BF16) for i in range(2)]
        woc = [sb("woc%d" % i, [128, D], BF16) for i in range(2)]
        cw_s = sb("cw_s", [128, NCC, 3])
        ccin = [sb("ccin%d" % i, [2 * NB, 128]) for i in range(2)]
        ccT = sb("ccT", [128, NCC, 2 * NB])
        uT = sb("uT", [128, NCC, NB])
        u_tok = [sb("u_tok%d" % i, [NB, 128]) for i in range(2)]
        sm = [sb("sm%d" % i, [128, 4 * NB]) for i in range(6)]
        gTs = sb("gTs", [128, NB], BF16)
        wkvs_sb = sb("wkvs_sb", [128, 8, 384], BF16)
        wqs_sb = sb("wqs_sb", [128, 8, 524], BF16)
        wos_sb = sb("wos_sb", [128, 2, D], BF16)
        kvs_sb = sb("kvs_sb", [NB, 384])
        KnT = sb("KnT", [64, 3, NB], BF16)
        QsT = sb("QsT", [64, 4, NB], BF16)
        gz_sb = sb("gz_sb", [NB, 268])
        nf_s = sb("nf_s", [NB, D])
        ptab = sb("ptab", [128, NB], I32)
        ptf = sb("ptf", [128, NB])
        idx8 = sb("idx8", [128, 8], I32)
        idxf = sb("idxf", [128, 8])
        stg = [sb("stg%d" % i, [128, 2048]) for i in range(2)]
        stgb = [sb("stgb%d" % i, [128, 2048], BF16) for i in range(2)]
        RT = sb("RT", [128, 16384 + 128], BF16)
        W1c = sb("W1c", [128, 32, 128], BF16)
        W2c = sb("W2c", [128, 128], BF16)
        W2cf = sb("W2cf", [128, 64])
        pebc = sb("pebc", [128, 1])
        peTc = sb("peTc", [128, 32], BF16)
        peTcf = sb("peTcf", [128, 32])
        hidS = sb("hidS", [128, 1024], BF16)
        KcS = sb("KcS", [128, 1024], BF16)
        VcS = sb("VcS", [128, 8, 65], BF16)
        mems = sb("mems", [128, 8, 257], BF16)
        PcS = sb("PcS", [128, 8, 4], BF16)
        impn = sb("impn", [4, 257])
        scr_s = sb("scr_s", [1, 257])
        scr_s2 = sb("scr_s2", [1, 257])
        m8s = sb("m8s", [1, 16])
        i8s = sb("i8s", [1, 16], mybir.dt.uint32)
        jf = sb("jf", [1, 16])
        jh = sb("jh", [1, 16])
        gbf = sb("gbf", [1, 16])
        gidx = sb("gidx", [128, 8], I32)
        pt1 = sb("pt1", [1, 128 + 8], I32)
        pt1f = sb("pt1f", [1, 128 + 8])
        onehot = sb("onehot", [1, 128])
        pgf = sb("pgf", [1, 16])
        gbi = sb("gbi", [1, 16], I32)
        identf = sb("identf", [128, 128])
        ones4 = sb("ones4", [4, 1])
        trf_ident[0] = identf
        iota_row = sb("iota_row", [1, 128])
        graw = sb("graw", [128, 8], I32)
        grawf = sb("grawf", [128, 8])
        selt = sb("selt", [128, 8, 128])
        selb = sb("selb", [128, 8, 129], BF16)
        KsT = sb("KsT", [64, 8, 128], BF16)
        PsS = sb("PsS", [128, 8, 4], BF16)
        wint = sb("wint", [128, 5, 128])
        winb = sb("winb", [128, 5, 129], BF16)
        KwT = sb("KwT", [64, 5, 128], BF16)
        PwS = sb("PwS", [128, 5, 4], BF16)
        o3 = sb("o3", [4, 3, 65])
        Oraw = sb("Oraw", [NB, 4, 3, 65])
        rzs_ = sb("rzs_", [NB, 4, 3])
        coefs = sb("coefs", [NB, 4, 3])
        Os = sb("Os", [NB, 256])
        Otm = sb("Otm", [NB, 256])
        ozs = sb("ozs", [NB, 256], BF16)
        ozsT = sb("ozsT", [128, 2, NB], BF16)
        part_sb = sb("part_sb", [NB, D])
        ys_sb = xs_sb
        mcol = sb("mcol", [128, 4])

        dma("sp", xs_sb[:], x_s, [], ["xs"], "s_xs")
        for j in range(3):
            dma("sp", cw_s[:, :, j], conv_w[j, :].rearrange("(c p) -> p c", p=128), [], ["cw_s"], "s_cw", allow_slow_non_contiguous=True)
        dma("sp", wkvs_sb[:], s_wkvs, [], ["wkvs"], "s_wkvs")
        dma("sp", wqs_sb[:], s_wqs, [], ["wqs"], "s_wqs")
        dma("sp", wos_sb[:], s_wos, [], ["wos"], "s_wos")
        dma("sp", nf_s[:], norm_f.partition_broadcast(NB), [], ["nf_s"], "s_nf")
        dma("sp", ptab[:], ptab_in.rearrange("b i -> i b"), [], ["ptab"], "s_pt", allow_slow_non_contiguous=True)
        dma("sp", mems[:].rearrange("p a j -> p (a j)"), s_mems, [], ["mems"], "s_mems")
        dma("sp", mcol[:], consts[:, C_MCOL:C_MCOL + 4], [], ["mcol"], "s_mcol")
        dma("sp", identf[:], consts[:, C_ID:C_ID + 128], [], ["identf"], "s_identf")
        dma("sp", iota_row[:], consts[0:1, C_IOTAR:C_IOTAR + 128], [], ["iota_row"], "s_iotar")
        memset("pool", ones4[:], 1.0, ["ones4"])
        memset("pool", W1c[:], 0.0, ["W1c"])
        memset("pool", W2c[:], 0.0, ["W2c"])
        memset("pool", VcS[:], 1.0, ["VcS"])
        memset("pool", selb[:], 1.0, ["selb"])
        memset("pool", winb[:], 1.0, ["winb"])
        memset("pool", wint[:], 0.0, ["wint"])
        for c in range(2):
            dma("sp", W1c[c * 64:(c + 1) * 64, :, c * 64:(c + 1) * 64],
                s_w1[c][c * 64:(c + 1) * 64, :].rearrange("p (j e) -> p j e", e=64), ["W1c"], ["W1c"], "s_w1c")
            dma("sp", W2cf[c * 64:(c + 1) * 64, :], cmp_w2[c], [], ["W2cf"], "s_w2c")
            dma("sp", peTcf[c * 64:(c + 1) * 64, :], cmp_pe[c].rearrange("j d -> d j"), [], ["peTcf"], "s_pec", allow_slow_non_contiguous=True)
        for c in range(2):
            cp("dve", W2c[c * 64:(c + 1) * 64, c * 64:(c + 1) * 64], W2cf[c * 64:(c + 1) * 64, :], ["W2cf", "W2c"], ["W2c"])
        cp("dve", peTc[:], peTcf[:], ["peTcf"], ["peTc"])
        for j in range(32):
            mm(psn[7][:, 0:1], W1c[:, j, :], peTc[:, j:j + 1], j == 0, j == 31, ["W1c", "peTc"], [("ps", 7)])
        cp("act", pebc[:], psn[7][:, 0:1], [("ps", 7)], ["pebc"])
        cp("dve", ptf[:], ptab[:], ["ptab"], ["ptf"])

        def small_norm_T(src, src_tok, dstT, dst_tok, slot):
            ssq = stat[0:NB, slot:slot + 1]
            rstd = stat[0:NB, slot + 1:slot + 2]
            act(xsn[:], src, AF.Square, [src_tok], ["xsn", ("sst", slot)], accum_out=ssq)
            act(ssq, ssq, AF.Sqrt, [("sst", slot), "eps"], [("sst", slot)], scale=1.0 / D, bias=eps_sb[0:NB, 0:1])
            S.op("dve", lambda e, rstd=rstd, ssq=ssq: e.reciprocal(rstd, ssq), [("sst", slot)], [("srstd", slot)])
            act(xsn[:], src, AF.Copy, [src_tok, ("srstd", slot)], ["xsn"], scale=rstd)
            for kc in range(8):
                tr(ps_tr[:, kc * NB:(kc + 1) * NB], xsn[:, kc * 128:(kc + 1) * 128], ["xsn"], ["ps_tr"])
            cp("act", dstT[:], ps_tr[:, 0:8 * NB].rearrange("p (k t) -> p k t", k=8), ["ps_tr"], [dst_tok])

        small_norm_T(xs_sb[:], "xs", xsT, "xsT", 20)
        for cc in range(NCC):
            bank = (5, 6)[cc % 2]
            k = cc % 2
            dma("sp", ccin[k][:], cconv_s[:, :, cc * 128:(cc + 1) * 128].rearrange("b r c -> (b r) c"), [], [("ccin", k)], "s_cc%d" % k)
            trf(psn[bank][:, 0:2 * NB], ccin[k][:], [("ccin", k)], [("ps", bank)])
            cp("act", ccT[:, cc, :], psn[bank][:, 0:2 * NB], [("ps", bank)], [("ccT", cc)])
        for cc in range(NCC):
            k = cc % 2
            dma("sp", wch[k][:], s_win[cc], [], [("wch", k)], "s_wch%d" % k)
            dma("sp", woc[k][:], s_wout[:, cc, :], [], [("woc", k)], "s_woc%d" % k)
            bank = (2, 3)[k]
            for part in range(4):
                for kc in range(8):
                    mm(psn[bank][:, part * NB:(part + 1) * NB], wch[k][:, kc, part * 128:(part + 1) * 128], xsT[:, kc, :],
                       part == 0 and kc == 0, part == 3 and kc == 7, [("wch", k), "xsT"], [("ps", bank)])
            pb_, pc_, ph_, pz_ = [psn[bank][:, p * NB:(p + 1) * NB] for p in range(4)]
            s0, s1, s2, s3, s4, s5 = [sm[i][:, 0:NB] for i in range(6)]
            cp("act", s0, pc_, [("ps", bank)], [("sm", 0)])
            tt("dve", uT[:, cc, :], s0, ph_, ALU.mult, [("sm", 0), ("ps", bank)], [("uT", cc)])
            ccv = ccT[:, cc, :].rearrange("p (b r) -> p r b", r=2)
            ts("dve", s1, uT[:, cc, :], cw_s[:, cc, 2:3], None, ALU.mult, None, [("uT", cc), "cw_s"], [("sm", 1)])
            stt("dve", s1, ccv[:, 1, :], cw_s[:, cc, 1:2], s1, ALU.mult, ALU.add, [("ccT", cc), "cw_s", ("sm", 1)], [("sm", 1)])
            stt("dve", s1, ccv[:, 0, :], cw_s[:, cc, 0:1], s1, ALU.mult, ALU.add, [("ccT", cc), "cw_s", ("sm", 1)], [("sm", 1)])
            act(s2, pz_, AF.Silu, [("ps", bank)], [("sm", 2)])
            tt("dve", s3, pb_, s2, ALU.mult, [("ps", bank), ("sm", 2)], [("sm", 3)])
            tt("dve", gTs[:], s3, s1, ALU.mult, [("sm", 3), ("sm", 1)], ["gTs"])
            for half in range(2):
                mm(psn[half][0:NB, :], gTs[:], woc[k][:, half * 512:(half + 1) * 512], cc == 0, cc == NCC - 1,
                   ["gTs", ("woc", k)], [("ps", half)])
        for half in range(2):
            tt("dve", hs_sb[:, half * 512:(half + 1) * 512], psn[half][0:NB, :], xs_sb[:, half * 512:(half + 1) * 512], ALU.add,
               [("ps", half), "xs"], ["hs"])
        for cc in range(NCC):
            bank = (5, 6)[cc % 2]
            k = cc % 2
            trf(psn[bank][0:NB, 0:128], uT[:, cc, :], [("uT", cc)], [("ps", bank)])
            cp("act", u_tok[k][:], psn[bank][0:NB, 0:128], [("ps", bank)], [("u_tok", k)])
            dma("pool", o_conv_s[:, 1, cc * 128:(cc + 1) * 128], u_tok[k][:], [("u_tok", k)], [("o_conv_s1", cc)], "s_ocs1_%d" % k, is_out=True)
        dma("pool", o_conv_s[:, 0, :], cconv_s[:, 1, :], [], ["o_conv_s0"], "s_ocs0", is_out=True)
        small_norm_T(hs_sb[:], "hs", hsT, "hsT", 22)
        for kc in range(8):
            mm(psn[5][0:NB, 0:384], hsT[:, kc, :], wkvs_sb[:, kc, :], kc == 0, kc == 7, ["hsT", "wkvs"], [("ps", 5)])
        cp("act", kvs_sb[:], psn[5][0:NB, 0:384], [("ps", 5)], ["kvs"])
        dma("pool", o_kv_s, kvs_sb[:], ["kvs"], ["o_kv_s"], "s_okv", is_out=True)
        for br in range(3):
            for kc in range(8):
                mm(psn[6][0:64, br * NB:(br + 1) * NB], wkvs_sb[:, kc, br * 128:br * 128 + 64], hsT[:, kc, :],
                   br == 0 and kc == 0, br == 2 and kc == 7, ["hsT", "wkvs"], [("ps", 6)])
        cp("act", KnT[:], psn[6][0:64, 0:3 * NB].rearrange("p (a b) -> p a b", a=3), [("ps", 6)], ["KnT"])
        for g in range(4):
            for kc in range(8):
                mm(psn[7][0:64, g * NB:(g + 1) * NB], wqs_sb[:, kc, g * 64:(g + 1) * 64], hsT[:, kc, :],
                   g == 0 and kc == 0, g == 3 and kc == 7, ["hsT", "wqs"], [("ps", 7)])
        cp("act", QsT[:], psn[7][0:64, 0:4 * NB].rearrange("p (a b) -> p a b", a=4), [("ps", 7)], ["QsT"])
        for kc in range(8):
            mm(psn[5][0:NB, 0:268], hsT[:, kc, :], wqs_sb[:, kc, 256:524], kc == 0, kc == 7, ["hsT", "wqs"], [("ps", 5)])
        act(gz_sb[:, 0:12], psn[5][0:NB, 0:12], AF.Sigmoid, [("ps", 5)], ["gz"])
        act(gz_sb[:, 12:268], psn[5][0:NB, 12:268], AF.Silu, [("ps", 5), "gz"], ["gz"])
        dma("pool", o_win_s[:, 0:511, :], win_s[:, 1:512, :], [], ["o_win_s0"], "s_ows0", is_out=True)
        dma("pool", o_win_s[:, 511, :], kvs_sb[:, 256:384], ["kvs"], ["o_win_s1"], "s_ows1", is_out=True)

        for b in range(NB if "nosattn" not in DBG_S else 0):
            for m in range(8):
                k = m % 2
                ts("dve", idxf[:, m:m + 1], ptf[:, b:b + 1], 8.0, float(m), ALU.mult, ALU.add, ["ptf"], [("idxf", m)])
                cp("dve", idx8[:, m:m + 1], idxf[:, m:m + 1], [("idxf", m)], [("idx8", m)])
                S.op("pool", lambda e, k=k, m=m: e.indirect_dma_start(
                    out=stg[k][:], out_offset=None, in_=pool_cmp,
                    in_offset=bass.IndirectOffsetOnAxis(ap=idx8[:, m:m + 1], axis=0)),
                    [("idx8", m)], [("stg", k)], dma="s_stg%d" % k)
                cp("act" if m % 2 else "dve", stgb[k][:], stg[k][:], [("stg", k)], [("stgb", k)])
                for rh in range(2):
                    for r8 in range(8):
                        r = rh * 8 + r8
                        tr(ps_tr[:, r8 * 128:(r8 + 1) * 128], stgb[k][:, r * 128:(r + 1) * 128], [("stgb", k)], ["ps_tr"])
                    base = 16 * m + rh * 8
                    dstv = RT[:, base:base + 16384].rearrange("p (i x) -> p x i", x=128)[:, 0:8, :]
                    cp("act", dstv, ps_tr[:].rearrange("p (r i) -> p r i", r=8), ["ps_tr"], ["RT"])
            for ci, (n0, nn) in enumerate(((0, 512), (512, 511))):
                bank = (0, 1)[ci]
                for j in range(32):
                    c0 = 16 * n0 + j
                    mm(psn[bank][:, 0:nn], W1c[:, j, :], RT[:, c0:c0 + 16 * (nn - 1) + 1:16], j == 0, j == 31, ["W1c", "RT"], [("ps", bank)])
                act(hidS[:, n0:n0 + nn], psn[bank][:, 0:nn], AF.Silu, [("ps", bank), "pebc"], ["hidS"], bias=pebc[:, 0:1])
            for ci, (n0, nn) in enumerate(((0, 512), (512, 511))):
                bank = (2, 3)[ci]
                mm(psn[bank][:, 0:nn], W2c[:], hidS[:, n0:n0 + nn], True, True, ["W2c", "hidS"], [("ps", bank)])
                cp("act", KcS[:, n0:n0 + nn], psn[bank][:, 0:nn], [("ps", bank)], ["KcS"])
            for ch in range(8):
                nn = 128 if ch < 7 else 127
                mm(psn[5][0:nn, (ch % 4) * 128:(ch % 4 + 1) * 128], hidS[:, ch * 128:ch * 128 + nn], W2c[:], ch % 4 == 0, ch % 4 == 3,
                   ["W2c", "hidS"], [("ps", 5)])
                if ch % 4 == 3:
                    c4 = ch - 3
                    for q in range(4):
                        nq = 128 if (c4 + q) < 7 else 127
                        cp("act", VcS[0:nq, c4 + q, 0:64], psn[5][0:nq, q * 128 + 64:(q + 1) * 128], [("ps", 5)], ["VcS"])
            Qb = QsT[:, :, b]
            for ch in range(8):
                nn = 128 if ch < 7 else 127
                mm(psn[6][0:nn, ch * 4:(ch + 1) * 4], KcS[0:64, ch * 128:ch * 128 + nn], Qb, ch == 0, ch == 7, ["KcS", "QsT"], [("ps", 6)])
            memset("dve", PcS[:], 0.0, ["PcS"])
            act(PcS[:, 0:7, :], psn[6][:, 0:28].rearrange("p (c g) -> p c g", g=4), AF.Exp, [("ps", 6), "PcS"], ["PcS"], scale=SCALE)
            act(PcS[0:127, 7, :], psn[6][0:127, 28:32], AF.Exp, [("ps", 6), "PcS"], ["PcS"], scale=SCALE)
            for ch in range(8):
                mm(psn[7][0:4, 0:65], PcS[:, ch, :], VcS[:, ch, :], ch == 0, ch == 7, ["PcS", "VcS"], [("ps", 7)])
            cp("act", o3[:, 0, :], psn[7][0:4, 0:65], [("ps", 7)], ["o3"])
            for ch in range(8):
                mm(psn[7][0:4, 128:128 + 257], PcS[:, ch, :], mems[:, ch, :], ch == 0, ch == 7, ["PcS", "mems"], [("ps", 7)])
            ts("dve", o3[:, 0, 64:65], o3[:, 0, 64:65], 1e-30, None, ALU.max, None, ["o3"], ["o3"])
            S.op("dve", lambda e: e.reciprocal(sm[4][0:4, 0:1], o3[:, 0, 64:65]), ["o3"], [("sm", 4)])
            ts("dve", impn[:], psn[7][0:4, 128:128 + 257], sm[4][0:4, 0:1], None, ALU.mult, None, [("ps", 7), ("sm", 4)], ["impn"])
            mm(psn[5][0:1, 0:257], ones4[:], impn[:], True, True, ["impn", "ones4"], [("ps", 5)])
            cp("act", scr_s[:], psn[5][0:1, 0:257], [("ps", 5)], ["scr_s"])
            memset("dve", scr_s[:, 0:1], 3e9, ["scr_s"])
            memset("dve", scr_s[:, 256:257], 2e9, ["scr_s"])
            memset("dve", scr_s[:, 255:256], 1e9, ["scr_s"])
            S.op("dve", lambda e: e.max(m8s[:, 0:8], scr_s[:]), ["scr_s"], ["m8s"])
            S.op("dve", lambda e: e.max_index(i8s[:, 0:8], m8s[:, 0:8], scr_s[:]), ["scr_s", "m8s"], ["i8s"])
            S.op("dve", lambda e: e.match_replace(scr_s2[:], m8s[:, 0:8], scr_s[:], -3e9), ["scr_s", "m8s"], ["scr_s2"])
            S.op("dve", lambda e: e.max(m8s[:, 8:16], scr_s2[:]), ["scr_s2"], ["m8s"])
            S.op("dve", lambda e: e.max_index(i8s[:, 8:16], m8s[:, 8:16], scr_s2[:]), ["scr_s2", "m8s"], ["i8s"])
            cp("dve", jf[:], i8s[:], ["i8s"], ["jf"])
            ts("dve", jf[:], jf[:], 255.0, None, ALU.min, None, ["jf"], ["jf"])
            cp("dve", i8s[:].bitcast(I32), jf[:], ["jf"], ["i8s"])
            S.op("dve", lambda e: e.tensor_single_scalar(i8s[:].bitcast(I32), i8s[:].bitcast(I32), 1, ALU.arith_shift_right), ["i8s"], ["i8s"])
            cp("dve", jh[:], i8s[:].bitcast(I32), ["i8s"], ["jh"])
            stt("dve", gbf[:], jh[:], -2.0, jf[:], ALU.mult, ALU.add, ["jh", "jf"], ["gbf"])
            dma("sp", pt1[:, 0:128], ptab_in[b:b + 1, :], [], ["pt1"], "s_pt1")
            cp("dve", pt1f[:, 0:128], pt1[:, 0:128], ["pt1"], ["pt1f"])
            for q in range(16):
                ts("dve", onehot[:], iota_row[:], jh[:, q:q + 1], None, ALU.is_equal, None, ["jh", "iota_row", "pgf"], ["oh"])
                tt("dve", onehot[:], onehot[:], pt1f[:, 0:128], ALU.mult, ["oh", "pt1f"], ["oh"])
                S.op("dve", lambda e, q=q: e.reduce_sum(pgf[:, q:q + 1], onehot[:], mybir.AxisListType.X), ["oh"], ["pgf"])
            stt("dve", gbf[:], pgf[:], 2.0, gbf[:], ALU.mult, ALU.add, ["pgf", "gbf"], ["gbf"])
            cp("dve", gbi[:], gbf[:], ["gbf"], ["gbi"])
            dma("sp", s_gbi[b:b + 1, :], gbi[:], ["gbi"], [("s_gbi", b)], "s_gbist")
            dma("sp", graw[0:64, :], s_gbi[b, 0:16:2].partition_broadcast(64), [("s_gbi", b)], ["graw"], "s_graw", allow_slow_non_contiguous=True)
            dma("sp", graw[64:128, :], s_gbi[b, 1:16:2].partition_broadcast(64), [("s_gbi", b), "graw"], ["graw"], "s_graw", allow_slow_non_contiguous=True)
            cp("dve", grawf[:], graw[:], ["graw"], ["grawf"])
            ts("dve", grawf[:], grawf[:], 64.0, mcol[:, 2:3], ALU.mult, ALU.add, ["grawf", "mcol"], ["grawf"])
            cp("dve", gidx[:], grawf[:], ["grawf"], ["gidx"])
            for q in range(8):
                S.op("pool", lambda e, q=q: e.indirect_dma_start(
                    out=selt[:, q, :], out_offset=None, in_=pool_sel,
                    in_offset=bass.IndirectOffsetOnAxis(ap=gidx[:, q:q + 1], axis=0)),
                    ["gidx"], [("selt", q)], dma="s_selt%d" % q)
            dma("pool", selt[64:65, 0, :], kvs_sb[b:b + 1, 128:256], ["kvs", ("selt", 0)], [("selt", 0)], "s_selt0")
            cp("dve", selb[:, :, 0:128], selt[:], [("selt", q) for q in range(8)] + ["selb"], ["selb"])
            for q in range(8):
                tr(ps_tr[0:64, q * 128:(q + 1) * 128], selb[:, q, 0:64], ["selb"], ["ps_tr"])
            cp("act", KsT[:], ps_tr[0:64, :].rearrange("p (q k) -> p q k", q=8), ["ps_tr"], ["KsT"])
            for q in range(8):
                mm(psn[6][:, q * 4:(q + 1) * 4], KsT[:, q, :], Qb, q == 0, q == 7, ["KsT", "QsT"], [("ps", 6)])
            act(PsS[:], psn[6][:, 0:32].rearrange("p (c g) -> p c g", g=4), AF.Exp, [("ps", 6)], ["PsS"], scale=SCALE)
            ts("dve", PsS[:, 0, :], PsS[:, 0, :], mcol[:, 0:1], None, ALU.mult, None, ["PsS", "mcol"], ["PsS"])
            for q in range(8):
                mm(psn[7][0:4, 0:65], PsS[:, q, :], selb[:, q, 64:129], q == 0, q == 7, ["PsS", "selb"], [("ps", 7)])
            cp("act", o3[:, 1, :], psn[7][0:4, 0:65], [("ps", 7)], ["o3"])
            dma("sp", wint[:, 0:4, :], win_s[b].rearrange("(t p) c -> p t c", p=128), ["wint"], ["wint"], "s_wint")
            dma("sp", wint[0:1, 4, :], kvs_sb[b:b + 1, 256:384], ["kvs", "wint"], ["wint"], "s_wint")
            cp("dve", winb[:, :, 0:128], wint[:], ["wint", "winb"], ["winb"])
            for q in range(5):
                tr(ps_tr[0:64, q * 128:(q + 1) * 128], winb[:, q, 0:64], ["winb"], ["ps_tr"])
            cp("act", KwT[:], ps_tr[0:64, 0:640].rearrange("p (q k) -> p q k", q=5), ["ps_tr"], ["KwT"])
            for q in range(5):
                mm(psn[6][:, q * 4:(q + 1) * 4], KwT[:, q, :], Qb, q == 0, q == 4, ["KwT", "QsT"], [("ps", 6)])
            act(PwS[:], psn[6][:, 0:20].rearrange("p (c g) -> p c g", g=4), AF.Exp, [("ps", 6)], ["PwS"], scale=SCALE)
            ts("dve", PwS[:, 4, :], PwS[:, 4, :], mcol[:, 1:2], None, ALU.mult, None, ["PwS", "mcol"], ["PwS"])
            for q in range(5):
                mm(psn[7][0:4, 0:65], PwS[:, q, :], winb[:, q, 64:129], q == 0, q == 4, ["PwS", "winb"], [("ps", 7)])
            cp("act", o3[:, 2, :], psn[7][0:4, 0:65], [("ps", 7)], ["o3"])
            dma("pool", s_o3[b], o3[:], ["o3"], [("s_o3", b)], "s_o3st")

        if "nosattn" in DBG_S:
            memset("dve", Oraw[:], 1.0, ["Oraw"])
            for b_ in range(NB):
                dma("pool", s_o3[b_], Oraw[0:4, 0, :, :], ["Oraw"], [("s_o3", b_)], "s_o3st")
        dma("sp", Oraw[:].rearrange("p g r e -> p (g r e)"), s_o3.rearrange("b g r e -> b (g r e)"), [("s_o3", b) for b in range(NB)], ["Oraw"], "s_orawld")
        ts("dve", rzs_[:], Oraw[:, :, :, 64], 1e-30, None, ALU.max, None, ["Oraw"], ["rzs"])
        S.op("dve", lambda e: e.reciprocal(rzs_[:], rzs_[:]), ["rzs"], ["rzs"])
        tt("dve", coefs[:], rzs_[:], gz_sb[:, 0:12].rearrange("p (g r) -> p g r", r=3), ALU.mult, ["rzs", "gz"], ["coefs"])
        for br in range(3):
            cbb = coefs[:, :, br:br + 1].to_broadcast([NB, 4, 64])
            dst = (Os if br == 0 else Otm)[:].rearrange("p (g d) -> p g d", d=64)
            tt("dve", dst, Oraw[:, :, br, 0:64], cbb, ALU.mult, ["Oraw", "coefs"], ["Os" if br == 0 else "Otm"])
            if br:
                tt("dve", Os[:], Os[:], Otm[:], ALU.add, ["Os", "Otm"], ["Os"])
        tt("dve", ozs[:], Os[:], gz_sb[:, 12:268], ALU.mult, ["Os", "gz"], ["ozs"])
        for kc in range(2):
            tr(ps_tr[:, kc * NB:(kc + 1) * NB], ozs[:, kc * 128:(kc + 1) * 128], ["ozs"], ["ps_tr"])
        cp("act", ozsT[:], ps_tr[:, 0:2 * NB].rearrange("p (k t) -> p k t", k=2), ["ps_tr"], ["ozsT"])
        for half in range(2):
            for kc in range(2):
                mm(psn[half][0:NB, :], ozsT[:, kc, :], wos_sb[:, kc, half * 512:(half + 1) * 512], kc == 0, kc == 1, ["ozsT", "wos"], [("ps", half)])
            cp("act", part_sb[:, half * 512:(half + 1) * 512], psn[half][0:NB, :], [("ps", half)], ["part"])
        dma("pool", cc_in, part_sb[:], ["part"], ["cc_in"], "s_ccin")
        if "ccdbg" in DBG_S:
            dma("pool", o_dpre, part_sb[:], ["part"], ["o_dpre"], "s_dpre", is_out=True)
        if use_cc:
            S.op("pool", lambda e: e.collective_compute("AllReduce", ALU.add, replica_groups=[[0, 1, 2, 3], [4, 5, 6, 7]],
                                                        ins=[cc_in.opt()], outs=[cc_out.opt()]), ["cc_in"], ["cc_out"], dma="s_cc_ar", inc=1)
            dma("sp", part_sb[:], cc_out, ["cc_out", "part"], ["part"], "s_ccout")
            if "ccdbg" in DBG_S:
                dma("pool", o_dpost, part_sb[:], ["part"], ["o_dpost"], "s_dpost", is_out=True)
        else:
            dma("pool", o_part, part_sb[:], ["part"], ["o_part"], "s_opart", is_out=True)
        tt("dve", hs_sb[:], hs_sb[:], part_sb[:], ALU.add, ["hs", "part"], ["hs"])
        ssq = stat[0:NB, 24:25]
        rstd = stat[0:NB, 25:26]
        act(ys_sb[:], hs_sb[:], AF.Square, ["hs"], ["ys", "ssqs"], accum_out=ssq)
        act(ssq, ssq, AF.Sqrt, ["ssqs", "eps"], ["ssqs"], scale=1.0 / D, bias=eps_sb[0:NB, 0:1])
        S.op("dve", lambda e, rstd=rstd, ssq=ssq: e.reciprocal(rstd, ssq), ["ssqs"], ["rstds"])
        stt("dve", ys_sb[:], hs_sb[:], rstd, nf_s[:], ALU.mult, ALU.mult, ["hs", "rstds", "nf_s", "ys"], ["ys"])
        dma("pool", o_y_s, ys_sb[:], ["ys"], ["o_y_s"], "s_oys", is_out=True)
        S.barrier()

    KT = [None, sbp("KT_sel", [128, 2, SEQ], BF16), sbp("KT_win", [128, 2, SEQ], BF16)]
    VV = [None, sbp("V_sel", [128, 32, 4, 72], BF16), sbp("V_win", [128, 32, 4, 72], BF16)]
    KcT = sbp("KcT", [128, 2, 256], BF16)
    Vc = sbp("Vc", [128, 2, 4, 65], BF16)
    memset("pool", VV[1][:], 1.0, ["V1init"])
    memset("pool", VV[2][:], 1.0, ["V2init"])
    memset("pool", Vc[:], 1.0, ["Vcinit"])

    with ExitStack() as es:
        def sb(name, shape, dt=F32):
            return es.enter_context(nc.sbuf_tensor(name, list(shape), dt))
        X = sb("X", [128, 4, D])
        xnb = [sb("xnb%d" % i, [128, D], BF16) for i in range(2)]
        aT = sb("aT", [128, 8, ST], BF16)
        NRING = 2
        wring = [sb("wring%d" % i, [128, 8, 512], BF16) for i in range(NRING)]
        wout_sb = sb("wout_sb", [128, NCC, D], BF16)
        G = sb("G", [128, NCC, ST], BF16)
        U = [sb("U%d" % i, [128, ST + 2]) for i in range(2)]
        UC = sb("UC", [128, NCC, 2])
        Vb = [sb("Vb%d" % i, [128, ST]) for i in range(2)]
        c_sb1 = sb("c_sb", [128, ST])
        c_sb = [c_sb1, c_sb1]
        sz_sb = [sb("sz_sb%d" % i, [128, ST]) for i in range(2)]
        t1_sb1 = sb("t1_sb", [128, ST])
        t1_sb = [t1_sb1, t1_sb1]
        kvst = [sb("kvst%d" % i, [128, 512]) for i in range(2)]
        cw_sb = sb("cw_sb", [128, NCC, 3])
        W1blk = sb("W1blk", [128, 2, 32, 128], BF16)
        W2f = sb("W2f", [128, 2, 64])
        W2blk = sb("W2blk", [128, 2, 128], BF16)
        peTf = sb("peTf", [128, 2, 32])
        peT = sb("peT", [128, 2, 32], BF16)
        pebias = sb("pebias", [128, 2])
        CK = sb("CK", [128, 4, 16 + ST], BF16)
        hidT = sb("hidT", [128, 2, 32], BF16)
        vcst = sb("vcst", [32, 4, 64], BF16)

        for j in range(3):
            dma("sp", cw_sb[:, :, j], conv_w[j, :].rearrange("(c p) -> p c", p=128), [], ["cw"], "c_cw", allow_slow_non_contiguous=True)
        memset("pool", UC[:], 0.0, [("UC", cc) for cc in range(NCC)])
        memset("pool", CK[:], 0.0, ["CK"])
        memset("pool", W1blk[:], 0.0, ["W1blk"])
        memset("pool", W2blk[:], 0.0, ["W2blk"])
        dma("sp", wout_sb[:], s_wout, [], ["wout"], "wres0")
        for c in range(2):
            for hf in range(2):
                dma("sp", W1blk[hf * 64:(hf + 1) * 64, c, :, hf * 64:(hf + 1) * 64],
                    s_w1[c][hf * 64:(hf + 1) * 64, :].rearrange("p (j e) -> p j e", e=64), [], ["W1blk"], "c_w1")
        for hf in range(2):
            dma("sp", W2f[hf * 64:(hf + 1) * 64, :, :], cmp_w2.rearrange("c e d -> e c d"), [], ["W2f"], "c_w2")
            for c in range(2):
                dma("sp", peTf[hf * 64:(hf + 1) * 64, c, :], cmp_pe[c].rearrange("j d -> d j"), [], ["peTf"], "c_pe", allow_slow_non_contiguous=True)
        for c in range(2):
            for hf in range(2):
                cp("dve", W2blk[hf * 64:(hf + 1) * 64, c, hf * 64:(hf + 1) * 64], W2f[hf * 64:(hf + 1) * 64, c, :], ["W2f", "W2blk"], ["W2blk"])
        cp("dve", peT[:], peTf[:], ["peTf"], ["peT"])
        import os
        DBG = os.environ.get("KDBG", "")
        for c in range(2):
            for j in range(32):
                mm(psn[7][:, c:c + 1], W1blk[:, c, j, :], peT[:, c, j:j + 1], j == 0, j == 31, ["W1blk", "peT"], [("ps", 7)])
        cp("act", pebias[:], psn[7][:, 0:2], [("ps", 7)], ["pebias"])

        def norm_transpose(src_ap, src_tok, i, dst_tok, slot):
            ssq = stat[:, slot:slot + 1]
            rstd = stat[:, 8 + slot:9 + slot]
            xb = xnb[slot % 2]
            xbt = ("xnb", slot % 2)
            act(xb[:], src_ap, AF.Square, [src_tok], [xbt, ("ssq", slot)], accum_out=ssq)
            act(ssq, ssq, AF.Sqrt, [("ssq", slot), "eps"], [("ssq", slot)], scale=1.0 / D, bias=eps_sb[:, 0:1])
            S.op("dve", lambda e, rstd=rstd, ssq=ssq: e.reciprocal(rstd, ssq), [("ssq", slot)], [("rstd", slot)])
            xb = xnb[slot % 2]
            xbt = ("xnb", slot % 2)
            act(xb[:], src_ap, AF.Copy, [src_tok, ("rstd", slot)], [xbt], scale=rstd)
            for kc in range(8):
                tr(ps_tr[:, kc * 128:(kc + 1) * 128], xb[:, kc * 128:(kc + 1) * 128], [xbt], ["ps_tr"])
            cp("dve", aT[:, :, i * 128:(i + 1) * 128], ps_tr[:].rearrange("p (k t) -> p k t", k=8), ["ps_tr"], [dst_tok])

        AT = [("aT", i) for i in range(4)]
        ring_n = [0]
        hs_n = [0]
        gb_n = [0]
        kv_n = [0]

        def gbank():
            b = (5, 6, 7)[gb_n[0] % 3]
            gb_n[0] += 1
            return b

        def ring_load(src):
            slot = ring_n[0] % NRING
            ring_n[0] += 1
            dma("sp", wring[slot][:], src, [], [("wr", slot)], "wr%d" % slot)
            return slot

        for t in range(nst):
            r0 = t * ST
            XT = [("X", i) for i in range(4)]
            dma("sp", X[:], x_p[r0:r0 + ST, :].rearrange("(i p) d -> p i d", p=128), [], XT, "xld")
            for i in range(4):
                norm_transpose(X[:, i, :], ("X", i), i, ("aT", i), i)
            for cc in range(NCC):
                slot = ring_load(s_win[cc])
                wt = ("wr", slot)
                k = cc % 2
                b0 = 2 * (hs_n[0] % 2)
                hs_n[0] += 1
                for part, bank in ((1, b0), (2, b0 + 1)):
                    for kc in range(8):
                        mm(psn[bank][:], wring[slot][:, kc, part * 128:(part + 1) * 128], aT[:, kc, :], kc == 0, kc == 7,
                           [wt] + AT, [("ps", bank)])
                cp("act", c_sb[k][:], psn[b0][:], [("ps", b0)], ["c_sb"])
                cp("pool", U[k][:, 0:2], UC[:, cc, :], [("UC", cc)], [("U", k)])
                tt("dve", U[k][:, 2:ST + 2], c_sb[k][:], psn[b0 + 1][:], ALU.mult, ["c_sb", ("ps", b0 + 1), ("U", k)], [("U", k)])
                cp("pool", UC[:, cc, :], U[k][:, ST:ST + 2], [("U", k)], [("UC", cc)])
                ts("dve", Vb[k][:], U[k][:, 2:ST + 2], cw_sb[:, cc, 2:3], None, ALU.mult, None, [("U", k), "cw"], [("Vb", k)])
                stt("dve", Vb[k][:], U[k][:, 1:ST + 1], cw_sb[:, cc, 1:2], Vb[k][:], ALU.mult, ALU.add, [("U", k), "cw", ("Vb", k)], [("Vb", k)])
                stt("dve", Vb[k][:], U[k][:, 0:ST], cw_sb[:, cc, 0:1], Vb[k][:], ALU.mult, ALU.add, [("U", k), "cw", ("Vb", k)], [("Vb", k)])
                b1 = 2 * (hs_n[0] % 2)
                hs_n[0] += 1
                for part, bank in ((0, b1), (3, b1 + 1)):
                    for kc in range(8):
                        mm(psn[bank][:], wring[slot][:, kc, part * 128:(part + 1) * 128], aT[:, kc, :], kc == 0, kc == 7,
                           [wt] + AT, [("ps", bank)])
                act(sz_sb[k][:], psn[b1 + 1][:], AF.Silu, [("ps", b1 + 1)], [("sz", k)])
                tt("dve", t1_sb[k][:], psn[b1][:], sz_sb[k][:], ALU.mult, [("ps", b1), ("sz", k)], ["t1"])
                tt("pool", G[:, cc, :], t1_sb[k][:], Vb[k][:], ALU.mult, ["t1", ("Vb", k)], [("G", cc)])
            for i in range(4):
                for half in range(2):
                    bank = gbank()
                    for cc in range(NCC):
                        mm(psn[bank][:], G[:, cc, i * 128:(i + 1) * 128], wout_sb[:, cc, half * 512:(half + 1) * 512], cc == 0, cc == NCC - 1,
                           [("G", cc), "wout"], [("ps", bank)])
                    xs_ = X[:, i, half * 512:(half + 1) * 512]
                    tt("dve", xs_, psn[bank][:], xs_, ALU.add, [("ps", bank), ("X", i)], [("X", i)])
                if "nosh" not in DBG:
                    dma("pool", s_h[r0 + i * 128:r0 + (i + 1) * 128, :], X[:, i, :], [("X", i)], [("s_h", t, i)], "hst%d" % i)
                norm_transpose(X[:, i, :], ("X", i), i, ("aT", i), 4 + i)
            if "nosh" not in DBG:
                dma("pool", s_hnT[t], aT[:].rearrange("p k t -> p (k t)"), AT, [("s_hnT", t)], "hnst")
            for br in range(3):
                if "nokv" in DBG:
                    break
                slot = ring_load(s_wkv[br])
                wt = ("wr", slot)
                wr = wring[slot]
                for i in range(4):
                    tile_idx = t * 4 + i
                    bank = gbank()
                    for kc in range(8):
                        mm(psn[bank][:], aT[:, kc, i * 128:(i + 1) * 128], wr[:, kc, :], kc == 0, kc == 7, [("aT", i), wt], [("ps", bank)])
                    need_out = (br < 2) or (t == NST - 1)
                    if need_out:
                        k = kv_n[0] % 2
                        kv_n[0] += 1
                        cp("act", kvst[k][:], psn[bank][:], [("ps", bank)], [("kvst", k)])
                        if br == 0:
                            dst = o_cmp_p[r0 + i * 128:r0 + (i + 1) * 128, :]
                        elif br == 1:
                            dst = o_sel_p[r0 + i * 128:r0 + (i + 1) * 128, :]
                        else:
                            dst = o_win_p[i * 128:(i + 1) * 128, :]
                        dma("pool", dst, kvst[k][:], [("kvst", k)], [("okv", t, i, br)], "kvst%d" % k, is_out=True)
                    if br >= 1 and "novv" not in DBG:
                        cp("act", VV[br][:, tile_idx, :, 0:64], psn[bank][:, 256:512].rearrange("p (h d) -> p h d", d=64),
                           [("ps", bank), "V%dinit" % br], [("V", br, tile_idx)])
                if "nock" in DBG:
                    continue
                if br == 0 and "nock0" in DBG:
                    continue
                if br >= 1 and "nokt" in DBG:
                    continue
                if br == 0:
                    for ch in range(4):
                        cp("pool", CK[:, ch, 0:16], CK[:, ch, ST:ST + 16], ["CK"], ["CK"])
                    for ch in range(4):
                        bank = gbank()
                        for kc in range(8):
                            mm(psn[bank][:], wr[:, kc, ch * 128:(ch + 1) * 128], aT[:, kc, :], kc == 0, kc == 7, AT + [wt], [("ps", bank)])
                        cp("act" if ch % 2 else "dve", CK[:, ch, 16:16 + ST], psn[bank][:], [("ps", bank), "CK"], ["CK"])
                else:
                    for hp in range(2):
                        bank = gbank()
                        for kc in range(8):
                            mm(psn[bank][:], wr[:, kc, hp * 128:(hp + 1) * 128], aT[:, kc, :], kc == 0, kc == 7, AT + [wt], [("ps", bank)])
                        cp("act" if hp else "dve", KT[br][:, hp, r0:r0 + ST], psn[bank][:], [("ps", bank)], [("KT", br, t)])
            m0 = 1 if t == 0 else 0
            nm = 32 - m0
            n_first = 0 if t == 0 else 32 * t - 1
            for c in range(2):
                if "nocmp" in DBG:
                    break
                bank = gbank()
                for hp in range(2):
                    ch = c * 2 + hp
                    for j in range(32):
                        mm(psn[bank][:, hp * 32:(hp + 1) * 32], W1blk[:, c, j, :], CK[:, ch, j:j + 16 * 31 + 1:16], j == 0, j == 31,
                           ["W1blk", "CK"], [("ps", bank)])
                act(hidT[:], psn[bank][:, 0:64].rearrange("p (h m) -> p h m", m=32), AF.Silu, [("ps", bank), "pebias"], ["hidT"],
                    bias=pebias[:, c:c + 1])
                bank2 = gbank()
                if c == 0:
                    for hp in range(2):
                        mm(psn[bank2][:, hp * 32:(hp + 1) * 32], W2blk[:, 0, :], hidT[:, hp, :], True, True, ["W2blk", "hidT"], [("ps", bank2)])
                    for hp in range(2):
                        cp("act", KcT[:, hp, n_first:n_first + nm], psn[bank2][:, hp * 32 + m0:(hp + 1) * 32], [("ps", bank2)], ["KcT"])
                else:
                    for hp in range(2):
                        mm(psn[bank2][0:32, hp * 128:(hp + 1) * 128], hidT[:, hp, :], W2blk[:, 1, :], True, True, ["W2blk", "hidT"], [("ps", bank2)])
                    cp("act", vcst[:], psn[bank2][0:32, 0:256].rearrange("p (h d) -> p h d", d=64), [("ps", bank2)], ["vcst"])
                    done = 0
                    while done < nm:
                        if "novc" in DBG:
                            break
                        n = n_first + done
                        chn, pr = n // 128, n % 128
                        cnt = min(nm - done, 128 - pr)
                        dma("pool", Vc[pr:pr + cnt, chn, :, 0:64], vcst[m0 + done:m0 + done + cnt, :, :], ["vcst", "Vcinit"], ["Vc"], "vcsc")
                        done += cnt
        for r in range(2):
            dma("pool", o_conv_p[r, :].rearrange("(c p) -> p c", p=128), UC[:, :, r], [("UC", cc) for cc in range(NCC)], [("o_conv", r)],
                "oconv%d" % r, is_out=True, allow_slow_non_contiguous=True)
        S.barrier()

    if with_b:
      with ExitStack() as es:
        def sb(name, shape, dt=F32):
            return es.enter_context(nc.sbuf_tensor(name, list(shape), dt))
        wq_sb = sb("wq_sb", [128, 8, 1024], BF16)
        wgz_sb = sb("wgz_sb", [128, 8, 1072], BF16)
        wo_sb = sb("wo_sb", [128, 8, 1024], BF16)
        Eb = sb("Eb", [128, 4096], BF16)
        cf = sb("cf", [128, 384])
        nf_sb = sb("nf_sb", [128, D])
        aT2 = sb("aT2", [128, 8, ST], BF16)
        Hq = sb("Hq", [128, D])
        QT = sb("QT", [128, 8, ST], BF16)
        SZ = sb("SZ", [128, 4, D], BF16)
        Gt = sb("Gt", [128, 4, 48])
        PT = [sb("PT%d" % i, [128, 512], BF16) for i in range(3)]
        Pc = [sb("Pc%d" % i, [128, 512], BF16) for i in range(2)]
        O = sb("O", [128, D])
        Otmp = sb("Otmp", [128, 256])
        ozb = sb("ozb", [128, D], BF16)
        ozT = sb("ozT", [128, 8, 128], BF16)
        Yst = sb("Yst", [128, D])
        imp = sb("imp", [128, 4, 64])
        scr = sb("scr", [128, 4, 64])
        scr2 = sb("scr2", [128, 4, 64])
        mb = sb("mb", [128, 4, 64])
        mbb = sb("mbb", [128, 256], BF16)
        mT = sb("mT", [128, 2, 128], BF16)
        m8 = sb("m8", [128, 4, 16])
        rz = sb("rz", [128, 16])
        coef = sb("coef", [128, 3, 4])

        dma("sp", wq_sb[:], s_wq, [], ["wq"], "b_wq")
        dma("sp", wgz_sb[:], s_wgz, [], ["wgz"], "b_wgz")
        dma("sp", wo_sb[:], s_wo, [], ["wo"], "b_wo")
        dma("sp", Eb[:], s_eb, [], ["Eb"], "b_eb")
        dma("sp", cf[:], consts[:, C_MUL:C_MUL + 384], [], ["cf"], "b_cf")
        dma("sp", nf_sb[:], norm_f.partition_broadcast(128), [], ["nf"], "b_nf")

        sb_n = [0]

        def sbank():
            b = (0, 1, 2)[sb_n[0] % 3]
            sb_n[0] += 1
            return b
        pt_n = [0]

        def bc4(ap2d, p):
            return ap2d.rearrange("p (o q) -> p o q", o=1).to_broadcast([p, 4, 128])
        TRIB = bc4(cb[:, C_TRIB:C_TRIB + 128], 128)
        TRIU = bc4(cb[:, C_TRIU:C_TRIU + 128], 128)
        STAIR = bc4(cb[0:8, C_STAIR:C_STAIR + 128], 8)
        QTT = [("QT", s) for s in range(8)]
        PSO = {0: 5, 1: 6, 2: 7}

        for t in range(nst):
            r0 = t * ST
            dma("sp", aT2[:].rearrange("p k t -> p (k t)"), s_hnT[t], [], ["aT2"], "b_aT2")
            for s in range(8):
                bank = sbank()
                for kc in range(8):
                    mm(psn[bank][:], wq_sb[:, kc, s * 128:(s + 1) * 128], aT2[:, kc, :], kc == 0, kc == 7, ["wq", "aT2"], [("ps", bank)])
                cp("act" if s % 2 else "dve", QT[:, s, :], psn[bank][:], [("ps", bank)], [("QT", s)])
            for i in range(4):
                for piece, (c0, cw_) in enumerate(((0, 48), (48, 512), (560, 512))):
                    bank = sbank()
                    for kc in range(8):
                        mm(psn[bank][:, 0:cw_], aT2[:, kc, i * 128:(i + 1) * 128], wgz_sb[:, kc, c0:c0 + cw_], kc == 0, kc == 7,
                           ["wgz", "aT2"], [("ps", bank)])
                    if piece == 0:
                        act(Gt[:, i, :], psn[bank][:, 0:48], AF.Sigmoid, [("ps", bank)], [("Gt", i)])
                    else:
                        act(SZ[:, i, (piece - 1) * 512:piece * 512], psn[bank][:, 0:512], AF.Silu, [("ps", bank)], [("SZ", i)])

            for i in range(4):
                qt = 4 * t + i
                q0 = 128 * qt
                dma("sp", Hq[:], s_h[q0:q0 + 128, :], [], ["Hq"], "b_hq")
                ncm = min(8 * qt + 7, 255)
                chunks = [(0, min(128, ncm))]
                if ncm > 128:
                    chunks.append((128, ncm - 128))
                for kvh in range(4):
                    pb = (kvh % 2) * 64
                    hp = kvh // 2
                    Qr = QT[pb:pb + 64, hp * 4:hp * 4 + 4, i * 128:(i + 1) * 128]
                    for ci, (n0, nn) in enumerate(chunks):
                        bank = sbank()
                        off = 129 + n0 - 8 * qt
                        need_stair = (8 * qt + 6 >= n0) and (8 * qt - 1 < n0 + nn) and (0 <= off) and (off + nn <= 264)
                        mm(psn[bank][0:nn, :], KcT[pb:pb + 64, hp, n0:n0 + nn], Qr, True, not need_stair, ["KcT"] + QTT, [("ps", bank)])
                        if need_stair:
                            mm(psn[bank][0:nn, :], cb[0:8, C_ZSEL + off:C_ZSEL + off + nn], STAIR, False, True, ["cb"], [("ps", bank)])
                        act(Pc[ci][0:nn, :], psn[bank][0:nn, :], AF.Exp, [("ps", bank)], [("Pc", ci)], scale=SCALE)
                    for g in range(4):
                        for ci, (n0, nn) in enumerate(chunks):
                            mm(psn[5][:, g * 65:(g + 1) * 65], Pc[ci][0:nn, g * 128:(g + 1) * 128], Vc[0:nn, ci, kvh, :],
                               g == 0 and ci == 0, g == 3 and ci == len(chunks) - 1, [("Pc", ci), "Vc"], [("ps", 5)])
                    for g in range(4):
                        for ci, (n0, nn) in enumerate(chunks):
                            mm(psn[3][:, g * 64:(g + 1) * 64], Pc[ci][0:nn, g * 128:(g + 1) * 128], cb[0:nn, C_MEM + ci * 64:C_MEM + (ci + 1) * 64],
                               g == 0 and ci == 0, g == 3 and ci == len(chunks) - 1, [("Pc", ci), "cb"], [("ps", 3)])
                    zc = psn[5][:, 0:260].rearrange("p (g e) -> p g e", e=65)[:, :, 64]
                    ts("dve", rz[:, 0:4], zc, 1e-30, None, ALU.max, None, [("ps", 5)], [("rz", kvh, 0)])
                    S.op("dve", lambda e: e.reciprocal(rz[:, 0:4], rz[:, 0:4]), [("rz", kvh, 0)], [("rz", kvh, 0)])
                    ts("dve", imp[:, kvh, :], psn[3][:, 0:64], rz[:, 0:1], None, ALU.mult, None, [("ps", 3), ("rz", kvh, 0)], [("imp", kvh)])
                    for g in range(1, 4):
                        stt("dve", imp[:, kvh, :], psn[3][:, g * 64:(g + 1) * 64], rz[:, g:g + 1], imp[:, kvh, :], ALU.mult, ALU.add,
                            [("ps", 3), ("rz", kvh, 0), ("imp", kvh)], [("imp", kvh)])
                    gview = Gt[:, i, kvh * 12:(kvh + 1) * 12].rearrange("p (g b) -> p b g", b=3)
                    tt("dve", coef[:, 0, :], rz[:, 0:4], gview[:, 0, :], ALU.mult, [("rz", kvh, 0), ("Gt", i)], [("coef", 0)])
                    ocv = psn[5][:, 0:260].rearrange("p (g e) -> p g e", e=65)[:, :, 0:64]
                    Ov = O[:, kvh * 256:(kvh + 1) * 256].rearrange("p (g d) -> p g d", d=64)
                    c0b = coef[:, 0, :].rearrange("p (g o) -> p g o", o=1).to_broadcast([128, 4, 64])
                    tt("dve", Ov, ocv, c0b, ALU.mult, [("ps", 5), ("coef", 0)], [("O", kvh)])
                IMP = [("imp", k) for k in range(4)]
                sl0 = 64 - 2 * qt
                mulb = cf[:, sl0:sl0 + 64].rearrange("p (o j) -> p o j", o=1).to_broadcast([128, 4, 64])
                addb = cf[:, 128 + sl0:128 + sl0 + 64].rearrange("p (o j) -> p o j", o=1).to_broadcast([128, 4, 64])
                valb = cf[:, 256 + sl0:256 + sl0 + 64].rearrange("p (o j) -> p o j", o=1).to_broadcast([128, 4, 64])
                tt("dve", scr[:], imp[:], mulb, ALU.mult, IMP + ["cf"], ["scr"])
                tt("dve", scr[:], scr[:], addb, ALU.add, ["scr", "cf"], ["scr"])
                if qt >= 1:
                    memset("dve", scr[:, :, 0:1], 3e9, ["scr"])
                for kvh in range(4):
                    S.op("dve", lambda e, kvh=kvh: e.max(m8[:, kvh, 0:8], scr[:, kvh, :]), ["scr"], [("m8", kvh)])
                    S.op("dve", lambda e, kvh=kvh: e.match_replace(scr2[:, kvh, :], m8[:, kvh, 0:8], scr[:, kvh, :], -3e9),
                         ["scr", ("m8", kvh)], [("scr2", kvh)])
                    S.op("dve", lambda e, kvh=kvh: e.max(m8[:, kvh, 8:16], scr2[:, kvh, :]), [("scr2", kvh)], [("m8", kvh)])
                    ts("dve", mb[:, kvh, :], scr[:, kvh, :], m8[:, kvh, 15:16], None, ALU.is_ge, None, ["scr", ("m8", kvh)], [("mb", kvh)])
                MB = [("mb", k) for k in range(4)]
                tt("dve", mb[:], mb[:], valb, ALU.mult, MB + ["cf"], MB)
                ts("dve", mbb[:], mb[:].rearrange("p k j -> p (k j)"), -1.0, BIG, ALU.add, ALU.mult, MB, ["mbb"])
                for pr in range(2):
                    tr(ps_tr[:, pr * 128:(pr + 1) * 128], mbb[:, pr * 128:(pr + 1) * 128], ["mbb"], ["ps_tr"])
                cp("dve", mT[:], ps_tr[:, 0:256].rearrange("p (a q) -> p a q", a=2), ["ps_tr"], ["mT"])
                for kvh in range(4):
                    pb = (kvh % 2) * 64
                    hp = kvh // 2
                    Qr = QT[pb:pb + 64, hp * 4:hp * 4 + 4, i * 128:(i + 1) * 128]
                    mTb = mT[pb:pb + 64, hp, :].rearrange("p (o q) -> p o q", o=1).to_broadcast([64, 4, 128])
                    for br in (1, 2):
                        kcs = list(range(0, qt + 1)) if br == 1 else list(range(max(0, qt - 4), qt + 1))
                        ob = PSO[br]
                        for kc in kcs:
                            bank = sbank()
                            diag = (kc == qt)
                            low = (br == 2 and kc == qt - 4)
                            last_simple = not (diag or low or br == 1)
                            mm(psn[bank][:], KT[br][pb:pb + 64, hp, kc * 128:(kc + 1) * 128], Qr, True, last_simple,
                               [("KT", br, kc // 4)] + QTT, [("ps", bank)])
                            if br == 1:
                                mm(psn[bank][:], Eb[pb:pb + 64, kc * 128:(kc + 1) * 128], mTb, False, not diag, ["Eb", "mT"], [("ps", bank)])
                            if diag:
                                mm(psn[bank][:], ident, TRIB, False, True, ["cb"], [("ps", bank)])
                            elif low:
                                mm(psn[bank][:], ident, TRIU, False, True, ["cb"], [("ps", bank)])
                            x = pt_n[0] % 3
                            pt_n[0] += 1
                            act(PT[x][:], psn[bank][:], AF.Exp, [("ps", bank)], [("PT", x)], scale=SCALE)
                            for g in range(4):
                                mm(psn[ob][:, g * 65:(g + 1) * 65], PT[x][:, g * 128:(g + 1) * 128], VV[br][:, kc, kvh, 0:65],
                                   g == 0 and kc == kcs[0], g == 3 and kc == kcs[-1], [("PT", x), ("V", br, kc)], [("ps", ob)])
                        zc = psn[ob][:, 0:260].rearrange("p (g e) -> p g e", e=65)[:, :, 64]
                        rzs = rz[:, 4 * br:4 * br + 4]
                        ts("dve", rzs, zc, 1e-30, None, ALU.max, None, [("ps", ob)], [("rz", kvh, br)])
                        S.op("dve", lambda e, rzs=rzs: e.reciprocal(rzs, rzs), [("rz", kvh, br)], [("rz", kvh, br)])
                        gview = Gt[:, i, kvh * 12:(kvh + 1) * 12].rearrange("p (g b) -> p b g", b=3)
                        tt("dve", coef[:, br, :], rzs, gview[:, br, :], ALU.mult, [("rz", kvh, br), ("Gt", i)], [("coef", br)])
                        ocv = psn[ob][:, 0:260].rearrange("p (g e) -> p g e", e=65)[:, :, 0:64]
                        cbb = coef[:, br, :].rearrange("p (g o) -> p g o", o=1).to_broadcast([128, 4, 64])
                        Ov = O[:, kvh * 256:(kvh + 1) * 256].rearrange("p (g d) -> p g d", d=64)
                        tt("dve", Otmp[:].rearrange("p (g d) -> p g d", d=64), ocv, cbb, ALU.mult, [("ps", ob), ("coef", br)], ["Otmp"])
                        tt("pool", O[:, kvh * 256:(kvh + 1) * 256], O[:, kvh * 256:(kvh + 1) * 256], Otmp[:], ALU.add, ["Otmp", ("O", kvh)], [("O", kvh)])
                OT = [("O", k) for k in range(4)]
                tt("pool", ozb[:], O[:], SZ[:, i, :], ALU.mult, OT + [("SZ", i)], ["ozb"])
                for kc in range(8):
                    tr(ps_tr[:, kc * 128:(kc + 1) * 128], ozb[:, kc * 128:(kc + 1) * 128], ["ozb"], ["ps_tr"])
                cp("dve", ozT[:], ps_tr[:].rearrange("p (k t) -> p k t", k=8), ["ps_tr"], ["ozT"])
                for half in range(2):
                    bank = sbank()
                    for kc in range(8):
                        mm(psn[bank][:], ozT[:, kc, :], wo_sb[:, kc, half * 512:(half + 1) * 512], kc == 0, kc == 7, ["ozT", "wo"], [("ps", bank)])
                    hv = Hq[:, half * 512:(half + 1) * 512]
                    tt("dve", hv, psn[bank][:], hv, ALU.add, [("ps", bank), "Hq"], ["Hq"])
                ssq = stat[:, 16:17]
                rstd = stat[:, 17:18]
                act(ozb[:], Hq[:], AF.Square, ["Hq"], ["ozb", "ssqf"], accum_out=ssq)
                act(ssq, ssq, AF.Sqrt, ["ssqf", "eps"], ["ssqf"], scale=1.0 / D, bias=eps_sb[:, 0:1])
                S.op("dve", lambda e, rstd=rstd, ssq=ssq: e.reciprocal(rstd, ssq), ["ssqf"], ["rstdf"])
                stt("dve", Yst[:], Hq[:], rstd, nf_sb[:], ALU.mult, ALU.mult, ["Hq", "rstdf", "nf"], ["Yst"])
                dma("pool", o_y_p[q0:q0 + 128, :], Yst[:], ["Yst"], [("o_y", qt)], "b_yst", is_out=True)
    S.emit()
    return nc


_CACHE = {}


def shared_inputs(inputs):
    f = lambda a: np.ascontiguousarray(np.asarray(a, dtype=np.float32))
    return {
        "norm_a": f(inputs["norm_a"][0]), "w_in": f(inputs["conv_w_in"][0]), "conv_w": f(inputs["conv_w"][0]),
        "w_out": f(inputs["conv_w_out"][0]), "norm_kv": f(inputs["norm_kv"]), "w_kv": f(inputs["w_kv"]),
        "cmp_pe": f(inputs["cmp_pe"]), "cmp_w1": f(inputs["cmp_w1"]), "cmp_w2": f(inputs["cmp_w2"]),
        "norm_b": f(inputs["norm_b"][0]), "w_qg": f(inputs["w_qg"][0]), "w_o": f(inputs["w_o"][0]),
        "norm_f": f(inputs["norm_f"]), "consts": make_consts(), "mems_in": make_mems(),
    }


def core_inputs(inputs, c, shared):
    f = lambda a: np.ascontiguousarray(np.asarray(a, dtype=np.float32))
    kvh, half = c % 4, c // 4
    bs = slice(SBC * half, SBC * half + SBC)
    m = dict(shared)
    m["x_p"] = f(inputs["x_prompt"][c])
    m["x_s"] = f(inputs["x_sample"][bs, 0, :])
    m["cconv_s"] = f(inputs["cache_conv"][0, bs])
    m["pool_cmp"] = f(inputs["cache_cmp_kv"][:, :, :, kvh, :]).reshape(5120 * 8, 2048)
    m["pool_sel"] = f(inputs["cache_sel_kv"][:, :, :, kvh, :]).reshape(5120 * 128, 128)
    m["win_s"] = f(inputs["cache_win_kv"][bs, :, :, kvh, :]).reshape(SBC, 512, 128)
    m["ptab_in"] = np.ascontiguousarray(np.asarray(inputs["page_table"][bs], dtype=np.int32))
    wqg = np.asarray(inputs["w_qg"][0], dtype=np.float32)
    hs = slice(kvh * 256, (kvh + 1) * 256)
    m["wqs_in"] = np.ascontiguousarray(np.concatenate([wqg[:, hs], wqg[:, 1024 + kvh * 12:1024 + (kvh + 1) * 12], wqg[:, 1072 + kvh * 256:1072 + (kvh + 1) * 256]], axis=1))
    wkv = np.asarray(inputs["w_kv"], dtype=np.float32).reshape(D, 3, 2, 4, 64)
    m["wkvs_in"] = np.ascontiguousarray(wkv[:, :, :, kvh, :].reshape(D, 384))
    m["wos_in"] = f(inputs["w_o"][0][hs, :])
    return m


def kernel(**inputs):
    if "nc" not in _CACHE:
        _CACHE["nc"] = build_program()
    nc = _CACHE["nc"]
    shared = shared_inputs(inputs)
    in_maps = [core_inputs(inputs, c, shared) for c in range(NCORES)]
    res = run_bass_kernel_spmd(nc, in_maps, core_ids=list(range(NCORES)))
    R = res.results
    B = NCORES
    y_prompt = np.stack([R[c]["o_y_p"] for c in range(B)])
    conv_p = np.stack([R[c]["o_conv_p"] for c in range(B)])[None]
    cmp_p = np.stack([R[c]["o_cmp_p"] for c in range(B)]).reshape(B, SEQ, 2, 4, 64)
    sel_p = np.stack([R[c]["o_sel_p"] for c in range(B)]).reshape(B, SEQ, 2, 4, 64)
    win_p = np.stack([R[c]["o_win_p"] for c in range(B)]).reshape(B, 512, 2, 4, 64)
    y_sample = np.zeros((DEC_B, 1, D), np.float32)
    conv_s = np.zeros((1, DEC_B, 2, DC), np.float32)
    cmp_s = np.zeros((DEC_B, 1, 2, 4, 64), np.float32)
    sel_s = np.zeros((DEC_B, 1, 2, 4, 64), np.float32)
    win_s = np.zeros((DEC_B, 512, 2, 4, 64), np.float32)
    for c in range(B):
        kvh, half = c % 4, c // 4
        bs = slice(SBC * half, SBC * half + SBC)
        kv = np.asarray(R[c]["o_kv_s"]).reshape(SBC, 3, 2, 64)
        cmp_s[bs, 0, :, kvh, :] = kv[:, 0]
        sel_s[bs, 0, :, kvh, :] = kv[:, 1]
        win_s[bs, :, :, kvh, :] = np.asarray(R[c]["o_win_s"]).reshape(SBC, 512, 2, 64)
        if kvh == 0:
            y_sample[bs, 0, :] = np.asarray(R[c]["o_y_s"])
            conv_s[0, bs] = np.asarray(R[c]["o_conv_s"])
    return (y_prompt, y_sample, conv_p, conv_s, cmp_p, cmp_s, sel_p, sel_s, win_p, win_s)
```

```python
import numpy as np
import concourse.bass as bass
import concourse.mybir as mybir
from concourse.bass_utils import run_bass_kernel_spmd

F32 = mybir.dt.float32
BF16 = mybir.dt.bfloat16
I32 = mybir.dt.int32
AF = mybir.ActivationFunctionType
ALU = mybir.AluOpType

NCORES = 8
D = 1024
SEQ = 4096
ST = 512
NST = SEQ // ST
DC = 2048
NCC = DC // 128
KVW = 1536
QGW = 2096
EPS = 1e-6
DEC_B = 32
SB = DEC_B // NCORES


class _Op:
    __slots__ = ("eng", "fn", "deps", "is_dma", "sem", "cum", "needs_inc", "count", "name", "inc")


class Sched:
    ENGS = ("pe", "act", "dve", "pool", "sp")

    def __init__(self, nc):
        self.nc = nc
        self.ops = {e: [] for e in self.ENGS}
        self.last_w = {}
        self.readers = {}
        self.dma_cum = {}
        self.out_dmas = []

    def op(self, eng, fn, reads=(), writes=(), dma=None, is_out=False, inc=16):
        o = _Op()
        o.eng = eng
        o.fn = fn
        o.is_dma = dma is not None
        o.sem = dma
        o.count = None
        deps = {}
        def add(d):
            if d is o:
                return
            if (not d.is_dma) and d.eng == "pe" and eng == "pe" and dma is None:
                return
            deps[id(d)] = d
        for t in reads:
            w = self.last_w.get(t)
            if w is not None:
                add(w)
        for t in writes:
            w = self.last_w.get(t)
            if w is not None:
                add(w)
            for r in self.readers.get(t, ()):
                add(r)
        o.deps = list(deps.values())
        for t in reads:
            self.readers.setdefault(t, []).append(o)
        for t in writes:
            self.last_w[t] = o
            self.readers[t] = []
        if o.is_dma:
            self.dma_cum[dma] = self.dma_cum.get(dma, 0) + inc
            o.inc = inc
            o.cum = self.dma_cum[dma]
            if is_out:
                self.out_dmas.append(o)
        o.needs_inc = o.is_dma
        for d in o.deps:
            d.needs_inc = True
        self.ops[eng].append(o)
        return o

    def barrier(self):
        lasts = []
        for e in self.ENGS:
            for o in reversed(self.ops[e]):
                if (not o.is_dma) and o.fn is not None:
                    o.needs_inc = True
                    lasts.append(o)
                    break
        seen = set()
        for e in self.ENGS:
            for o in reversed(self.ops[e]):
                if o.is_dma and o.sem not in seen:
                    seen.add(o.sem)
                    lasts.append(o)
        for e in self.ENGS:
            b = _Op()
            b.eng = e
            b.fn = None
            b.is_dma = False
            b.sem = None
            b.count = None
            b.needs_inc = False
            b.deps = list(lasts)
            self.ops[e].append(b)

    def emit(self):
        nc = self.nc
        eng_sem = {e: nc.alloc_semaphore("sem_" + e) for e in ("pe", "act", "dve", "pool")}
        dma_sem = {k: nc.alloc_semaphore("dsem_" + str(k)) for k in self.dma_cum}
        for e in self.ENGS:
            c = 0
            for o in self.ops[e]:
                if o.needs_inc and not o.is_dma:
                    c += 1
                    o.count = c
        final_waits = {}
        for o in self.out_dmas:
            final_waits[o.sem] = max(final_waits.get(o.sem, 0), self.dma_cum[o.sem])

        def run(e, engine):
            known = {}
            for o in self.ops[e]:
                need = {}
                for d in o.deps:
                    if d.is_dma:
                        key = ("d", d.sem)
                        val = d.cum
                    else:
                        key = ("e", d.eng)
                        val = d.count
                    if need.get(key, 0) < val:
                        need[key] = val
                for key, val in need.items():
                    if known.get(key, 0) >= val:
                        continue
                    known[key] = val
                    sem = dma_sem[key[1]] if key[0] == "d" else eng_sem[key[1]]
                    engine.wait_ge(sem, val)
                if o.fn is None:
                    continue
                ins = o.fn(engine)
                if o.needs_inc:
                    if o.is_dma:
                        ins.then_inc(dma_sem[o.sem], o.inc)
                    else:
                        ins.then_inc(eng_sem[o.eng], 1)
            if e == "sp":
                for k, v in final_waits.items():
                    engine.wait_ge(dma_sem[k], v)

        with nc.Block() as block:
            @block.tensor
            def _(eng):
                run("pe", eng)

            @block.scalar
            def _(eng):
                run("act", eng)

            @block.vector
            def _(eng):
                run("dve", eng)

            @block.gpsimd
            def _(eng):
                run("pool", eng)

            @block.sync
            def _(eng):
                run("sp", eng)


BIG = 30000.0
SCALE = 0.125
C_ID, C_TRIB, C_TRIU, C_MUL, C_ADD, C_VAL, C_MEM, C_STAIR, C_ZSEL, C_EB = 0, 128, 256, 384, 512, 640, 768, 896, 1024, 1288
C_MCOL = C_EB + 4096
C_IOTAR = C_MCOL + 4
C_W = C_IOTAR + 128
SBC = 16
NMEM = 8 * 257


def make_consts():
    c = np.zeros((128, C_W), np.float32)
    c[:, C_ID:C_ID + 128] = np.eye(128)
    kk = np.arange(128)[:, None]
    ql = np.arange(128)[None, :]
    c[:, C_TRIB:C_TRIB + 128] = np.where(kk <= ql, 0.0, -BIG)
    c[:, C_TRIU:C_TRIU + 128] = np.where(kk >= ql, 0.0, -BIG)
    qq = np.arange(128)[:, None]
    jrel = np.arange(128)[None, :] - 64
    lo = qq < 64
    mul = np.zeros((128, 128), np.float32)
    add = np.zeros((128, 128), np.float32)
    mul[:] = np.where(jrel <= -2, 1.0, 0.0)
    mul += np.where((jrel == -1) & ~lo, 1.0, 0.0)
    add += np.where(jrel >= 2, -1e9, 0.0)
    add += np.where((jrel == -1) & lo, 1e9, 0.0)
    add += np.where((jrel == 0) & lo, 2e9, 0.0)
    add += np.where((jrel == 0) & ~lo, 1e9, 0.0)
    add += np.where((jrel == 1) & lo, -1e9, 0.0)
    add += np.where((jrel == 1) & ~lo, 2e9, 0.0)
    c[:, C_MUL:C_MUL + 128] = mul
    c[:, C_ADD:C_ADD + 128] = add
    c[:, C_VAL:C_VAL + 128] = (add > -0.5e9).astype(np.float32)
    for ch in range(2):
        n = ch * 128 + np.arange(128)[:, None]
        j = np.arange(64)[None, :]
        m = ((n >= 4 * j - 1) & (n <= 4 * j + 3) & (n < 255)).astype(np.float32)
        c[:, C_MEM + ch * 64:C_MEM + (ch + 1) * 64] = m
    for k in range(8):
        rel = k - 1
        c[k, C_STAIR:C_STAIR + 128] = np.where(np.arange(128) >= 16 * rel + 31, 0.0, -BIG)
        c[k, C_ZSEL + 128 + k] = 1.0
    for j in range(64):
        c[j, C_EB + 64 * j:C_EB + 64 * j + 64] = 1.0
        c[64 + j, C_EB + 64 * j:C_EB + 64 * j + 64] = 1.0
    p = np.arange(128)
    c[:, C_MCOL + 0] = (p <= 64)
    c[:, C_MCOL + 1] = (p == 0)
    c[:, C_MCOL + 2] = p % 64
    c[0, C_IOTAR:C_IOTAR + 128] = np.arange(128)
    return c


def make_mems():
    m = np.zeros((128, 8, 257), np.float32)
    for ch in range(8):
        n = ch * 128 + np.arange(128)[:, None]
        j = np.arange(257)[None, :]
        m[:, ch, :] = ((n >= 4 * j - 1) & (n <= 4 * j + 3) & (n < 1023))
    return m.reshape(128, NMEM)


def build_program(nst=NST, with_b=True, with_s=True, use_cc=True):
    import os
    DBG_S = os.environ.get("KDBG", "")
    from contextlib import ExitStack
    nc = bass.Bass("TRN2", target_bir_lowering=False)
    S = Sched(nc)

    def din(name, shape, dt=F32):
        return nc.dram_tensor(name, list(shape), dt, kind="ExternalInput").ap()

    def dout(name, shape, dt=F32):
        return nc.dram_tensor(name, list(shape), dt, kind="ExternalOutput").ap()

    def dscr(name, shape, dt):
        return nc.dram_tensor(name, list(shape), dt).ap()

    x_p = din("x_p", [SEQ, D])
    norm_a = din("norm_a", [D])
    w_in = din("w_in", [D, 4 * DC])
    conv_w = din("conv_w", [3, DC])
    w_out = din("w_out", [DC, D])
    norm_kv = din("norm_kv", [D])
    w_kv = din("w_kv", [D, KVW])
    cmp_pe = din("cmp_pe", [2, 32, 64])
    cmp_w1 = din("cmp_w1", [2, 2048, 64])
    cmp_w2 = din("cmp_w2", [2, 64, 64])
    norm_b = din("norm_b", [D])
    w_qg = din("w_qg", [D, QGW])
    w_o = din("w_o", [D, D])
    norm_f = din("norm_f", [D])
    consts = din("consts", [128, C_W])
    if with_s:
        NPOOL = 8 if "tinypool" in DBG_S else 5120
        if "ccdbg" in DBG_S:
            o_dpre = dout("o_dpre", [SBC, D])
            o_dpost = dout("o_dpost", [SBC, D])
        x_s = din("x_s", [SBC, D])
        cconv_s = din("cconv_s", [SBC, 2, DC])
        pool_cmp = din("pool_cmp", [NPOOL * 8, 2048])
        pool_sel = din("pool_sel", [NPOOL * 128, 128])
        win_s = din("win_s", [SBC, 512, 128])
        ptab_in = din("ptab_in", [SBC, 128], I32)
        wqs_in = din("wqs_in", [D, 524])
        wkvs_in = din("wkvs_in", [D, 384])
        wos_in = din("wos_in", [256, D])
        mems_in = din("mems_in", [128, NMEM])
        o_y_s = dout("o_y_s", [SBC, D])
        o_conv_s = dout("o_conv_s", [SBC, 2, DC])
        o_kv_s = dout("o_kv_s", [SBC, 384])
        o_win_s = dout("o_win_s", [SBC, 512, 128])
        if not use_cc:
            o_part = dout("o_part", [SBC, D])
        s_wqs = dscr("s_wqs", [128, 8, 524], BF16)
        s_wkvs = dscr("s_wkvs", [128, 8, 384], BF16)
        s_wos = dscr("s_wos", [128, 2, D], BF16)
        s_mems = dscr("s_mems", [128, NMEM], BF16)
        s_o3 = dscr("s_o3", [SBC, 4, 3, 65], F32)
        cc_in = dscr("cc_in", [SBC, D], F32)
        s_gbi = dscr("s_gbi", [SBC, 16], I32)
        cc_out = dscr("cc_out", [SBC, D], F32)

    o_y_p = dout("o_y_p", [SEQ, D])
    o_conv_p = dout("o_conv_p", [2, DC])
    o_cmp_p = dout("o_cmp_p", [SEQ, 512])
    o_sel_p = dout("o_sel_p", [SEQ, 512])
    o_win_p = dout("o_win_p", [512, 512])

    s_h = dscr("s_h", [SEQ, D], F32)
    s_hnT = dscr("s_hnT", [NST, 128, 8 * ST], BF16)
    s_win = dscr("s_win", [NCC, 128, 8, 512], BF16)
    s_wkv = dscr("s_wkv", [3, 128, 8, 512], BF16)
    s_wout = dscr("s_wout", [128, NCC, D], BF16)
    s_wq = dscr("s_wq", [128, 8, 1024], BF16)
    s_wgz = dscr("s_wgz", [128, 8, 1072], BF16)
    s_wo = dscr("s_wo", [128, 8, 1024], BF16)
    s_w1 = dscr("s_w1", [2, 128, 2048], BF16)
    s_eb = dscr("s_eb", [128, 4096], BF16)

    def sbp(name, shape, dt=F32):
        return nc.alloc_sbuf_tensor(name, list(shape), dt)

    cb = sbp("cb", [128, C_EB], BF16)
    eps_sb = sbp("eps_sb", [128, 1])
    stat = sbp("stat", [128, 32])
    ident = cb[:, C_ID:C_ID + 128]

    psn = {}
    for i in (0, 1, 2, 3, 5, 6, 7):
        psn[i] = nc.alloc_psum_tensor("ps%d" % i, [128, 512], F32)
    ps_tr = nc.alloc_psum_tensor("ps_tr", [128, 1024], BF16)

    def dma(eng, out, in_, reads, writes, sem, is_out=False, **kw):
        return S.op(eng, lambda e: e.dma_start(out=out, in_=in_, **kw), reads, writes, dma=sem, is_out=is_out)

    def mm(out, lhsT, rhs, start, stop, reads, writes):
        return S.op("pe", lambda e: e.matmul(out, lhsT, rhs, start=start, stop=stop), reads, writes)

    def tr(out, in_, reads, writes):
        kk = in_.shape[0]
        return S.op("pe", lambda e: e.transpose(out, in_, cb[0:kk, C_ID:C_ID + kk]), list(reads) + ["cb"], writes)

    trf_ident = [None]

    def trf(out, in_, reads, writes):
        kk = in_.shape[0]
        idf = trf_ident[0]
        return S.op("pe", lambda e: e.transpose(out, in_, idf[0:kk, 0:kk]), list(reads) + ["identf"], writes)

    def act(out, in_, func, reads, writes, **kw):
        return S.op("act", lambda e: e.activation(out, in_, func, **kw), reads, writes)

    def tt(eng, out, in0, in1, op, reads, writes):
        return S.op(eng, lambda e: e.tensor_tensor(out, in0, in1, op), reads, writes)

    def ts(eng, out, in0, s1, s2, op0, op1, reads, writes):
        if s2 is None:
            return S.op(eng, lambda e: e.tensor_scalar(out, in0, s1, None, op0), reads, writes)
        return S.op(eng, lambda e: e.tensor_scalar(out, in0, s1, s2, op0, op1), reads, writes)

    def stt(eng, out, in0, sc, in1, op0, op1, reads, writes):
        return S.op(eng, lambda e: e.scalar_tensor_tensor(out, in0, sc, in1, op0, op1), reads, writes)

    def cp(eng, out, in_, reads, writes):
        if eng == "act":
            return act(out, in_, AF.Copy, reads, writes)
        return S.op(eng, lambda e: e.tensor_copy(out, in_), reads, writes)

    def memset(eng, ap, val, writes):
        return S.op(eng, lambda e: e.memset(ap, val), [], writes)

    with ExitStack() as es:
        def sb(name, shape, dt=F32):
            return es.enter_context(nc.sbuf_tensor(name, list(shape), dt))
        cst = sb("cst", [128, C_W])
        stage = [sb("stage%d" % i, [128, 2304]) for i in range(2)]
        stageb = [sb("stageb%d" % i, [128, 2304], BF16) for i in range(2)]
        na_sb = sb("na_sb", [128, 8])
        nkv_sb = sb("nkv_sb", [128, 8])
        nb_sb = sb("nb_sb", [128, 8])

        dma("sp", cst[:], consts, [], ["cst"], "c_cst")
        cp("dve", cb[:], cst[:, 0:C_EB], ["cst"], ["cb"])
        ebst = sb("ebst", [128, 4096], BF16)
        cp("act", ebst[:], cst[:, C_EB:C_EB + 4096], ["cst"], ["ebst"])
        dma("pool", s_eb, ebst[:], ["ebst"], ["s_w"], "c_ebst")
        dma("sp", na_sb[:], norm_a.rearrange("(k p) -> p k", p=128), [], ["na"], "c_na", allow_slow_non_contiguous=True)
        dma("sp", nkv_sb[:], norm_kv.rearrange("(k p) -> p k", p=128), [], ["nkv"], "c_nkv", allow_slow_non_contiguous=True)
        dma("sp", nb_sb[:], norm_b.rearrange("(k p) -> p k", p=128), [], ["nb"], "c_nb", allow_slow_non_contiguous=True)
        memset("pool", eps_sb[:], EPS, ["eps"])

        prep_i = [0]

        def prep(src_ap, width, scale_ap, scale_tok, dst_ap, dst_tok, src_view=None, permute_q=False):
            i = prep_i[0] % 2
            n = prep_i[0]
            prep_i[0] += 1
            st_t, sb_t = "stage%d" % i, "stageb%d" % i
            dma("sp", stage[i][:, 0:width], src_ap, [], [st_t], "prep_ld%d" % i)
            rd = [st_t] + ([scale_tok] if scale_tok else [])
            o_, i_ = stageb[i][:, 0:width], stage[i][:, 0:width]
            if permute_q:
                for a in range(2):
                    ov = stageb[i][:, a * 512:(a + 1) * 512].rearrange("p (g h d) -> p g h d", g=4, h=2, d=64)
                    iv = stage[i][:, a * 512:(a + 1) * 512].rearrange("p (h g d) -> p g h d", g=4, h=2, d=64)
                    ts("dve" if a == 0 else "pool", ov, iv, scale_ap, None, ALU.mult, None, rd, [sb_t])
            elif scale_ap is None:
                cp("dve" if n % 2 == 0 else "pool", o_, i_, rd, [sb_t])
            elif n % 4 < 2:
                ts("dve", o_, i_, scale_ap, None, ALU.mult, None, rd, [sb_t])
            else:
                act(o_, i_, AF.Copy, rd, [sb_t], scale=scale_ap)
            src = o_ if src_view is None else src_view(o_)
            dma("pool", dst_ap, src, [sb_t], [dst_tok], "prep_st%d" % i)

        for kc in range(8):
            for part in range(4):
                prep(w_in[kc * 128:(kc + 1) * 128, part * DC:(part + 1) * DC], DC, na_sb[:, kc:kc + 1], "na",
                     s_win[:, :, kc, part * 128:(part + 1) * 128].rearrange("c p j -> p c j"), "s_w",
                     src_view=lambda a: a.rearrange("p (c j) -> p c j", j=128))
        for cc in range(NCC):
            prep(w_out[cc * 128:(cc + 1) * 128, :], D, None, None, s_wout[:, cc, :], "s_w")
        for kc in range(8):
            prep(w_kv[kc * 128:(kc + 1) * 128, :], KVW, nkv_sb[:, kc:kc + 1], "nkv",
                 s_wkv[:, :, kc, :].rearrange("b p j -> p b j"), "s_w",
                 src_view=lambda a: a.rearrange("p (b j) -> p b j", j=512))
        for kc in range(8):
            prep(w_qg[kc * 128:(kc + 1) * 128, 0:1024], 1024, nb_sb[:, kc:kc + 1], "nb", s_wq[:, kc, :], "s_w", permute_q=True)
            prep(w_qg[kc * 128:(kc + 1) * 128, 1024:QGW], 1072, nb_sb[:, kc:kc + 1], "nb", s_wgz[:, kc, :], "s_w")
            prep(w_o[kc * 128:(kc + 1) * 128, :], D, None, None, s_wo[:, kc, :], "s_w")
        if with_s:
            for kc in range(8):
                prep(wqs_in[kc * 128:(kc + 1) * 128, :], 524, nb_sb[:, kc:kc + 1], "nb", s_wqs[:, kc, :], "s_w")
                prep(wkvs_in[kc * 128:(kc + 1) * 128, :], 384, nkv_sb[:, kc:kc + 1], "nkv", s_wkvs[:, kc, :], "s_w")
            for kc in range(2):
                prep(wos_in[kc * 128:(kc + 1) * 128, :], D, None, None, s_wos[:, kc, :], "s_w")
            prep(mems_in, NMEM, None, None, s_mems, "s_w")
        for c in range(2):
            i = prep_i[0] % 2
            prep_i[0] += 1
            for hf in range(2):
                dma("sp", stage[i][hf * 64:(hf + 1) * 64, 0:2048].rearrange("d (j e) -> d j e", e=64),
                    cmp_w1[c].rearrange("(j d) e -> d j e", d=64), [], ["stage%d" % i], "prep_ld%d" % i)
            cp("dve", stageb[i][:, 0:2048], stage[i][:, 0:2048], ["stage%d" % i], ["stageb%d" % i])
            dma("pool", s_w1[c], stageb[i][:, 0:2048], ["stageb%d" % i], ["s_w"], "prep_st%d" % i)
        S.barrier()

    if with_s:
      with ExitStack() as es:
        def sb(name, shape, dt=F32):
            return es.enter_context(nc.sbuf_tensor(name, list(shape), dt))
        NB = SBC
        xs_sb = sb("xs_sb", [NB, D])
        hs_sb = sb("hs_sb", [NB, D])
        xsn = sb("xsn", [NB, D], BF16)
        xsT = sb("xsT", [128, 8, NB], BF16)
        hsT = sb("hsT", [128, 8, NB], BF16)
        wch = [sb("wch%d" % i, [128, 8, 512], BF16) for i in range(2)]
        woc = [sb("woc%d" % i, [128, D], BF16) for i in range(2)]
        cw_s = sb("cw_s", [128, NCC, 3])
        ccin = [sb("ccin%d" % i, [2 * NB, 128]) for i in range(2)]
        ccT = sb("ccT", [128, NCC, 2 * NB])
        uT = sb("uT", [128, NCC, NB])
        u_tok = [sb("u_tok%d" % i, [NB, 128]) for i in range(2)]
        sm = [sb("sm%d" % i, [128, 4 * NB]) for i in range(6)]
        gTs = sb("gTs", [128, NB], BF16)
        wkvs_sb = sb("wkvs_sb", [128, 8, 384], BF16)
        wqs_sb = sb("wqs_sb", [128, 8, 524], BF16)
        wos_sb = sb("wos_sb", [128, 2, D], BF16)
        kvs_sb = sb("kvs_sb", [NB, 384])
        KnT = sb("KnT", [64, 3, NB], BF16)
        QsT = sb("QsT", [64, 4, NB], BF16)
        gz_sb = sb("gz_sb", [NB, 268])
        nf_s = sb("nf_s", [NB, D])
        ptab = sb("ptab", [128, NB], I32)
        ptf = sb("ptf", [128, NB])
        idx8 = sb("idx8", [128, 8], I32)
        idxf = sb("idxf", [128, 8])
        stg = [sb("stg%d" % i, [128, 2048]) for i in range(2)]
        stgb = [sb("stgb%d" % i, [128, 2048], BF16) for i in range(2)]
        RTs = [sb("RT%d" % i, [128, 16384 + 128], BF16) for i in range(2)]
        W1c = sb("W1c", [128, 32, 128], BF16)
        W2c = sb("W2c", [128, 128], BF16)
        W2cf = sb("W2cf", [128, 64])
        pebc = sb("pebc", [128, 1])
        peTc = sb("peTc", [128, 32], BF16)
        peTcf = sb("peTcf", [128, 32])
        hidS = sb("hidS", [128, 1024], BF16)
        KcS = sb("KcS", [128, 1024], BF16)
        VcS = sb("VcS", [128, 8, 65], BF16)
        mems = sb("mems", [128, 8, 257], BF16)
        PcS = sb("PcS", [128, 8, 4], BF16)
        impn = sb("impn", [4, 257])
        scr_s = sb("scr_s", [1, 257])
        scr_s2 = sb("scr_s2", [1, 257])
        m8s = sb("m8s", [1, 16])
        i8s = sb("i8s", [1, 16], mybir.dt.uint32)
        jf = sb("jf", [1, 16])
        jh = sb("jh", [1, 16])
        gbf = sb("gbf", [1, 16])
        gidx = sb("gidx", [128, 8], I32)
        pt1 = sb("pt1", [1, 128 + 8], I32)
        pt1f = sb("pt1f", [1, 128 + 8])
        onehot = sb("onehot", [1, 128])
        pgf = sb("pgf", [1, 16])
        gbi = sb("gbi", [1, 16], I32)
        identf = sb("identf", [128, 128])
        ones4 = sb("ones4", [4, 1])
        trf_ident[0] = identf
        iota_row = sb("iota_row", [1, 128])
        graw = sb("graw", [128, 8], I32)
        grawf = sb("grawf", [128, 8])
        selt = sb("selt", [128, 8, 128])
        selb = sb("selb", [128, 8, 129], BF16)
        KsT = sb("KsT", [64, 8, 128], BF16)
        PsS = sb("PsS", [128, 8, 4], BF16)
        wint = sb("wint", [128, 5, 128])
        winb = sb("winb", [128, 5, 129], BF16)
        KwT = sb("KwT", [64, 5, 128], BF16)
        PwS = sb("PwS", [128, 5, 4], BF16)
        o3 = sb("o3", [4, 3, 65])
        Oraw = sb("Oraw", [NB, 4, 3, 65])
        rzs_ = sb("rzs_", [NB, 4, 3])
        coefs = sb("coefs", [NB, 4, 3])
        Os = sb("Os", [NB, 256])
        Otm = sb("Otm", [NB, 256])
        ozs = sb("ozs", [NB, 256], BF16)
        ozsT = sb("ozsT", [128, 2, NB], BF16)
        part_sb = sb("part_sb", [NB, D])
        ys_sb = xs_sb
        mcol = sb("mcol", [128, 4])

        dma("sp", xs_sb[:], x_s, [], ["xs"], "s_xs")
        for j in range(3):
            dma("sp", cw_s[:, :, j], conv_w[j, :].rearrange("(c p) -> p c", p=128), [], ["cw_s"], "s_cw", allow_slow_non_contiguous=True)
        dma("sp", wkvs_sb[:], s_wkvs, [], ["wkvs"], "s_wkvs")
        dma("sp", wqs_sb[:], s_wqs, [], ["wqs"], "s_wqs")
        dma("sp", wos_sb[:], s_wos, [], ["wos"], "s_wos")
        dma("sp", nf_s[:], norm_f.partition_broadcast(NB), [], ["nf_s"], "s_nf")
        dma("sp", ptab[:], ptab_in.rearrange("b i -> i b"), [], ["ptab"], "s_pt", allow_slow_non_contiguous=True)
        dma("sp", mems[:].rearrange("p a j -> p (a j)"), s_mems, [], ["mems"], "s_mems")
        dma("sp", mcol[:], consts[:, C_MCOL:C_MCOL + 4], [], ["mcol"], "s_mcol")
        dma("sp", identf[:], consts[:, C_ID:C_ID + 128], [], ["identf"], "s_identf")
        dma("sp", iota_row[:], consts[0:1, C_IOTAR:C_IOTAR + 128], [], ["iota_row"], "s_iotar")
        memset("pool", ones4[:], 1.0, ["ones4"])
        memset("pool", W1c[:], 0.0, ["W1c"])
        memset("pool", W2c[:], 0.0, ["W2c"])
        memset("pool", VcS[:], 1.0, ["VcS"])
        memset("pool", selb[:], 1.0, ["selb"])
        memset("pool", winb[:], 1.0, ["winb"])
        memset("pool", wint[:], 0.0, ["wint"])
        for c in range(2):
            dma("sp", W1c[c * 64:(c + 1) * 64, :, c * 64:(c + 1) * 64],
                s_w1[c][c * 64:(c + 1) * 64, :].rearrange("p (j e) -> p j e", e=64), ["W1c"], ["W1c"], "s_w1c")
            dma("sp", W2cf[c * 64:(c + 1) * 64, :], cmp_w2[c], [], ["W2cf"], "s_w2c")
            dma("sp", peTcf[c * 64:(c + 1) * 64, :], cmp_pe[c].rearrange("j d -> d j"), [], ["peTcf"], "s_pec", allow_slow_non_contiguous=True)
        for c in range(2):
            cp("dve", W2c[c * 64:(c + 1) * 64, c * 64:(c + 1) * 64], W2cf[c * 64:(c + 1) * 64, :], ["W2cf", "W2c"], ["W2c"])
        cp("dve", peTc[:], peTcf[:], ["peTcf"], ["peTc"])
        for j in range(32):
            mm(psn[7][:, 0:1], W1c[:, j, :], peTc[:, j:j + 1], j == 0, j == 31, ["W1c", "peTc"], [("ps", 7)])
        cp("act", pebc[:], psn[7][:, 0:1], [("ps", 7)], ["pebc"])
        cp("dve", ptf[:], ptab[:], ["ptab"], ["ptf"])

        def small_norm_T(src, src_tok, dstT, dst_tok, slot):
            ssq = stat[0:NB, slot:slot + 1]
            rstd = stat[0:NB, slot + 1:slot + 2]
            act(xsn[:], src, AF.Square, [src_tok], ["xsn", ("sst", slot)], accum_out=ssq)
            act(ssq, ssq, AF.Sqrt, [("sst", slot), "eps"], [("sst", slot)], scale=1.0 / D, bias=eps_sb[0:NB, 0:1])
            S.op("dve", lambda e, rstd=rstd, ssq=ssq: e.reciprocal(rstd, ssq), [("sst", slot)], [("srstd", slot)])
            act(xsn[:], src, AF.Copy, [src_tok, ("srstd", slot)], ["xsn"], scale=rstd)
            for kc in range(8):
                tr(ps_tr[:, kc * NB:(kc + 1) * NB], xsn[:, kc * 128:(kc + 1) * 128], ["xsn"], ["ps_tr"])
            cp("act", dstT[:], ps_tr[:, 0:8 * NB].rearrange("p (k t) -> p k t", k=8), ["ps_tr"], [dst_tok])

        small_norm_T(xs_sb[:], "xs", xsT, "xsT", 20)
        for cc in range(NCC):
            bank = (5, 6)[cc % 2]
            k = cc % 2
            dma("sp", ccin[k][:], cconv_s[:, :, cc * 128:(cc + 1) * 128].rearrange("b r c -> (b r) c"), [], [("ccin", k)], "s_cc%d" % k)
            trf(psn[bank][:, 0:2 * NB], ccin[k][:], [("ccin", k)], [("ps", bank)])
            cp("act", ccT[:, cc, :], psn[bank][:, 0:2 * NB], [("ps", bank)], [("ccT", cc)])
        for cc in range(NCC):
            k = cc % 2
            dma("sp", wch[k][:], s_win[cc], [], [("wch", k)], "s_wch%d" % k)
            dma("sp", woc[k][:], s_wout[:, cc, :], [], [("woc", k)], "s_woc%d" % k)
            bank = (2, 3)[k]
            for part in range(4):
                for kc in range(8):
                    mm(psn[bank][:, part * NB:(part + 1) * NB], wch[k][:, kc, part * 128:(part + 1) * 128], xsT[:, kc, :],
                       part == 0 and kc == 0, part == 3 and kc == 7, [("wch", k), "xsT"], [("ps", bank)])
            pb_, pc_, ph_, pz_ = [psn[bank][:, p * NB:(p + 1) * NB] for p in range(4)]
            s0, s1, s2, s3, s4, s5 = [sm[i][:, 0:NB] for i in range(6)]
            cp("act", s0, pc_, [("ps", bank)], [("sm", 0)])
            tt("dve", uT[:, cc, :], s0, ph_, ALU.mult, [("sm", 0), ("ps", bank)], [("uT", cc)])
            ccv = ccT[:, cc, :].rearrange("p (b r) -> p r b", r=2)
            ts("dve", s1, uT[:, cc, :], cw_s[:, cc, 2:3], None, ALU.mult, None, [("uT", cc), "cw_s"], [("sm", 1)])
            stt("dve", s1, ccv[:, 1, :], cw_s[:, cc, 1:2], s1, ALU.mult, ALU.add, [("ccT", cc), "cw_s", ("sm", 1)], [("sm", 1)])
            stt("dve", s1, ccv[:, 0, :], cw_s[:, cc, 0:1], s1, ALU.mult, ALU.add, [("ccT", cc), "cw_s", ("sm", 1)], [("sm", 1)])
            act(s2, pz_, AF.Silu, [("ps", bank)], [("sm", 2)])
            tt("dve", s3, pb_, s2, ALU.mult, [("ps", bank), ("sm", 2)], [("sm", 3)])
            tt("dve", gTs[:], s3, s1, ALU.mult, [("sm", 3), ("sm", 1)], ["gTs"])
            for half in range(2):
                mm(psn[half][0:NB, :], gTs[:], woc[k][:, half * 512:(half + 1) * 512], cc == 0, cc == NCC - 1,
                   ["gTs", ("woc", k)], [("ps", half)])
        for half in range(2):
            tt("dve", hs_sb[:, half * 512:(half + 1) * 512], psn[half][0:NB, :], xs_sb[:, half * 512:(half + 1) * 512], ALU.add,
               [("ps", half), "xs"], ["hs"])
        for cc in range(NCC):
            bank = (5, 6)[cc % 2]
            k = cc % 2
            trf(psn[bank][0:NB, 0:128], uT[:, cc, :], [("uT", cc)], [("ps", bank)])
            cp("act", u_tok[k][:], psn[bank][0:NB, 0:128], [("ps", bank)], [("u_tok", k)])
            dma("pool", o_conv_s[:, 1, cc * 128:(cc + 1) * 128], u_tok[k][:], [("u_tok", k)], [("o_conv_s1", cc)], "s_ocs1_%d" % k, is_out=True)
        dma("pool", o_conv_s[:, 0, :], cconv_s[:, 1, :], [], ["o_conv_s0"], "s_ocs0", is_out=True)
        small_norm_T(hs_sb[:], "hs", hsT, "hsT", 22)
        for kc in range(8):
            mm(psn[5][0:NB, 0:384], hsT[:, kc, :], wkvs_sb[:, kc, :], kc == 0, kc == 7, ["hsT", "wkvs"], [("ps", 5)])
        cp("act", kvs_sb[:], psn[5][0:NB, 0:384], [("ps", 5)], ["kvs"])
        dma("pool", o_kv_s, kvs_sb[:], ["kvs"], ["o_kv_s"], "s_okv", is_out=True)
        for br in range(3):
            for kc in range(8):
                mm(psn[6][0:64, br * NB:(br + 1) * NB], wkvs_sb[:, kc, br * 128:br * 128 + 64], hsT[:, kc, :],
                   br == 0 and kc == 0, br == 2 and kc == 7, ["hsT", "wkvs"], [("ps", 6)])
        cp("act", KnT[:], psn[6][0:64, 0:3 * NB].rearrange("p (a b) -> p a b", a=3), [("ps", 6)], ["KnT"])
        for g in range(4):
            for kc in range(8):
                mm(psn[7][0:64, g * NB:(g + 1) * NB], wqs_sb[:, kc, g * 64:(g + 1) * 64], hsT[:, kc, :],
                   g == 0 and kc == 0, g == 3 and kc == 7, ["hsT", "wqs"], [("ps", 7)])
        cp("act", QsT[:], psn[7][0:64, 0:4 * NB].rearrange("p (a b) -> p a b", a=4), [("ps", 7)], ["QsT"])
        for kc in range(8):
            mm(psn[5][0:NB, 0:268], hsT[:, kc, :], wqs_sb[:, kc, 256:524], kc == 0, kc == 7, ["hsT", "wqs"], [("ps", 5)])
        act(gz_sb[:, 0:12], psn[5][0:NB, 0:12], AF.Sigmoid, [("ps", 5)], ["gz"])
        act(gz_sb[:, 12:268], psn[5][0:NB, 12:268], AF.Silu, [("ps", 5), "gz"], ["gz"])
        dma("pool", o_win_s[:, 0:511, :], win_s[:, 1:512, :], [], ["o_win_s0"], "s_ows0", is_out=True)
        dma("pool", o_win_s[:, 511, :], kvs_sb[:, 256:384], ["kvs"], ["o_win_s1"], "s_ows1", is_out=True)

        def gather_chunk(b, m):
            RT = RTs[b % 2]
            RTt = ("RT", b % 2)
            if True:
                k = m % 2
                ts("dve", idxf[:, m:m + 1], ptf[:, b:b + 1], 8.0, float(m), ALU.mult, ALU.add, ["ptf"], [("idxf", m)])
                cp("dve", idx8[:, m:m + 1], idxf[:, m:m + 1], [("idxf", m)], [("idx8", m)])
                S.op("pool", lambda e, k=k, m=m: e.indirect_dma_start(
                    out=stg[k][:], out_offset=None, in_=pool_cmp,
                    in_offset=bass.IndirectOffsetOnAxis(ap=idx8[:, m:m + 1], axis=0)),
                    [("idx8", m)], [("stg", k)], dma="s_stg%d" % k)
                cp("act" if m % 2 else "dve", stgb[k][:], stg[k][:], [("stg", k)], [("stgb", k)])
                for rh in range(2):
                    for r8 in range(8):
                        r = rh * 8 + r8
                        tr(ps_tr[:, r8 * 128:(r8 + 1) * 128], stgb[k][:, r * 128:(r + 1) * 128], [("stgb", k)], ["ps_tr"])
                    base = 16 * m + rh * 8
                    dstv = RT[:, base:base + 16384].rearrange("p (i x) -> p x i", x=128)[:, 0:8, :]
                    cp("act", dstv, ps_tr[:].rearrange("p (r i) -> p r i", r=8), ["ps_tr"], [RTt])
        def attn_body(b):
            RT = RTs[b % 2]
            RTt = ("RT", b % 2)
            for ci, (n0, nn) in enumerate(((0, 512), (512, 511))):
                bank = (0, 1)[ci]
                for j in range(32):
                    c0 = 16 * n0 + j
                    mm(psn[bank][:, 0:nn], W1c[:, j, :], RT[:, c0:c0 + 16 * (nn - 1) + 1:16], j == 0, j == 31, ["W1c", RTt], [("ps", bank)])
                act(hidS[:, n0:n0 + nn], psn[bank][:, 0:nn], AF.Silu, [("ps", bank), "pebc"], ["hidS"], bias=pebc[:, 0:1])
            yield
            for ci, (n0, nn) in enumerate(((0, 512), (512, 511))):
                bank = (2, 3)[ci]
                mm(psn[bank][:, 0:nn], W2c[:], hidS[:, n0:n0 + nn], True, True, ["W2c", "hidS"], [("ps", bank)])
                cp("act", KcS[:, n0:n0 + nn], psn[bank][:, 0:nn], [("ps", bank)], ["KcS"])
            for ch in range(8):
                nn = 128 if ch < 7 else 127
                mm(psn[5][0:nn, (ch % 4) * 128:(ch % 4 + 1) * 128], hidS[:, ch * 128:ch * 128 + nn], W2c[:], ch % 4 == 0, ch % 4 == 3,
                   ["W2c", "hidS"], [("ps", 5)])
                if ch % 4 == 3:
                    c4 = ch - 3
                    for q in range(4):
                        nq = 128 if (c4 + q) < 7 else 127
                        cp("act", VcS[0:nq, c4 + q, 0:64], psn[5][0:nq, q * 128 + 64:(q + 1) * 128], [("ps", 5)], ["VcS"])
            yield
            Qb = QsT[:, :, b]
            for ch in range(8):
                nn = 128 if ch < 7 else 127
                mm(psn[6][0:nn, ch * 4:(ch + 1) * 4], KcS[0:64, ch * 128:ch * 128 + nn], Qb, ch == 0, ch == 7, ["KcS", "QsT"], [("ps", 6)])
            memset("dve", PcS[:], 0.0, ["PcS"])
            act(PcS[:, 0:7, :], psn[6][:, 0:28].rearrange("p (c g) -> p c g", g=4), AF.Exp, [("ps", 6), "PcS"], ["PcS"], scale=SCALE)
            act(PcS[0:127, 7, :], psn[6][0:127, 28:32], AF.Exp, [("ps", 6), "PcS"], ["PcS"], scale=SCALE)
            for ch in range(8):
                mm(psn[7][0:4, 0:65], PcS[:, ch, :], VcS[:, ch, :], ch == 0, ch == 7, ["PcS", "VcS"], [("ps", 7)])
            cp("act", o3[:, 0, :], psn[7][0:4, 0:65], [("ps", 7)], ["o3"])
            for ch in range(8):
                mm(psn[7][0:4, 128:128 + 257], PcS[:, ch, :], mems[:, ch, :], ch == 0, ch == 7, ["PcS", "mems"], [("ps", 7)])
            ts("dve", o3[:, 0, 64:65], o3[:, 0, 64:65], 1e-30, None, ALU.max, None, ["o3"], ["o3"])
            S.op("dve", lambda e: e.reciprocal(sm[4][0:4, 0:1], o3[:, 0, 64:65]), ["o3"], [("sm", 4)])
            ts("dve", impn[:], psn[7][0:4, 128:128 + 257], sm[4][0:4, 0:1], None, ALU.mult, None, [("ps", 7), ("sm", 4)], ["impn"])
            mm(psn[5][0:1, 0:257], ones4[:], impn[:], True, True, ["impn", "ones4"], [("ps", 5)])
            cp("act", scr_s[:], psn[5][0:1, 0:257], [("ps", 5)], ["scr_s"])
            memset("dve", scr_s[:, 0:1], 3e9, ["scr_s"])
            memset("dve", scr_s[:, 256:257], 2e9, ["scr_s"])
            memset("dve", scr_s[:, 255:256], 1e9, ["scr_s"])
            S.op("dve", lambda e: e.max(m8s[:, 0:8], scr_s[:]), ["scr_s"], ["m8s"])
            S.op("dve", lambda e: e.max_index(i8s[:, 0:8], m8s[:, 0:8], scr_s[:]), ["scr_s", "m8s"], ["i8s"])
            S.op("dve", lambda e: e.match_replace(scr_s2[:], m8s[:, 0:8], scr_s[:], -3e9), ["scr_s", "m8s"], ["scr_s2"])
            S.op("dve", lambda e: e.max(m8s[:, 8:16], scr_s2[:]), ["scr_s2"], ["m8s"])
            S.op("dve", lambda e: e.max_index(i8s[:, 8:16], m8s[:, 8:16], scr_s2[:]), ["scr_s2", "m8s"], ["i8s"])
            yield
            cp("dve", jf[:], i8s[:], ["i8s"], ["jf"])
            ts("dve", jf[:], jf[:], 255.0, None, ALU.min, None, ["jf"], ["jf"])
            cp("dve", i8s[:].bitcast(I32), jf[:], ["jf"], ["i8s"])
            S.op("dve", lambda e: e.tensor_single_scalar(i8s[:].bitcast(I32), i8s[:].bitcast(I32), 1, ALU.arith_shift_right), ["i8s"], ["i8s"])
            cp("dve", jh[:], i8s[:].bitcast(I32), ["i8s"], ["jh"])
            stt("dve", gbf[:], jh[:], -2.0, jf[:], ALU.mult, ALU.add, ["jh", "jf"], ["gbf"])
            dma("sp", pt1[:, 0:128], ptab_in[b:b + 1, :], [], ["pt1"], "s_pt1")
            cp("dve", pt1f[:, 0:128], pt1[:, 0:128], ["pt1"], ["pt1f"])
            for q in range(16):
                ts("dve", onehot[:], iota_row[:], jh[:, q:q + 1], None, ALU.is_equal, None, ["jh", "iota_row", "pgf"], ["oh"])
                tt("dve", onehot[:], onehot[:], pt1f[:, 0:128], ALU.mult, ["oh", "pt1f"], ["oh"])
                S.op("dve", lambda e, q=q: e.reduce_sum(pgf[:, q:q + 1], onehot[:], mybir.AxisListType.X), ["oh"], ["pgf"])
            stt("dve", gbf[:], pgf[:], 2.0, gbf[:], ALU.mult, ALU.add, ["pgf", "gbf"], ["gbf"])
            cp("dve", gbi[:], gbf[:], ["gbf"], ["gbi"])
            dma("sp", s_gbi[b:b + 1, :], gbi[:], ["gbi"], [("s_gbi", b)], "s_gbist")
            dma("sp", graw[0:64, :], s_gbi[b, 0:16:2].partition_broadcast(64), [("s_gbi", b)], ["graw"], "s_graw", allow_slow_non_contiguous=True)
            dma("sp", graw[64:128, :], s_gbi[b, 1:16:2].partition_broadcast(64), [("s_gbi", b), "graw"], ["graw"], "s_graw", allow_slow_non_contiguous=True)
            cp("dve", grawf[:], graw[:], ["graw"], ["grawf"])
            ts("dve", grawf[:], grawf[:], 64.0, mcol[:, 2:3], ALU.mult, ALU.add, ["grawf", "mcol"], ["grawf"])
            cp("dve", gidx[:], grawf[:], ["grawf"], ["gidx"])
            for q in range(8):
                S.op("pool", lambda e, q=q: e.indirect_dma_start(
                    out=selt[:, q, :], out_offset=None, in_=pool_sel,
                    in_offset=bass.IndirectOffsetOnAxis(ap=gidx[:, q:q + 1], axis=0)),
                    ["gidx"], [("selt", q)], dma="s_selt%d" % q)
            dma("pool", selt[64:65, 0, :], kvs_sb[b:b + 1, 128:256], ["kvs", ("selt", 0)], [("selt", 0)], "s_selt0")
            cp("dve", selb[:, :, 0:128], selt[:], [("selt", q) for q in range(8)] + ["selb"], ["selb"])
            for q in range(8):
                tr(ps_tr[0:64, q * 128:(q + 1) * 128], selb[:, q, 0:64], ["selb"], ["ps_tr"])
            cp("act", KsT[:], ps_tr[0:64, :].rearrange("p (q k) -> p q k", q=8), ["ps_tr"], ["KsT"])
            for q in range(8):
                mm(psn[6][:, q * 4:(q + 1) * 4], KsT[:, q, :], Qb, q == 0, q == 7, ["KsT", "QsT"], [("ps", 6)])
            act(PsS[:], psn[6][:, 0:32].rearrange("p (c g) -> p c g", g=4), AF.Exp, [("ps", 6)], ["PsS"], scale=SCALE)
            ts("dve", PsS[:, 0, :], PsS[:, 0, :], mcol[:, 0:1], None, ALU.mult, None, ["PsS", "mcol"], ["PsS"])
            for q in range(8):
                mm(psn[7][0:4, 0:65], PsS[:, q, :], selb[:, q, 64:129], q == 0, q == 7, ["PsS", "selb"], [("ps", 7)])
            cp("act", o3[:, 1, :], psn[7][0:4, 0:65], [("ps", 7)], ["o3"])
            yield
            dma("sp", wint[:, 0:4, :], win_s[b].rearrange("(t p) c -> p t c", p=128), ["wint"], ["wint"], "s_wint")
            dma("sp", wint[0:1, 4, :], kvs_sb[b:b + 1, 256:384], ["kvs", "wint"], ["wint"], "s_wint")
            cp("dve", winb[:, :, 0:128], wint[:], ["wint", "winb"], ["winb"])
            for q in range(5):
                tr(ps_tr[0:64, q * 128:(q + 1) * 128], winb[:, q, 0:64], ["winb"], ["ps_tr"])
            cp("act", KwT[:], ps_tr[0:64, 0:640].rearrange("p (q k) -> p q k", q=5), ["ps_tr"], ["KwT"])
            for q in range(5):
                mm(psn[6][:, q * 4:(q + 1) * 4], KwT[:, q, :], Qb, q == 0, q == 4, ["KwT", "QsT"], [("ps", 6)])
            act(PwS[:], psn[6][:, 0:20].rearrange("p (c g) -> p c g", g=4), AF.Exp, [("ps", 6)], ["PwS"], scale=SCALE)
            ts("dve", PwS[:, 4, :], PwS[:, 4, :], mcol[:, 1:2], None, ALU.mult, None, ["PwS", "mcol"], ["PwS"])
            for q in range(5):
                mm(psn[7][0:4, 0:65], PwS[:, q, :], winb[:, q, 64:129], q == 0, q == 4, ["PwS", "winb"], [("ps", 7)])
            cp("act", o3[:, 2, :], psn[7][0:4, 0:65], [("ps", 7)], ["o3"])
            dma("pool", s_o3[b], o3[:], ["o3"], [("s_o3", b)], "s_o3st")

        NBR = NB if "nosattn" not in DBG_S else 0
        if NBR:
            for m in range(8):
                gather_chunk(0, m)
        for b in range(NBR):
            pending = list(range(8)) if b + 1 < NBR else []
            for _ in attn_body(b):
                for _k in range(2):
                    if pending:
                        gather_chunk(b + 1, pending.pop(0))
            while pending:
                gather_chunk(b + 1, pending.pop(0))
        if "nosattn" in DBG_S:
            memset("dve", Oraw[:], 1.0, ["Oraw"])
            for b_ in range(NB):
                dma("pool", s_o3[b_], Oraw[0:4, 0, :, :], ["Oraw"], [("s_o3", b_)], "s_o3st")
        dma("sp", Oraw[:].rearrange("p g r e -> p (g r e)"), s_o3.rearrange("b g r e -> b (g r e)"), [("s_o3", b) for b in range(NB)], ["Oraw"], "s_orawld")
        ts("dve", rzs_[:], Oraw[:, :, :, 64], 1e-30, None, ALU.max, None, ["Oraw"], ["rzs"])
        S.op("dve", lambda e: e.reciprocal(rzs_[:], rzs_[:]), ["rzs"], ["rzs"])
        tt("dve", coefs[:], rzs_[:], gz_sb[:, 0:12].rearrange("p (g r) -> p g r", r=3), ALU.mult, ["rzs", "gz"], ["coefs"])
        for br in range(3):
            cbb = coefs[:, :, br:br + 1].to_broadcast([NB, 4, 64])
            dst = (Os if br == 0 else Otm)[:].rearrange("p (g d) -> p g d", d=64)
            tt("dve", dst, Oraw[:, :, br, 0:64], cbb, ALU.mult, ["Oraw", "coefs"], ["Os" if br == 0 else "Otm"])
            if br:
                tt("dve", Os[:], Os[:], Otm[:], ALU.add, ["Os", "Otm"], ["Os"])
        tt("dve", ozs[:], Os[:], gz_sb[:, 12:268], ALU.mult, ["Os", "gz"], ["ozs"])
        for kc in range(2):
            tr(ps_tr[:, kc * NB:(kc + 1) * NB], ozs[:, kc * 128:(kc + 1) * 128], ["ozs"], ["ps_tr"])
        cp("act", ozsT[:], ps_tr[:, 0:2 * NB].rearrange("p (k t) -> p k t", k=2), ["ps_tr"], ["ozsT"])
        for half in range(2):
            for kc in range(2):
                mm(psn[half][0:NB, :], ozsT[:, kc, :], wos_sb[:, kc, half * 512:(half + 1) * 512], kc == 0, kc == 1, ["ozsT", "wos"], [("ps", half)])
            cp("act", part_sb[:, half * 512:(half + 1) * 512], psn[half][0:NB, :], [("ps", half)], ["part"])
        dma("pool", cc_in, part_sb[:], ["part"], ["cc_in"], "s_ccin")
        if "ccdbg" in DBG_S:
            dma("pool", o_dpre, part_sb[:], ["part"], ["o_dpre"], "s_dpre", is_out=True)
        if use_cc:
            S.op("pool", lambda e: e.collective_compute("AllReduce", ALU.add, replica_groups=[[0, 1, 2, 3], [4, 5, 6, 7]],
                                                        ins=[cc_in.opt()], outs=[cc_out.opt()]), ["cc_in"], ["cc_out"], dma="s_cc_ar", inc=1)
            dma("sp", part_sb[:], cc_out, ["cc_out", "part"], ["part"], "s_ccout")
            if "ccdbg" in DBG_S:
                dma("pool", o_dpost, part_sb[:], ["part"], ["o_dpost"], "s_dpost", is_out=True)
        else:
            dma("pool", o_part, part_sb[:], ["part"], ["o_part"], "s_opart", is_out=True)
        tt("dve", hs_sb[:], hs_sb[:], part_sb[:], ALU.add, ["hs", "part"], ["hs"])
        ssq = stat[0:NB, 24:25]
        rstd = stat[0:NB, 25:26]
        act(ys_sb[:], hs_sb[:], AF.Square, ["hs"], ["ys", "ssqs"], accum_out=ssq)
        act(ssq, ssq, AF.Sqrt, ["ssqs", "eps"], ["ssqs"], scale=1.0 / D, bias=eps_sb[0:NB, 0:1])
        S.op("dve", lambda e, rstd=rstd, ssq=ssq: e.reciprocal(rstd, ssq), ["ssqs"], ["rstds"])
        stt("dve", ys_sb[:], hs_sb[:], rstd, nf_s[:], ALU.mult, ALU.mult, ["hs", "rstds", "nf_s", "ys"], ["ys"])
        dma("pool", o_y_s, ys_sb[:], ["ys"], ["o_y_s"], "s_oys", is_out=True)
        S.barrier()

    KT = [None, sbp("KT_sel", [128, 2, SEQ], BF16), sbp("KT_win", [128, 2, SEQ], BF16)]
    VV = [None, sbp("V_sel", [128, 32, 4, 72], BF16), sbp("V_win", [128, 32, 4, 72], BF16)]
    KcT = sbp("KcT", [128, 2, 256], BF16)
    Vc = sbp("Vc", [128, 2, 4, 65], BF16)
    memset("pool", VV[1][:], 1.0, ["V1init"])
    memset("pool", VV[2][:], 1.0, ["V2init"])
    memset("pool", Vc[:], 1.0, ["Vcinit"])

    with ExitStack() as es:
        def sb(name, shape, dt=F32):
            return es.enter_context(nc.sbuf_tensor(name, list(shape), dt))
        X = sb("X", [128, 4, D])
        xnb = [sb("xnb%d" % i, [128, D], BF16) for i in range(2)]
        aT = sb("aT", [128, 8, ST], BF16)
        NRING = 2
        wring = [sb("wring%d" % i, [128, 8, 512], BF16) for i in range(NRING)]
        wout_sb = sb("wout_sb", [128, NCC, D], BF16)
        G = sb("G", [128, NCC, ST], BF16)
        U = [sb("U%d" % i, [128, ST + 2]) for i in range(2)]
        UC = sb("UC", [128, NCC, 2])
        Vb = [sb("Vb%d" % i, [128, ST]) for i in range(2)]
        c_sb1 = sb("c_sb", [128, ST])
        c_sb = [c_sb1, c_sb1]
        sz_sb = [sb("sz_sb%d" % i, [128, ST]) for i in range(2)]
        t1_sb1 = sb("t1_sb", [128, ST])
        t1_sb = [t1_sb1, t1_sb1]
        kvst = [sb("kvst%d" % i, [128, 512]) for i in range(2)]
        cw_sb = sb("cw_sb", [128, NCC, 3])
        W1blk = sb("W1blk", [128, 2, 32, 128], BF16)
        W2f = sb("W2f", [128, 2, 64])
        W2blk = sb("W2blk", [128, 2, 128], BF16)
        peTf = sb("peTf", [128, 2, 32])
        peT = sb("peT", [128, 2, 32], BF16)
        pebias = sb("pebias", [128, 2])
        CK = sb("CK", [128, 4, 16 + ST], BF16)
        hidT = sb("hidT", [128, 2, 32], BF16)
        vcst = sb("vcst", [32, 4, 64], BF16)

        for j in range(3):
            dma("sp", cw_sb[:, :, j], conv_w[j, :].rearrange("(c p) -> p c", p=128), [], ["cw"], "c_cw", allow_slow_non_contiguous=True)
        memset("pool", UC[:], 0.0, [("UC", cc) for cc in range(NCC)])
        memset("pool", CK[:], 0.0, ["CK"])
        memset("pool", W1blk[:], 0.0, ["W1blk"])
        memset("pool", W2blk[:], 0.0, ["W2blk"])
        dma("sp", wout_sb[:], s_wout, [], ["wout"], "wres0")
        for c in range(2):
            for hf in range(2):
                dma("sp", W1blk[hf * 64:(hf + 1) * 64, c, :, hf * 64:(hf + 1) * 64],
                    s_w1[c][hf * 64:(hf + 1) * 64, :].rearrange("p (j e) -> p j e", e=64), [], ["W1blk"], "c_w1")
        for hf in range(2):
            dma("sp", W2f[hf * 64:(hf + 1) * 64, :, :], cmp_w2.rearrange("c e d -> e c d"), [], ["W2f"], "c_w2")
            for c in range(2):
                dma("sp", peTf[hf * 64:(hf + 1) * 64, c, :], cmp_pe[c].rearrange("j d -> d j"), [], ["peTf"], "c_pe", allow_slow_non_contiguous=True)
        for c in range(2):
            for hf in range(2):
                cp("dve", W2blk[hf * 64:(hf + 1) * 64, c, hf * 64:(hf + 1) * 64], W2f[hf * 64:(hf + 1) * 64, c, :], ["W2f", "W2blk"], ["W2blk"])
        cp("dve", peT[:], peTf[:], ["peTf"], ["peT"])
        import os
        DBG = os.environ.get("KDBG", "")
        for c in range(2):
            for j in range(32):
                mm(psn[7][:, c:c + 1], W1blk[:, c, j, :], peT[:, c, j:j + 1], j == 0, j == 31, ["W1blk", "peT"], [("ps", 7)])
        cp("act", pebias[:], psn[7][:, 0:2], [("ps", 7)], ["pebias"])

        def norm_transpose(src_ap, src_tok, i, dst_tok, slot):
            ssq = stat[:, slot:slot + 1]
            rstd = stat[:, 8 + slot:9 + slot]
            xb = xnb[slot % 2]
            xbt = ("xnb", slot % 2)
            act(xb[:], src_ap, AF.Square, [src_tok], [xbt, ("ssq", slot)], accum_out=ssq)
            act(ssq, ssq, AF.Sqrt, [("ssq", slot), "eps"], [("ssq", slot)], scale=1.0 / D, bias=eps_sb[:, 0:1])
            S.op("dve", lambda e, rstd=rstd, ssq=ssq: e.reciprocal(rstd, ssq), [("ssq", slot)], [("rstd", slot)])
            xb = xnb[slot % 2]
            xbt = ("xnb", slot % 2)
            act(xb[:], src_ap, AF.Copy, [src_tok, ("rstd", slot)], [xbt], scale=rstd)
            for kc in range(8):
                tr(ps_tr[:, kc * 128:(kc + 1) * 128], xb[:, kc * 128:(kc + 1) * 128], [xbt], ["ps_tr"])
            cp("dve", aT[:, :, i * 128:(i + 1) * 128], ps_tr[:].rearrange("p (k t) -> p k t", k=8), ["ps_tr"], [dst_tok])

        AT = [("aT", i) for i in range(4)]
        ring_n = [0]
        hs_n = [0]
        gb_n = [0]
        kv_n = [0]

        def gbank():
            b = (5, 6, 7)[gb_n[0] % 3]
            gb_n[0] += 1
            return b

        def ring_load(src):
            slot = ring_n[0] % NRING
            ring_n[0] += 1
            dma("sp", wring[slot][:], src, [], [("wr", slot)], "wr%d" % slot)
            return slot

        for t in range(nst):
            r0 = t * ST
            XT = [("X", i) for i in range(4)]
            dma("sp", X[:], x_p[r0:r0 + ST, :].rearrange("(i p) d -> p i d", p=128), [], XT, "xld")
            for i in range(4):
                norm_transpose(X[:, i, :], ("X", i), i, ("aT", i), i)
            for cc in range(NCC):
                slot = ring_load(s_win[cc])
                wt = ("wr", slot)
                k = cc % 2
                b0 = 2 * (hs_n[0] % 2)
                hs_n[0] += 1
                for part, bank in ((1, b0), (2, b0 + 1)):
                    for kc in range(8):
                        mm(psn[bank][:], wring[slot][:, kc, part * 128:(part + 1) * 128], aT[:, kc, :], kc == 0, kc == 7,
                           [wt] + AT, [("ps", bank)])
                cp("act", c_sb[k][:], psn[b0][:], [("ps", b0)], ["c_sb"])
                cp("pool", U[k][:, 0:2], UC[:, cc, :], [("UC", cc)], [("U", k)])
                tt("dve", U[k][:, 2:ST + 2], c_sb[k][:], psn[b0 + 1][:], ALU.mult, ["c_sb", ("ps", b0 + 1), ("U", k)], [("U", k)])
                cp("pool", UC[:, cc, :], U[k][:, ST:ST + 2], [("U", k)], [("UC", cc)])
                ts("dve", Vb[k][:], U[k][:, 2:ST + 2], cw_sb[:, cc, 2:3], None, ALU.mult, None, [("U", k), "cw"], [("Vb", k)])
                stt("dve", Vb[k][:], U[k][:, 1:ST + 1], cw_sb[:, cc, 1:2], Vb[k][:], ALU.mult, ALU.add, [("U", k), "cw", ("Vb", k)], [("Vb", k)])
                stt("dve", Vb[k][:], U[k][:, 0:ST], cw_sb[:, cc, 0:1], Vb[k][:], ALU.mult, ALU.add, [("U", k), "cw", ("Vb", k)], [("Vb", k)])
                b1 = 2 * (hs_n[0] % 2)
                hs_n[0] += 1
                for part, bank in ((0, b1), (3, b1 + 1)):
                    for kc in range(8):
                        mm(psn[bank][:], wring[slot][:, kc, part * 128:(part + 1) * 128], aT[:, kc, :], kc == 0, kc == 7,
                           [wt] + AT, [("ps", bank)])
                act(sz_sb[k][:], psn[b1 + 1][:], AF.Silu, [("ps", b1 + 1)], [("sz", k)])
                tt("dve", t1_sb[k][:], psn[b1][:], sz_sb[k][:], ALU.mult, [("ps", b1), ("sz", k)], ["t1"])
                tt("pool", G[:, cc, :], t1_sb[k][:], Vb[k][:], ALU.mult, ["t1", ("Vb", k)], [("G", cc)])
            for i in range(4):
                for half in range(2):
                    bank = gbank()
                    for cc in range(NCC):
                        mm(psn[bank][:], G[:, cc, i * 128:(i + 1) * 128], wout_sb[:, cc, half * 512:(half + 1) * 512], cc == 0, cc == NCC - 1,
                           [("G", cc), "wout"], [("ps", bank)])
                    xs_ = X[:, i, half * 512:(half + 1) * 512]
                    tt("dve", xs_, psn[bank][:], xs_, ALU.add, [("ps", bank), ("X", i)], [("X", i)])
                if "nosh" not in DBG:
                    dma("pool", s_h[r0 + i * 128:r0 + (i + 1) * 128, :], X[:, i, :], [("X", i)], [("s_h", t, i)], "hst%d" % i)
                norm_transpose(X[:, i, :], ("X", i), i, ("aT", i), 4 + i)
            if "nosh" not in DBG:
                dma("pool", s_hnT[t], aT[:].rearrange("p k t -> p (k t)"), AT, [("s_hnT", t)], "hnst")
            for br in range(3):
                if "nokv" in DBG:
                    break
                slot = ring_load(s_wkv[br])
                wt = ("wr", slot)
                wr = wring[slot]
                for i in range(4):
                    tile_idx = t * 4 + i
                    bank = gbank()
                    for kc in range(8):
                        mm(psn[bank][:], aT[:, kc, i * 128:(i + 1) * 128], wr[:, kc, :], kc == 0, kc == 7, [("aT", i), wt], [("ps", bank)])
                    need_out = (br < 2) or (t == NST - 1)
                    if need_out:
                        k = kv_n[0] % 2
                        kv_n[0] += 1
                        cp("act", kvst[k][:], psn[bank][:], [("ps", bank)], [("kvst", k)])
                        if br == 0:
                            dst = o_cmp_p[r0 + i * 128:r0 + (i + 1) * 128, :]
                        elif br == 1:
                            dst = o_sel_p[r0 + i * 128:r0 + (i + 1) * 128, :]
                        else:
                            dst = o_win_p[i * 128:(i + 1) * 128, :]
                        dma("pool", dst, kvst[k][:], [("kvst", k)], [("okv", t, i, br)], "kvst%d" % k, is_out=True)
                    if br >= 1 and "novv" not in DBG:
                        cp("act", VV[br][:, tile_idx, :, 0:64], psn[bank][:, 256:512].rearrange("p (h d) -> p h d", d=64),
                           [("ps", bank), "V%dinit" % br], [("V", br, tile_idx)])
                if "nock" in DBG:
                    continue
                if br == 0 and "nock0" in DBG:
                    continue
                if br >= 1 and "nokt" in DBG:
                    continue
                if br == 0:
                    for ch in range(4):
                        cp("pool", CK[:, ch, 0:16], CK[:, ch, ST:ST + 16], ["CK"], ["CK"])
                    for ch in range(4):
                        bank = gbank()
                        for kc in range(8):
                            mm(psn[bank][:], wr[:, kc, ch * 128:(ch + 1) * 128], aT[:, kc, :], kc == 0, kc == 7, AT + [wt], [("ps", bank)])
                        cp("act" if ch % 2 else "dve", CK[:, ch, 16:16 + ST], psn[bank][:], [("ps", bank), "CK"], ["CK"])
                else:
                    for hp in range(2):
                        bank = gbank()
                        for kc in range(8):
                            mm(psn[bank][:], wr[:, kc, hp * 128:(hp + 1) * 128], aT[:, kc, :], kc == 0, kc == 7, AT + [wt], [("ps", bank)])
                        cp("act" if hp else "dve", KT[br][:, hp, r0:r0 + ST], psn[bank][:], [("ps", bank)], [("KT", br, t)])
            m0 = 1 if t == 0 else 0
            nm = 32 - m0
            n_first = 0 if t == 0 else 32 * t - 1
            for c in range(2):
                if "nocmp" in DBG:
                    break
                bank = gbank()
                for hp in range(2):
                    ch = c * 2 + hp
                    for j in range(32):
                        mm(psn[bank][:, hp * 32:(hp + 1) * 32], W1blk[:, c, j, :], CK[:, ch, j:j + 16 * 31 + 1:16], j == 0, j == 31,
                           ["W1blk", "CK"], [("ps", bank)])
                act(hidT[:], psn[bank][:, 0:64].rearrange("p (h m) -> p h m", m=32), AF.Silu, [("ps", bank), "pebias"], ["hidT"],
                    bias=pebias[:, c:c + 1])
                bank2 = gbank()
                if c == 0:
                    for hp in range(2):
                        mm(psn[bank2][:, hp * 32:(hp + 1) * 32], W2blk[:, 0, :], hidT[:, hp, :], True, True, ["W2blk", "hidT"], [("ps", bank2)])
                    for hp in range(2):
                        cp("act", KcT[:, hp, n_first:n_first + nm], psn[bank2][:, hp * 32 + m0:(hp + 1) * 32], [("ps", bank2)], ["KcT"])
                else:
                    for hp in range(2):
                        mm(psn[bank2][0:32, hp * 128:(hp + 1) * 128], hidT[:, hp, :], W2blk[:, 1, :], True, True, ["W2blk", "hidT"], [("ps", bank2)])
                    cp("act", vcst[:], psn[bank2][0:32, 0:256].rearrange("p (h d) -> p h d", d=64), [("ps", bank2)], ["vcst"])
                    done = 0
                    while done < nm:
                        if "novc" in DBG:
                            break
                        n = n_first + done
                        chn, pr = n // 128, n % 128
                        cnt = min(nm - done, 128 - pr)
                        dma("pool", Vc[pr:pr + cnt, chn, :, 0:64], vcst[m0 + done:m0 + done + cnt, :, :], ["vcst", "Vcinit"], ["Vc"], "vcsc")
                        done += cnt
        for r in range(2):
            dma("pool", o_conv_p[r, :].rearrange("(c p) -> p c", p=128), UC[:, :, r], [("UC", cc) for cc in range(NCC)], [("o_conv", r)],
                "oconv%d" % r, is_out=True, allow_slow_non_contiguous=True)
        S.barrier()

    if with_b:
      with ExitStack() as es:
        def sb(name, shape, dt=F32):
            return es.enter_context(nc.sbuf_tensor(name, list(shape), dt))
        wq_sb = sb("wq_sb", [128, 8, 1024], BF16)
        wgz_sb = sb("wgz_sb", [128, 8, 1072], BF16)
        wo_sb = sb("wo_sb", [128, 8, 1024], BF16)
        Eb = sb("Eb", [128, 4096], BF16)
        cf = sb("cf", [128, 384])
        nf_sb = sb("nf_sb", [128, D])
        aT2 = sb("aT2", [128, 8, ST], BF16)
        Hq = sb("Hq", [128, D])
        QT = sb("QT", [128, 8, ST], BF16)
        SZ = sb("SZ", [128, 4, D], BF16)
        Gt = sb("Gt", [128, 4, 48])
        PT = [sb("PT%d" % i, [128, 512], BF16) for i in range(3)]
        Pc = [sb("Pc%d" % i, [128, 512], BF16) for i in range(2)]
        O = sb("O", [128, D])
        Otmp = sb("Otmp", [128, 256])
        ozb = sb("ozb", [128, D], BF16)
        ozT = sb("ozT", [128, 8, 128], BF16)
        Yst = sb("Yst", [128, D])
        imp = sb("imp", [128, 4, 64])
        scr = sb("scr", [128, 4, 64])
        scr2 = sb("scr2", [128, 4, 64])
        mb = sb("mb", [128, 4, 64])
        mbb = sb("mbb", [128, 256], BF16)
        mT = sb("mT", [128, 2, 128], BF16)
        m8 = sb("m8", [128, 4, 16])
        rz = sb("rz", [128, 16])
        coef = sb("coef", [128, 3, 4])

        dma("sp", wq_sb[:], s_wq, [], ["wq"], "b_wq")
        dma("sp", wgz_sb[:], s_wgz, [], ["wgz"], "b_wgz")
        dma("sp", wo_sb[:], s_wo, [], ["wo"], "b_wo")
        dma("sp", Eb[:], s_eb, [], ["Eb"], "b_eb")
        dma("sp", cf[:], consts[:, C_MUL:C_MUL + 384], [], ["cf"], "b_cf")
        dma("sp", nf_sb[:], norm_f.partition_broadcast(128), [], ["nf"], "b_nf")

        sb_n = [0]

        def sbank():
            b = (0, 1, 2)[sb_n[0] % 3]
            sb_n[0] += 1
            return b
        pt_n = [0]

        def bc4(ap2d, p):
            return ap2d.rearrange("p (o q) -> p o q", o=1).to_broadcast([p, 4, 128])
        TRIB = bc4(cb[:, C_TRIB:C_TRIB + 128], 128)
        TRIU = bc4(cb[:, C_TRIU:C_TRIU + 128], 128)
        STAIR = bc4(cb[0:8, C_STAIR:C_STAIR + 128], 8)
        QTT = [("QT", s) for s in range(8)]
        PSO = {0: 5, 1: 6, 2: 7}

        for t in range(nst):
            r0 = t * ST
            dma("sp", aT2[:].rearrange("p k t -> p (k t)"), s_hnT[t], [], ["aT2"], "b_aT2")
            for s in range(8):
                bank = sbank()
                for kc in range(8):
                    mm(psn[bank][:], wq_sb[:, kc, s * 128:(s + 1) * 128], aT2[:, kc, :], kc == 0, kc == 7, ["wq", "aT2"], [("ps", bank)])
                cp("act" if s % 2 else "dve", QT[:, s, :], psn[bank][:], [("ps", bank)], [("QT", s)])
            for i in range(4):
                for piece, (c0, cw_) in enumerate(((0, 48), (48, 512), (560, 512))):
                    bank = sbank()
                    for kc in range(8):
                        mm(psn[bank][:, 0:cw_], aT2[:, kc, i * 128:(i + 1) * 128], wgz_sb[:, kc, c0:c0 + cw_], kc == 0, kc == 7,
                           ["wgz", "aT2"], [("ps", bank)])
                    if piece == 0:
                        act(Gt[:, i, :], psn[bank][:, 0:48], AF.Sigmoid, [("ps", bank)], [("Gt", i)])
                    else:
                        act(SZ[:, i, (piece - 1) * 512:piece * 512], psn[bank][:, 0:512], AF.Silu, [("ps", bank)], [("SZ", i)])

            for i in range(4):
                qt = 4 * t + i
                q0 = 128 * qt
                dma("sp", Hq[:], s_h[q0:q0 + 128, :], [], ["Hq"], "b_hq")
                ncm = min(8 * qt + 7, 255)
                chunks = [(0, min(128, ncm))]
                if ncm > 128:
                    chunks.append((128, ncm - 128))
                for kvh in range(4):
                    pb = (kvh % 2) * 64
                    hp = kvh // 2
                    Qr = QT[pb:pb + 64, hp * 4:hp * 4 + 4, i * 128:(i + 1) * 128]
                    for ci, (n0, nn) in enumerate(chunks):
                        bank = sbank()
                        off = 129 + n0 - 8 * qt
                        need_stair = (8 * qt + 6 >= n0) and (8 * qt - 1 < n0 + nn) and (0 <= off) and (off + nn <= 264)
                        mm(psn[bank][0:nn, :], KcT[pb:pb + 64, hp, n0:n0 + nn], Qr, True, not need_stair, ["KcT"] + QTT, [("ps", bank)])
                        if need_stair:
                            mm(psn[bank][0:nn, :], cb[0:8, C_ZSEL + off:C_ZSEL + off + nn], STAIR, False, True, ["cb"], [("ps", bank)])
                        act(Pc[ci][0:nn, :], psn[bank][0:nn, :], AF.Exp, [("ps", bank)], [("Pc", ci)], scale=SCALE)
                    for g in range(4):
                        for ci, (n0, nn) in enumerate(chunks):
                            mm(psn[5][:, g * 65:(g + 1) * 65], Pc[ci][0:nn, g * 128:(g + 1) * 128], Vc[0:nn, ci, kvh, :],
                               g == 0 and ci == 0, g == 3 and ci == len(chunks) - 1, [("Pc", ci), "Vc"], [("ps", 5)])
                    for g in range(4):
                        for ci, (n0, nn) in enumerate(chunks):
                            mm(psn[3][:, g * 64:(g + 1) * 64], Pc[ci][0:nn, g * 128:(g + 1) * 128], cb[0:nn, C_MEM + ci * 64:C_MEM + (ci + 1) * 64],
                               g == 0 and ci == 0, g == 3 and ci == len(chunks) - 1, [("Pc", ci), "cb"], [("ps", 3)])
                    zc = psn[5][:, 0:260].rearrange("p (g e) -> p g e", e=65)[:, :, 64]
                    ts("dve", rz[:, 0:4], zc, 1e-30, None, ALU.max, None, [("ps", 5)], [("rz", kvh, 0)])
                    S.op("dve", lambda e: e.reciprocal(rz[:, 0:4], rz[:, 0:4]), [("rz", kvh, 0)], [("rz", kvh, 0)])
                    ts("dve", imp[:, kvh, :], psn[3][:, 0:64], rz[:, 0:1], None, ALU.mult, None, [("ps", 3), ("rz", kvh, 0)], [("imp", kvh)])
                    for g in range(1, 4):
                        stt("dve", imp[:, kvh, :], psn[3][:, g * 64:(g + 1) * 64], rz[:, g:g + 1], imp[:, kvh, :], ALU.mult, ALU.add,
                            [("ps", 3), ("rz", kvh, 0), ("imp", kvh)], [("imp", kvh)])
                    gview = Gt[:, i, kvh * 12:(kvh + 1) * 12].rearrange("p (g b) -> p b g", b=3)
                    tt("dve", coef[:, 0, :], rz[:, 0:4], gview[:, 0, :], ALU.mult, [("rz", kvh, 0), ("Gt", i)], [("coef", 0)])
                    ocv = psn[5][:, 0:260].rearrange("p (g e) -> p g e", e=65)[:, :, 0:64]
                    Ov = O[:, kvh * 256:(kvh + 1) * 256].rearrange("p (g d) -> p g d", d=64)
                    c0b = coef[:, 0, :].rearrange("p (g o) -> p g o", o=1).to_broadcast([128, 4, 64])
                    tt("dve", Ov, ocv, c0b, ALU.mult, [("ps", 5), ("coef", 0)], [("O", kvh)])
                IMP = [("imp", k) for k in range(4)]
                sl0 = 64 - 2 * qt
                mulb = cf[:, sl0:sl0 + 64].rearrange("p (o j) -> p o j", o=1).to_broadcast([128, 4, 64])
                addb = cf[:, 128 + sl0:128 + sl0 + 64].rearrange("p (o j) -> p o j", o=1).to_broadcast([128, 4, 64])
                valb = cf[:, 256 + sl0:256 + sl0 + 64].rearrange("p (o j) -> p o j", o=1).to_broadcast([128, 4, 64])
                tt("dve", scr[:], imp[:], mulb, ALU.mult, IMP + ["cf"], ["scr"])
                tt("dve", scr[:], scr[:], addb, ALU.add, ["scr", "cf"], ["scr"])
                if qt >= 1:
                    memset("dve", scr[:, :, 0:1], 3e9, ["scr"])
                for kvh in range(4):
                    S.op("dve", lambda e, kvh=kvh: e.max(m8[:, kvh, 0:8], scr[:, kvh, :]), ["scr"], [("m8", kvh)])
                    S.op("dve", lambda e, kvh=kvh: e.match_replace(scr2[:, kvh, :], m8[:, kvh, 0:8], scr[:, kvh, :], -3e9),
                         ["scr", ("m8", kvh)], [("scr2", kvh)])
                    S.op("dve", lambda e, kvh=kvh: e.max(m8[:, kvh, 8:16], scr2[:, kvh, :]), [("scr2", kvh)], [("m8", kvh)])
                    ts("dve", mb[:, kvh, :], scr[:, kvh, :], m8[:, kvh, 15:16], None, ALU.is_ge, None, ["scr", ("m8", kvh)], [("mb", kvh)])
                MB = [("mb", k) for k in range(4)]
                tt("dve", mb[:], mb[:], valb, ALU.mult, MB + ["cf"], MB)
                ts("dve", mbb[:], mb[:].rearrange("p k j -> p (k j)"), -1.0, BIG, ALU.add, ALU.mult, MB, ["mbb"])
                for pr in range(2):
                    tr(ps_tr[:, pr * 128:(pr + 1) * 128], mbb[:, pr * 128:(pr + 1) * 128], ["mbb"], ["ps_tr"])
                cp("dve", mT[:], ps_tr[:, 0:256].rearrange("p (a q) -> p a q", a=2), ["ps_tr"], ["mT"])
                for kvh in range(4):
                    pb = (kvh % 2) * 64
                    hp = kvh // 2
                    Qr = QT[pb:pb + 64, hp * 4:hp * 4 + 4, i * 128:(i + 1) * 128]
                    mTb = mT[pb:pb + 64, hp, :].rearrange("p (o q) -> p o q", o=1).to_broadcast([64, 4, 128])
                    for br in (1, 2):
                        kcs = list(range(0, qt + 1)) if br == 1 else list(range(max(0, qt - 4), qt + 1))
                        ob = PSO[br]
                        for kc in kcs:
                            bank = sbank()
                            diag = (kc == qt)
                            low = (br == 2 and kc == qt - 4)
                            last_simple = not (diag or low or br == 1)
                            mm(psn[bank][:], KT[br][pb:pb + 64, hp, kc * 128:(kc + 1) * 128], Qr, True, last_simple,
                               [("KT", br, kc // 4)] + QTT, [("ps", bank)])
                            if br == 1:
                                mm(psn[bank][:], Eb[pb:pb + 64, kc * 128:(kc + 1) * 128], mTb, False, not diag, ["Eb", "mT"], [("ps", bank)])
                            if diag:
                                mm(psn[bank][:], ident, TRIB, False, True, ["cb"], [("ps", bank)])
                            elif low:
                                mm(psn[bank][:], ident, TRIU, False, True, ["cb"], [("ps", bank)])
                            x = pt_n[0] % 3
                            pt_n[0] += 1
                            act(PT[x][:], psn[bank][:], AF.Exp, [("ps", bank)], [("PT", x)], scale=SCALE)
                            for g in range(4):
                                mm(psn[ob][:, g * 65:(g + 1) * 65], PT[x][:, g * 128:(g + 1) * 128], VV[br][:, kc, kvh, 0:65],
                                   g == 0 and kc == kcs[0], g == 3 and kc == kcs[-1], [("PT", x), ("V", br, kc)], [("ps", ob)])
                        zc = psn[ob][:, 0:260].rearrange("p (g e) -> p g e", e=65)[:, :, 64]
                        rzs = rz[:, 4 * br:4 * br + 4]
                        ts("dve", rzs, zc, 1e-30, None, ALU.max, None, [("ps", ob)], [("rz", kvh, br)])
                        S.op("dve", lambda e, rzs=rzs: e.reciprocal(rzs, rzs), [("rz", kvh, br)], [("rz", kvh, br)])
                        gview = Gt[:, i, kvh * 12:(kvh + 1) * 12].rearrange("p (g b) -> p b g", b=3)
                        tt("dve", coef[:, br, :], rzs, gview[:, br, :], ALU.mult, [("rz", kvh, br), ("Gt", i)], [("coef", br)])
                        ocv = psn[ob][:, 0:260].rearrange("p (g e) -> p g e", e=65)[:, :, 0:64]
                        cbb = coef[:, br, :].rearrange("p (g o) -> p g o", o=1).to_broadcast([128, 4, 64])
                        Ov = O[:, kvh * 256:(kvh + 1) * 256].rearrange("p (g d) -> p g d", d=64)
                        tt("dve", Otmp[:].rearrange("p (g d) -> p g d", d=64), ocv, cbb, ALU.mult, [("ps", ob), ("coef", br)], ["Otmp"])
                        tt("pool", O[:, kvh * 256:(kvh + 1) * 256], O[:, kvh * 256:(kvh + 1) * 256], Otmp[:], ALU.add, ["Otmp", ("O", kvh)], [("O", kvh)])
                OT = [("O", k) for k in range(4)]
                tt("pool", ozb[:], O[:], SZ[:, i, :], ALU.mult, OT + [("SZ", i)], ["ozb"])
                for kc in range(8):
                    tr(ps_tr[:, kc * 128:(kc + 1) * 128], ozb[:, kc * 128:(kc + 1) * 128], ["ozb"], ["ps_tr"])
                cp("dve", ozT[:], ps_tr[:].rearrange("p (k t) -> p k t", k=8), ["ps_tr"], ["ozT"])
                for half in range(2):
                    bank = sbank()
                    for kc in range(8):
                        mm(psn[bank][:], ozT[:, kc, :], wo_sb[:, kc, half * 512:(half + 1) * 512], kc == 0, kc == 7, ["ozT", "wo"], [("ps", bank)])
                    hv = Hq[:, half * 512:(half + 1) * 512]
                    tt("dve", hv, psn[bank][:], hv, ALU.add, [("ps", bank), "Hq"], ["Hq"])
                ssq = stat[:, 16:17]
                rstd = stat[:, 17:18]
                act(ozb[:], Hq[:], AF.Square, ["Hq"], ["ozb", "ssqf"], accum_out=ssq)
                act(ssq, ssq, AF.Sqrt, ["ssqf", "eps"], ["ssqf"], scale=1.0 / D, bias=eps_sb[:, 0:1])
                S.op("dve", lambda e, rstd=rstd, ssq=ssq: e.reciprocal(rstd, ssq), ["ssqf"], ["rstdf"])
                stt("dve", Yst[:], Hq[:], rstd, nf_sb[:], ALU.mult, ALU.mult, ["Hq", "rstdf", "nf"], ["Yst"])
                dma("pool", o_y_p[q0:q0 + 128, :], Yst[:], ["Yst"], [("o_y", qt)], "b_yst", is_out=True)
    S.emit()
    return nc


_CACHE = {}


def shared_inputs(inputs):
    f = lambda a: np.ascontiguousarray(np.asarray(a, dtype=np.float32))
    return {
        "norm_a": f(inputs["norm_a"][0]), "w_in": f(inputs["conv_w_in"][0]), "conv_w": f(inputs["conv_w"][0]),
        "w_out": f(inputs["conv_w_out"][0]), "norm_kv": f(inputs["norm_kv"]), "w_kv": f(inputs["w_kv"]),
        "cmp_pe": f(inputs["cmp_pe"]), "cmp_w1": f(inputs["cmp_w1"]), "cmp_w2": f(inputs["cmp_w2"]),
        "norm_b": f(inputs["norm_b"][0]), "w_qg": f(inputs["w_qg"][0]), "w_o": f(inputs["w_o"][0]),
        "norm_f": f(inputs["norm_f"]), "consts": make_consts(), "mems_in": make_mems(),
    }


def core_inputs(inputs, c, shared):
    f = lambda a: np.ascontiguousarray(np.asarray(a, dtype=np.float32))
    kvh, half = c % 4, c // 4
    bs = slice(SBC * half, SBC * half + SBC)
    m = dict(shared)
    m["x_p"] = f(inputs["x_prompt"][c])
    m["x_s"] = f(inputs["x_sample"][bs, 0, :])
    m["cconv_s"] = f(inputs["cache_conv"][0, bs])
    m["pool_cmp"] = f(inputs["cache_cmp_kv"][:, :, :, kvh, :]).reshape(5120 * 8, 2048)
    m["pool_sel"] = f(inputs["cache_sel_kv"][:, :, :, kvh, :]).reshape(5120 * 128, 128)
    m["win_s"] = f(inputs["cache_win_kv"][bs, :, :, kvh, :]).reshape(SBC, 512, 128)
    m["ptab_in"] = np.ascontiguousarray(np.asarray(inputs["page_table"][bs], dtype=np.int32))
    wqg = np.asarray(inputs["w_qg"][0], dtype=np.float32)
    hs = slice(kvh * 256, (kvh + 1) * 256)
    m["wqs_in"] = np.ascontiguousarray(np.concatenate([wqg[:, hs], wqg[:, 1024 + kvh * 12:1024 + (kvh + 1) * 12], wqg[:, 1072 + kvh * 256:1072 + (kvh + 1) * 256]], axis=1))
    wkv = np.asarray(inputs["w_kv"], dtype=np.float32).reshape(D, 3, 2, 4, 64)
    m["wkvs_in"] = np.ascontiguousarray(wkv[:, :, :, kvh, :].reshape(D, 384))
    m["wos_in"] = f(inputs["w_o"][0][hs, :])
    return m


def kernel(**inputs):
    if "nc" not in _CACHE:
        _CACHE["nc"] = build_program()
    nc = _CACHE["nc"]
    shared = shared_inputs(inputs)
    in_maps = [core_inputs(inputs, c, shared) for c in range(NCORES)]
    res = run_bass_kernel_spmd(nc, in_maps, core_ids=list(range(NCORES)))
    R = res.results
    B = NCORES
    y_prompt = np.stack([R[c]["o_y_p"] for c in range(B)])
    conv_p = np.stack([R[c]["o_conv_p"] for c in range(B)])[None]
    cmp_p = np.stack([R[c]["o_cmp_p"] for c in range(B)]).reshape(B, SEQ, 2, 4, 64)
    sel_p = np.stack([R[c]["o_sel_p"] for c in range(B)]).reshape(B, SEQ, 2, 4, 64)
    win_p = np.stack([R[c]["o_win_p"] for c in range(B)]).reshape(B, 512, 2, 4, 64)
    y_sample = np.zeros((DEC_B, 1, D), np.float32)
    conv_s = np.zeros((1, DEC_B, 2, DC), np.float32)
    cmp_s = np.zeros((DEC_B, 1, 2, 4, 64), np.float32)
    sel_s = np.zeros((DEC_B, 1, 2, 4, 64), np.float32)
    win_s = np.zeros((DEC_B, 512, 2, 4, 64), np.float32)
    for c in range(B):
        kvh, half = c % 4, c // 4
        bs = slice(SBC * half, SBC * half + SBC)
        kv = np.asarray(R[c]["o_kv_s"]).reshape(SBC, 3, 2, 64)
        cmp_s[bs, 0, :, kvh, :] = kv[:, 0]
        sel_s[bs, 0, :, kvh, :] = kv[:, 1]
        win_s[bs, :, :, kvh, :] = np.asarray(R[c]["o_win_s"]).reshape(SBC, 512, 2, 64)
        if kvh == 0:
            y_sample[bs, 0, :] = np.asarray(R[c]["o_y_s"])
            conv_s[0, bs] = np.asarray(R[c]["o_conv_s"])
    return (y_prompt, y_sample, conv_p, conv_s, cmp_p, cmp_s, sel_p, sel_s, win_p, win_s)
```

```python
import numpy as np
import concourse.bass as bass
import concourse.mybir as mybir
from concourse.bass_utils import run_bass_kernel_spmd

F32 = mybir.dt.float32
BF16 = mybir.dt.bfloat16
I32 = mybir.dt.int32
AF = mybir.ActivationFunctionType
ALU = mybir.AluOpType

NCORES = 8
D = 1024
SEQ = 4096
ST = 512
NST = SEQ // ST
DC = 2048
NCC = DC // 128
KVW = 1536
QGW = 2096
EPS = 1e-6
DEC_B = 32
SB = DEC_B // NCORES


class _Op:
    __slots__ = ("eng", "fn", "deps", "is_dma", "sem", "cum", "needs_inc", "count", "name", "inc")


class Sched:
    ENGS = ("pe", "act", "dve", "pool", "sp")

    def __init__(self, nc):
        self.nc = nc
        self.ops = {e: [] for e in self.ENGS}
        self.last_w = {}
        self.readers = {}
        self.dma_cum = {}
        self.out_dmas = []

    def op(self, eng, fn, reads=(), writes=(), dma=None, is_out=False, inc=16):
        o = _Op()
        o.eng = eng
        o.fn = fn
        o.is_dma = dma is not None
        o.sem = dma
        o.count = None
        deps = {}
        def add(d):
            if d is o:
                return
            if (not d.is_dma) and d.eng == "pe" and eng == "pe" and dma is None:
                return
            deps[id(d)] = d
        for t in reads:
            w = self.last_w.get(t)
            if w is not None:
                add(w)
        for t in writes:
            w = self.last_w.get(t)
            if w is not None:
                add(w)
            for r in self.readers.get(t, ()):
                add(r)
        o.deps = list(deps.values())
        for t in reads:
            self.readers.setdefault(t, []).append(o)
        for t in writes:
            self.last_w[t] = o
            self.readers[t] = []
        if o.is_dma:
            self.dma_cum[dma] = self.dma_cum.get(dma, 0) + inc
            o.inc = inc
            o.cum = self.dma_cum[dma]
            if is_out:
                self.out_dmas.append(o)
        o.needs_inc = o.is_dma
        for d in o.deps:
            d.needs_inc = True
        self.ops[eng].append(o)
        return o

    def barrier(self):
        lasts = []
        for e in self.ENGS:
            for o in reversed(self.ops[e]):
                if (not o.is_dma) and o.fn is not None:
                    o.needs_inc = True
                    lasts.append(o)
                    break
        seen = set()
        for e in self.ENGS:
            for o in reversed(self.ops[e]):
                if o.is_dma and o.sem not in seen:
                    seen.add(o.sem)
                    lasts.append(o)
        for e in self.ENGS:
            b = _Op()
            b.eng = e
            b.fn = None
            b.is_dma = False
            b.sem = None
            b.count = None
            b.needs_inc = False
            b.deps = list(lasts)
            self.ops[e].append(b)

    def emit(self):
        nc = self.nc
        eng_sem = {e: nc.alloc_semaphore("sem_" + e) for e in ("pe", "act", "dve", "pool")}
        dma_sem = {k: nc.alloc_semaphore("dsem_" + str(k)) for k in self.dma_cum}
        for e in self.ENGS:
            c = 0
            for o in self.ops[e]:
                if o.needs_inc and not o.is_dma:
                    c += 1
                    o.count = c
        final_waits = {}
        for o in self.out_dmas:
            final_waits[o.sem] = max(final_waits.get(o.sem, 0), self.dma_cum[o.sem])

        def run(e, engine):
            known = {}
            for o in self.ops[e]:
                need = {}
                for d in o.deps:
                    if d.is_dma:
                        key = ("d", d.sem)
                        val = d.cum
                    else:
                        key = ("e", d.eng)
                        val = d.count
                    if need.get(key, 0) < val:
                        need[key] = val
                for key, val in need.items():
                    if known.get(key, 0) >= val:
                        continue
                    known[key] = val
                    sem = dma_sem[key[1]] if key[0] == "d" else eng_sem[key[1]]
                    engine.wait_ge(sem, val)
                if o.fn is None:
                    continue
                ins = o.fn(engine)
                if o.needs_inc:
                    if o.is_dma:
                        ins.then_inc(dma_sem[o.sem], o.inc)
                    else:
                        ins.then_inc(eng_sem[o.eng], 1)
            if e == "sp":
                for k, v in final_waits.items():
                    engine.wait_ge(dma_sem[k], v)

        with nc.Block() as block:
            @block.tensor
            def _(eng):
                run("pe", eng)

            @block.scalar
            def _(eng):
                run("act", eng)

            @block.vector
            def _(eng):
                run("dve", eng)

            @block.gpsimd
            def _(eng):
                run("pool", eng)

            @block.sync
            def _(eng):
                run("sp", eng)


BIG = 30000.0
SCALE = 0.125
C_ID, C_TRIB, C_TRIU, C_MUL, C_ADD, C_VAL, C_MEM, C_STAIR, C_ZSEL, C_EB = 0, 128, 256, 384, 512, 640, 768, 896, 1024, 1288
C_MCOL = C_EB + 4096
C_IOTAR = C_MCOL + 4
C_W = C_IOTAR + 128
SBC = 16
NMEM = 8 * 257


def make_consts():
    c = np.zeros((128, C_W), np.float32)
    c[:, C_ID:C_ID + 128] = np.eye(128)
    kk = np.arange(128)[:, None]
    ql = np.arange(128)[None, :]
    c[:, C_TRIB:C_TRIB + 128] = np.where(kk <= ql, 0.0, -BIG)
    c[:, C_TRIU:C_TRIU + 128] = np.where(kk >= ql, 0.0, -BIG)
    qq = np.arange(128)[:, None]
    jrel = np.arange(128)[None, :] - 64
    lo = qq < 64
    mul = np.zeros((128, 128), np.float32)
    add = np.zeros((128, 128), np.float32)
    mul[:] = np.where(jrel <= -2, 1.0, 0.0)
    mul += np.where((jrel == -1) & ~lo, 1.0, 0.0)
    add += np.where(jrel >= 2, -1e9, 0.0)
    add += np.where((jrel == -1) & lo, 1e9, 0.0)
    add += np.where((jrel == 0) & lo, 2e9, 0.0)
    add += np.where((jrel == 0) & ~lo, 1e9, 0.0)
    add += np.where((jrel == 1) & lo, -1e9, 0.0)
    add += np.where((jrel == 1) & ~lo, 2e9, 0.0)
    c[:, C_MUL:C_MUL + 128] = mul
    c[:, C_ADD:C_ADD + 128] = add
    c[:, C_VAL:C_VAL + 128] = (add > -0.5e9).astype(np.float32)
    for ch in range(2):
        n = ch * 128 + np.arange(128)[:, None]
        j = np.arange(64)[None, :]
        m = ((n >= 4 * j - 1) & (n <= 4 * j + 3) & (n < 255)).astype(np.float32)
        c[:, C_MEM + ch * 64:C_MEM + (ch + 1) * 64] = m
    for k in range(8):
        rel = k - 1
        c[k, C_STAIR:C_STAIR + 128] = np.where(np.arange(128) >= 16 * rel + 31, 0.0, -BIG)
        c[k, C_ZSEL + 128 + k] = 1.0
    for j in range(64):
        c[j, C_EB + 64 * j:C_EB + 64 * j + 64] = 1.0
        c[64 + j, C_EB + 64 * j:C_EB + 64 * j + 64] = 1.0
    p = np.arange(128)
    c[:, C_MCOL + 0] = (p <= 64)
    c[:, C_MCOL + 1] = (p == 0)
    c[:, C_MCOL + 2] = p % 64
    c[0, C_IOTAR:C_IOTAR + 128] = np.arange(128)
    return c


def make_mems():
    m = np.zeros((128, 8, 257), np.float32)
    for ch in range(8):
        n = ch * 128 + np.arange(128)[:, None]
        j = np.arange(257)[None, :]
        m[:, ch, :] = ((n >= 4 * j - 1) & (n <= 4 * j + 3) & (n < 1023))
    return m.reshape(128, NMEM)


def build_program(nst=NST, with_b=True, with_s=True, use_cc=True):
    import os
    DBG_S = os.environ.get("KDBG", "")
    from contextlib import ExitStack
    nc = bass.Bass("TRN2", target_bir_lowering=False)
    S = Sched(nc)

    def din(name, shape, dt=F32):
        return nc.dram_tensor(name, list(shape), dt, kind="ExternalInput").ap()

    def dout(name, shape, dt=F32):
        return nc.dram_tensor(name, list(shape), dt, kind="ExternalOutput").ap()

    def dscr(name, shape, dt):
        return nc.dram_tensor(name, list(shape), dt).ap()

    x_p = din("x_p", [SEQ, D])
    norm_a = din("norm_a", [D])
    w_in = din("w_in", [D, 4 * DC])
    conv_w = din("conv_w", [3, DC])
    w_out = din("w_out", [DC, D])
    norm_kv = din("norm_kv", [D])
    w_kv = din("w_kv", [D, KVW])
    cmp_pe = din("cmp_pe", [2, 32, 64])
    cmp_w1 = din("cmp_w1", [2, 2048, 64])
    cmp_w2 = din("cmp_w2", [2, 64, 64])
    norm_b = din("norm_b", [D])
    w_qg = din("w_qg", [D, QGW])
    w_o = din("w_o", [D, D])
    norm_f = din("norm_f", [D])
    consts = din("consts", [128, C_W])
    if with_s:
        NPOOL = 8 if "tinypool" in DBG_S else 5120
        if "ccdbg" in DBG_S:
            o_dpre = dout("o_dpre", [SBC, D])
            o_dpost = dout("o_dpost", [SBC, D])
        x_s = din("x_s", [SBC, D])
        cconv_s = din("cconv_s", [SBC, 2, DC])
        pool_cmp = din("pool_cmp", [NPOOL * 8, 2048])
        pool_sel = din("pool_sel", [NPOOL * 128, 128])
        win_s = din("win_s", [SBC, 512, 128])
        ptab_in = din("ptab_in", [SBC, 128], I32)
        wqs_in = din("wqs_in", [D, 524])
        wkvs_in = din("wkvs_in", [D, 384])
        wos_in = din("wos_in", [256, D])
        mems_in = din("mems_in", [128, NMEM])
        o_y_s = dout("o_y_s", [SBC, D])
        o_conv_s = dout("o_conv_s", [SBC, 2, DC])
        o_kv_s = dout("o_kv_s", [SBC, 384])
        o_win_s = dout("o_win_s", [SBC, 512, 128])
        if not use_cc:
            o_part = dout("o_part", [SBC, D])
        s_wqs = dscr("s_wqs", [128, 8, 524], BF16)
        s_wkvs = dscr("s_wkvs", [128, 8, 384], BF16)
        s_wos = dscr("s_wos", [128, 2, D], BF16)
        s_mems = dscr("s_mems", [128, NMEM], BF16)
        s_o3 = dscr("s_o3", [SBC, 4, 3, 65], F32)
        cc_in = dscr("cc_in", [SBC, D], F32)
        s_gbi = dscr("s_gbi", [SBC, 16], I32)
        cc_out = dscr("cc_out", [SBC, D], F32)

    o_y_p = dout("o_y_p", [SEQ, D])
    o_conv_p = dout("o_conv_p", [2, DC])
    o_cmp_p = dout("o_cmp_p", [SEQ, 512])
    o_sel_p = dout("o_sel_p", [SEQ, 512])
    o_win_p = dout("o_win_p", [512, 512])

    s_h = dscr("s_h", [SEQ, D], F32)
    s_hnT = dscr("s_hnT", [NST, 128, 8 * ST], BF16)
    s_win = dscr("s_win", [NCC, 128, 8, 512], BF16)
    s_wkv = dscr("s_wkv", [3, 128, 8, 512], BF16)
    s_wout = dscr("s_wout", [128, NCC, D], BF16)
    s_wq = dscr("s_wq", [128, 8, 1024], BF16)
    s_wgz = dscr("s_wgz", [128, 8, 1072], BF16)
    s_wo = dscr("s_wo", [128, 8, 1024], BF16)
    s_w1 = dscr("s_w1", [2, 128, 2048], BF16)
    s_eb = dscr("s_eb", [128, 4096], BF16)

    def sbp(name, shape, dt=F32):
        return nc.alloc_sbuf_tensor(name, list(shape), dt)

    cb = sbp("cb", [128, C_EB], BF16)
    eps_sb = sbp("eps_sb", [128, 1])
    stat = sbp("stat", [128, 32])
    ident = cb[:, C_ID:C_ID + 128]

    psn = {}
    for i in (0, 1, 2, 3, 5, 6, 7):
        psn[i] = nc.alloc_psum_tensor("ps%d" % i, [128, 512], F32)
    ps_tr = nc.alloc_psum_tensor("ps_tr", [128, 1024], BF16)

    def dma(eng, out, in_, reads, writes, sem, is_out=False, **kw):
        return S.op(eng, lambda e: e.dma_start(out=out, in_=in_, **kw), reads, writes, dma=sem, is_out=is_out)

    def mm(out, lhsT, rhs, start, stop, reads, writes):
        return S.op("pe", lambda e: e.matmul(out, lhsT, rhs, start=start, stop=stop), reads, writes)

    def tr(out, in_, reads, writes):
        kk = in_.shape[0]
        return S.op("pe", lambda e: e.transpose(out, in_, cb[0:kk, C_ID:C_ID + kk]), list(reads) + ["cb"], writes)

    trf_ident = [None]

    def trf(out, in_, reads, writes):
        kk = in_.shape[0]
        idf = trf_ident[0]
        return S.op("pe", lambda e: e.transpose(out, in_, idf[0:kk, 0:kk]), list(reads) + ["identf"], writes)

    def act(out, in_, func, reads, writes, **kw):
        return S.op("act", lambda e: e.activation(out, in_, func, **kw), reads, writes)

    def tt(eng, out, in0, in1, op, reads, writes):
        return S.op(eng, lambda e: e.tensor_tensor(out, in0, in1, op), reads, writes)

    def ts(eng, out, in0, s1, s2, op0, op1, reads, writes):
        if s2 is None:
            return S.op(eng, lambda e: e.tensor_scalar(out, in0, s1, None, op0), reads, writes)
        return S.op(eng, lambda e: e.tensor_scalar(out, in0, s1, s2, op0, op1), reads, writes)

    def stt(eng, out, in0, sc, in1, op0, op1, reads, writes):
        return S.op(eng, lambda e: e.scalar_tensor_tensor(out, in0, sc, in1, op0, op1), reads, writes)

    def cp(eng, out, in_, reads, writes):
        if eng == "act":
            return act(out, in_, AF.Copy, reads, writes)
        return S.op(eng, lambda e: e.tensor_copy(out, in_), reads, writes)

    def memset(eng, ap, val, writes):
        return S.op(eng, lambda e: e.memset(ap, val), [], writes)

    with ExitStack() as es:
        def sb(name, shape, dt=F32):
            return es.enter_context(nc.sbuf_tensor(name, list(shape), dt))
        cst = sb("cst", [128, C_W])
        NSTG = 4
        stage = [sb("stage%d" % i, [128, 2304]) for i in range(NSTG)]
        stageb = [sb("stageb%d" % i, [128, 2304], BF16) for i in range(NSTG)]
        na_sb = sb("na_sb", [128, 8])
        nkv_sb = sb("nkv_sb", [128, 8])
        nb_sb = sb("nb_sb", [128, 8])

        dma("sp", cst[:], consts, [], ["cst"], "c_cst")
        cp("dve", cb[:], cst[:, 0:C_EB], ["cst"], ["cb"])
        ebst = sb("ebst", [128, 4096], BF16)
        cp("act", ebst[:], cst[:, C_EB:C_EB + 4096], ["cst"], ["ebst"])
        dma("pool", s_eb, ebst[:], ["ebst"], ["s_w"], "c_ebst")
        dma("sp", na_sb[:], norm_a.rearrange("(k p) -> p k", p=128), [], ["na"], "c_na", allow_slow_non_contiguous=True)
        dma("sp", nkv_sb[:], norm_kv.rearrange("(k p) -> p k", p=128), [], ["nkv"], "c_nkv", allow_slow_non_contiguous=True)
        dma("sp", nb_sb[:], norm_b.rearrange("(k p) -> p k", p=128), [], ["nb"], "c_nb", allow_slow_non_contiguous=True)
        memset("pool", eps_sb[:], EPS, ["eps"])

        prep_i = [0]

        def prep(src_ap, width, scale_ap, scale_tok, dst_ap, dst_tok, src_view=None, permute_q=False):
            i = prep_i[0] % NSTG
            n = prep_i[0]
            prep_i[0] += 1
            st_t, sb_t = "stage%d" % i, "stageb%d" % i
            dma("sp", stage[i][:, 0:width], src_ap, [], [st_t], "prep_ld%d" % i)
            rd = [st_t] + ([scale_tok] if scale_tok else [])
            o_, i_ = stageb[i][:, 0:width], stage[i][:, 0:width]
            if permute_q:
                for a in range(2):
                    ov = stageb[i][:, a * 512:(a + 1) * 512].rearrange("p (g h d) -> p g h d", g=4, h=2, d=64)
                    iv = stage[i][:, a * 512:(a + 1) * 512].rearrange("p (h g d) -> p g h d", g=4, h=2, d=64)
                    ts("dve" if a == 0 else "pool", ov, iv, scale_ap, None, ALU.mult, None, rd, [sb_t])
            elif scale_ap is None:
                cp("dve" if n % 2 == 0 else "pool", o_, i_, rd, [sb_t])
            elif n % 4 < 2:
                ts("dve", o_, i_, scale_ap, None, ALU.mult, None, rd, [sb_t])
            else:
                act(o_, i_, AF.Copy, rd, [sb_t], scale=scale_ap)
            src = o_ if src_view is None else src_view(o_)
            dma("pool", dst_ap, src, [sb_t], [dst_tok], "prep_st%d" % i)

        for kc in range(8):
            for part in range(4):
                prep(w_in[kc * 128:(kc + 1) * 128, part * DC:(part + 1) * DC], DC, na_sb[:, kc:kc + 1], "na",
                     s_win[:, :, kc, part * 128:(part + 1) * 128].rearrange("c p j -> p c j"), "s_w",
                     src_view=lambda a: a.rearrange("p (c j) -> p c j", j=128))
        for cc in range(NCC):
            prep(w_out[cc * 128:(cc + 1) * 128, :], D, None, None, s_wout[:, cc, :], "s_w")
        for kc in range(8):
            prep(w_kv[kc * 128:(kc + 1) * 128, :], KVW, nkv_sb[:, kc:kc + 1], "nkv",
                 s_wkv[:, :, kc, :].rearrange("b p j -> p b j"), "s_w",
                 src_view=lambda a: a.rearrange("p (b j) -> p b j", j=512))
        for kc in range(8):
            prep(w_qg[kc * 128:(kc + 1) * 128, 0:1024], 1024, nb_sb[:, kc:kc + 1], "nb", s_wq[:, kc, :], "s_w", permute_q=True)
            prep(w_qg[kc * 128:(kc + 1) * 128, 1024:QGW], 1072, nb_sb[:, kc:kc + 1], "nb", s_wgz[:, kc, :], "s_w")
            prep(w_o[kc * 128:(kc + 1) * 128, :], D, None, None, s_wo[:, kc, :], "s_w")
        if with_s:
            for kc in range(8):
                prep(wqs_in[kc * 128:(kc + 1) * 128, :], 524, nb_sb[:, kc:kc + 1], "nb", s_wqs[:, kc, :], "s_w")
                prep(wkvs_in[kc * 128:(kc + 1) * 128, :], 384, nkv_sb[:, kc:kc + 1], "nkv", s_wkvs[:, kc, :], "s_w")
            for kc in range(2):
                prep(wos_in[kc * 128:(kc + 1) * 128, :], D, None, None, s_wos[:, kc, :], "s_w")
            prep(mems_in, NMEM, None, None, s_mems, "s_w")
        for c in range(2):
            i = prep_i[0] % NSTG
            prep_i[0] += 1
            for hf in range(2):
                dma("sp", stage[i][hf * 64:(hf + 1) * 64, 0:2048].rearrange("d (j e) -> d j e", e=64),
                    cmp_w1[c].rearrange("(j d) e -> d j e", d=64), [], ["stage%d" % i], "prep_ld%d" % i)
            cp("dve", stageb[i][:, 0:2048], stage[i][:, 0:2048], ["stage%d" % i], ["stageb%d" % i])
            dma("pool", s_w1[c], stageb[i][:, 0:2048], ["stageb%d" % i], ["s_w"], "prep_st%d" % i)
        S.barrier()

    if with_s:
      with ExitStack() as es:
        def sb(name, shape, dt=F32):
            return es.enter_context(nc.sbuf_tensor(name, list(shape), dt))
        NB = SBC
        xs_sb = sb("xs_sb", [NB, D])
        hs_sb = sb("hs_sb", [NB, D])
        xsn = sb("xsn", [NB, D], BF16)
        xsT = sb("xsT", [128, 8, NB], BF16)
        hsT = sb("hsT", [128, 8, NB], BF16)
        wch = [sb("wch%d" % i, [128, 8, 512], BF16) for i in range(2)]
        woc = [sb("woc%d" % i, [128, D], BF16) for i in range(2)]
        cw_s = sb("cw_s", [128, NCC, 3])
        ccin = [sb("ccin%d" % i, [2 * NB, 128]) for i in range(2)]
        ccT = sb("ccT", [128, NCC, 2 * NB])
        uT = sb("uT", [128, NCC, NB])
        u_tok = [sb("u_tok%d" % i, [NB, 128]) for i in range(2)]
        sm = [sb("sm%d" % i, [128, 4 * NB]) for i in range(6)]
        gTs = sb("gTs", [128, NB], BF16)
        wkvs_sb = sb("wkvs_sb", [128, 8, 384], BF16)
        wqs_sb = sb("wqs_sb", [128, 8, 524], BF16)
        wos_sb = sb("wos_sb", [128, 2, D], BF16)
        kvs_sb = sb("kvs_sb", [NB, 384])
        KnT = sb("KnT", [64, 3, NB], BF16)
        QsT = sb("QsT", [64, 4, NB], BF16)
        gz_sb = sb("gz_sb", [NB, 268])
        nf_s = sb("nf_s", [NB, D])
        ptab = sb("ptab", [128, NB], I32)
        ptf = sb("ptf", [128, NB])
        idx8 = sb("idx8", [128, 8], I32)
        idxf = sb("idxf", [128, 8])
        stg = [sb("stg%d" % i, [128, 2048]) for i in range(2)]
        stgb = [sb("stgb%d" % i, [128, 2048], BF16) for i in range(2)]
        RTs = [sb("RT%d" % i, [128, 16384 + 128], BF16) for i in range(2)]
        W1c = sb("W1c", [128, 32, 128], BF16)
        W2c = sb("W2c", [128, 128], BF16)
        W2cf = sb("W2cf", [128, 64])
        pebc = sb("pebc", [128, 1])
        peTc = sb("peTc", [128, 32], BF16)
        peTcf = sb("peTcf", [128, 32])
        hidS = sb("hidS", [128, 1024], BF16)
        KcS = sb("KcS", [128, 1024], BF16)
        VcS = sb("VcS", [128, 8, 65], BF16)
        mems = sb("mems", [128, 8, 257], BF16)
        PcS = sb("PcS", [128, 8, 4], BF16)
        impn = sb("impn", [4, 257])
        scr_s = sb("scr_s", [1, 257])
        scr_s2 = sb("scr_s2", [1, 257])
        m8s = sb("m8s", [1, 16])
        i8s = sb("i8s", [1, 16], mybir.dt.uint32)
        jf = sb("jf", [1, 16])
        jh = sb("jh", [1, 16])
        gbf = sb("gbf", [1, 16])
        gidx = sb("gidx", [128, 8], I32)
        pt1 = sb("pt1", [1, 128 + 8], I32)
        pt1f = sb("pt1f", [1, 128 + 8])
        onehot = sb("onehot", [1, 128])
        pgf = sb("pgf", [1, 16])
        gbi = sb("gbi", [1, 16], I32)
        identf = sb("identf", [128, 128])
        ones4 = sb("ones4", [4, 1])
        trf_ident[0] = identf
        iota_row = sb("iota_row", [1, 128])
        graw = sb("graw", [128, 8], I32)
        grawf = sb("grawf", [128, 8])
        selt = sb("selt", [128, 8, 128])
        selb = sb("selb", [128, 8, 129], BF16)
        KsT = sb("KsT", [64, 8, 128], BF16)
        PsS = sb("PsS", [128, 8, 4], BF16)
        wint = sb("wint", [128, 5, 128])
        winb = sb("winb", [128, 5, 129], BF16)
        KwT = sb("KwT", [64, 5, 128], BF16)
        PwS = sb("PwS", [128, 5, 4], BF16)
        o3 = sb("o3", [4, 3, 65])
        Oraw = sb("Oraw", [NB, 4, 3, 65])
        rzs_ = sb("rzs_", [NB, 4, 3])
        coefs = sb("coefs", [NB, 4, 3])
        Os = sb("Os", [NB, 256])
        Otm = sb("Otm", [NB, 256])
        ozs = sb("ozs", [NB, 256], BF16)
        ozsT = sb("ozsT", [128, 2, NB], BF16)
        part_sb = sb("part_sb", [NB, D])
        ys_sb = xs_sb
        mcol = sb("mcol", [128, 4])

        dma("sp", xs_sb[:], x_s, [], ["xs"], "s_xs")
        for j in range(3):
            dma("sp", cw_s[:, :, j], conv_w[j, :].rearrange("(c p) -> p c", p=128), [], ["cw_s"], "s_cw", allow_slow_non_contiguous=True)
        dma("sp", wkvs_sb[:], s_wkvs, [], ["wkvs"], "s_wkvs")
        dma("sp", wqs_sb[:], s_wqs, [], ["wqs"], "s_wqs")
        dma("sp", wos_sb[:], s_wos, [], ["wos"], "s_wos")
        dma("sp", nf_s[:], norm_f.partition_broadcast(NB), [], ["nf_s"], "s_nf")
        dma("sp", ptab[:], ptab_in.rearrange("b i -> i b"), [], ["ptab"], "s_pt", allow_slow_non_contiguous=True)
        dma("sp", mems[:].rearrange("p a j -> p (a j)"), s_mems, [], ["mems"], "s_mems")
        dma("sp", mcol[:], consts[:, C_MCOL:C_MCOL + 4], [], ["mcol"], "s_mcol")
        dma("sp", identf[:], consts[:, C_ID:C_ID + 128], [], ["identf"], "s_identf")
        dma("sp", iota_row[:], consts[0:1, C_IOTAR:C_IOTAR + 128], [], ["iota_row"], "s_iotar")
        memset("pool", ones4[:], 1.0, ["ones4"])
        memset("pool", W1c[:], 0.0, ["W1c"])
        memset("pool", W2c[:], 0.0, ["W2c"])
        memset("pool", VcS[:], 1.0, ["VcS"])
        memset("pool", selb[:], 1.0, ["selb"])
        memset("pool", winb[:], 1.0, ["winb"])
        memset("pool", wint[:], 0.0, ["wint"])
        for c in range(2):
            dma("sp", W1c[c * 64:(c + 1) * 64, :, c * 64:(c + 1) * 64],
                s_w1[c][c * 64:(c + 1) * 64, :].rearrange("p (j e) -> p j e", e=64), ["W1c"], ["W1c"], "s_w1c")
            dma("sp", W2cf[c * 64:(c + 1) * 64, :], cmp_w2[c], [], ["W2cf"], "s_w2c")
            dma("sp", peTcf[c * 64:(c + 1) * 64, :], cmp_pe[c].rearrange("j d -> d j"), [], ["peTcf"], "s_pec", allow_slow_non_contiguous=True)
        for c in range(2):
            cp("dve", W2c[c * 64:(c + 1) * 64, c * 64:(c + 1) * 64], W2cf[c * 64:(c + 1) * 64, :], ["W2cf", "W2c"], ["W2c"])
        cp("dve", peTc[:], peTcf[:], ["peTcf"], ["peTc"])
        for j in range(32):
            mm(psn[7][:, 0:1], W1c[:, j, :], peTc[:, j:j + 1], j == 0, j == 31, ["W1c", "peTc"], [("ps", 7)])
        cp("act", pebc[:], psn[7][:, 0:1], [("ps", 7)], ["pebc"])
        cp("dve", ptf[:], ptab[:], ["ptab"], ["ptf"])

        def small_norm_T(src, src_tok, dstT, dst_tok, slot):
            ssq = stat[0:NB, slot:slot + 1]
            rstd = stat[0:NB, slot + 1:slot + 2]
            act(xsn[:], src, AF.Square, [src_tok], ["xsn", ("sst", slot)], accum_out=ssq)
            act(ssq, ssq, AF.Sqrt, [("sst", slot), "eps"], [("sst", slot)], scale=1.0 / D, bias=eps_sb[0:NB, 0:1])
            S.op("dve", lambda e, rstd=rstd, ssq=ssq: e.reciprocal(rstd, ssq), [("sst", slot)], [("srstd", slot)])
            act(xsn[:], src, AF.Copy, [src_tok, ("srstd", slot)], ["xsn"], scale=rstd)
            for kc in range(8):
                tr(ps_tr[:, kc * NB:(kc + 1) * NB], xsn[:, kc * 128:(kc + 1) * 128], ["xsn"], ["ps_tr"])
            cp("act", dstT[:], ps_tr[:, 0:8 * NB].rearrange("p (k t) -> p k t", k=8), ["ps_tr"], [dst_tok])

        small_norm_T(xs_sb[:], "xs", xsT, "xsT", 20)
        for cc in range(NCC):
            bank = (5, 6)[cc % 2]
            k = cc % 2
            dma("sp", ccin[k][:], cconv_s[:, :, cc * 128:(cc + 1) * 128].rearrange("b r c -> (b r) c"), [], [("ccin", k)], "s_cc%d" % k)
            trf(psn[bank][:, 0:2 * NB], ccin[k][:], [("ccin", k)], [("ps", bank)])
            cp("act", ccT[:, cc, :], psn[bank][:, 0:2 * NB], [("ps", bank)], [("ccT", cc)])
        for cc in range(NCC):
            k = cc % 2
            dma("sp", wch[k][:], s_win[cc], [], [("wch", k)], "s_wch%d" % k)
            dma("sp", woc[k][:], s_wout[:, cc, :], [], [("woc", k)], "s_woc%d" % k)
            bank = (2, 3)[k]
            for part in range(4):
                for kc in range(8):
                    mm(psn[bank][:, part * NB:(part + 1) * NB], wch[k][:, kc, part * 128:(part + 1) * 128], xsT[:, kc, :],
                       part == 0 and kc == 0, part == 3 and kc == 7, [("wch", k), "xsT"], [("ps", bank)])
            pb_, pc_, ph_, pz_ = [psn[bank][:, p * NB:(p + 1) * NB] for p in range(4)]
            s0, s1, s2, s3, s4, s5 = [sm[i][:, 0:NB] for i in range(6)]
            cp("act", s0, pc_, [("ps", bank)], [("sm", 0)])
            tt("dve", uT[:, cc, :], s0, ph_, ALU.mult, [("sm", 0), ("ps", bank)], [("uT", cc)])
            ccv = ccT[:, cc, :].rearrange("p (b r) -> p r b", r=2)
            ts("dve", s1, uT[:, cc, :], cw_s[:, cc, 2:3], None, ALU.mult, None, [("uT", cc), "cw_s"], [("sm", 1)])
            stt("dve", s1, ccv[:, 1, :], cw_s[:, cc, 1:2], s1, ALU.mult, ALU.add, [("ccT", cc), "cw_s", ("sm", 1)], [("sm", 1)])
            stt("dve", s1, ccv[:, 0, :], cw_s[:, cc, 0:1], s1, ALU.mult, ALU.add, [("ccT", cc), "cw_s", ("sm", 1)], [("sm", 1)])
            act(s2, pz_, AF.Silu, [("ps", bank)], [("sm", 2)])
            tt("dve", s3, pb_, s2, ALU.mult, [("ps", bank), ("sm", 2)], [("sm", 3)])
            tt("dve", gTs[:], s3, s1, ALU.mult, [("sm", 3), ("sm", 1)], ["gTs"])
            for half in range(2):
                mm(psn[half][0:NB, :], gTs[:], woc[k][:, half * 512:(half + 1) * 512], cc == 0, cc == NCC - 1,
                   ["gTs", ("woc", k)], [("ps", half)])
        for half in range(2):
            tt("dve", hs_sb[:, half * 512:(half + 1) * 512], psn[half][0:NB, :], xs_sb[:, half * 512:(half + 1) * 512], ALU.add,
               [("ps", half), "xs"], ["hs"])
        for cc in range(NCC):
            bank = (5, 6)[cc % 2]
            k = cc % 2
            trf(psn[bank][0:NB, 0:128], uT[:, cc, :], [("uT", cc)], [("ps", bank)])
            cp("act", u_tok[k][:], psn[bank][0:NB, 0:128], [("ps", bank)], [("u_tok", k)])
            dma("pool", o_conv_s[:, 1, cc * 128:(cc + 1) * 128], u_tok[k][:], [("u_tok", k)], [("o_conv_s1", cc)], "s_ocs1_%d" % k, is_out=True)
        dma("pool", o_conv_s[:, 0, :], cconv_s[:, 1, :], [], ["o_conv_s0"], "s_ocs0", is_out=True)
        small_norm_T(hs_sb[:], "hs", hsT, "hsT", 22)
        for kc in range(8):
            mm(psn[5][0:NB, 0:384], hsT[:, kc, :], wkvs_sb[:, kc, :], kc == 0, kc == 7, ["hsT", "wkvs"], [("ps", 5)])
        cp("act", kvs_sb[:], psn[5][0:NB, 0:384], [("ps", 5)], ["kvs"])
        dma("pool", o_kv_s, kvs_sb[:], ["kvs"], ["o_kv_s"], "s_okv", is_out=True)
        for br in range(3):
            for kc in range(8):
                mm(psn[6][0:64, br * NB:(br + 1) * NB], wkvs_sb[:, kc, br * 128:br * 128 + 64], hsT[:, kc, :],
                   br == 0 and kc == 0, br == 2 and kc == 7, ["hsT", "wkvs"], [("ps", 6)])
        cp("act", KnT[:], psn[6][0:64, 0:3 * NB].rearrange("p (a b) -> p a b", a=3), [("ps", 6)], ["KnT"])
        for g in range(4):
            for kc in range(8):
                mm(psn[7][0:64, g * NB:(g + 1) * NB], wqs_sb[:, kc, g * 64:(g + 1) * 64], hsT[:, kc, :],
                   g == 0 and kc == 0, g == 3 and kc == 7, ["hsT", "wqs"], [("ps", 7)])
        cp("act", QsT[:], psn[7][0:64, 0:4 * NB].rearrange("p (a b) -> p a b", a=4), [("ps", 7)], ["QsT"])
        for kc in range(8):
            mm(psn[5][0:NB, 0:268], hsT[:, kc, :], wqs_sb[:, kc, 256:524], kc == 0, kc == 7, ["hsT", "wqs"], [("ps", 5)])
        act(gz_sb[:, 0:12], psn[5][0:NB, 0:12], AF.Sigmoid, [("ps", 5)], ["gz"])
        act(gz_sb[:, 12:268], psn[5][0:NB, 12:268], AF.Silu, [("ps", 5), "gz"], ["gz"])
        dma("pool", o_win_s[:, 0:511, :], win_s[:, 1:512, :], [], ["o_win_s0"], "s_ows0", is_out=True)
        dma("pool", o_win_s[:, 511, :], kvs_sb[:, 256:384], ["kvs"], ["o_win_s1"], "s_ows1", is_out=True)

        def gather_chunk(b, m):
            RT = RTs[b % 2]
            RTt = ("RT", b % 2)
            if True:
                k = m % 2
                ts("dve", idxf[:, m:m + 1], ptf[:, b:b + 1], 8.0, float(m), ALU.mult, ALU.add, ["ptf"], [("idxf", m)])
                cp("dve", idx8[:, m:m + 1], idxf[:, m:m + 1], [("idxf", m)], [("idx8", m)])
                S.op("pool", lambda e, k=k, m=m: e.indirect_dma_start(
                    out=stg[k][:], out_offset=None, in_=pool_cmp,
                    in_offset=bass.IndirectOffsetOnAxis(ap=idx8[:, m:m + 1], axis=0)),
                    [("idx8", m)], [("stg", k)], dma="s_stg%d" % k)
                cp("act" if m % 2 else "dve", stgb[k][:], stg[k][:], [("stg", k)], [("stgb", k)])
                for rh in range(2):
                    for r8 in range(8):
                        r = rh * 8 + r8
                        tr(ps_tr[:, r8 * 128:(r8 + 1) * 128], stgb[k][:, r * 128:(r + 1) * 128], [("stgb", k)], ["ps_tr"])
                    base = 16 * m + rh * 8
                    dstv = RT[:, base:base + 16384].rearrange("p (i x) -> p x i", x=128)[:, 0:8, :]
                    cp("act", dstv, ps_tr[:].rearrange("p (r i) -> p r i", r=8), ["ps_tr"], [RTt])
        def attn_body(b):
            RT = RTs[b % 2]
            RTt = ("RT", b % 2)
            for ci, (n0, nn) in enumerate(((0, 512), (512, 511))):
                bank = (0, 1)[ci]
                for j in range(32):
                    c0 = 16 * n0 + j
                    mm(psn[bank][:, 0:nn], W1c[:, j, :], RT[:, c0:c0 + 16 * (nn - 1) + 1:16], j == 0, j == 31, ["W1c", RTt], [("ps", bank)])
                act(hidS[:, n0:n0 + nn], psn[bank][:, 0:nn], AF.Silu, [("ps", bank), "pebc"], ["hidS"], bias=pebc[:, 0:1])
            yield
            for ci, (n0, nn) in enumerate(((0, 512), (512, 511))):
                bank = (2, 3)[ci]
                mm(psn[bank][:, 0:nn], W2c[:], hidS[:, n0:n0 + nn], True, True, ["W2c", "hidS"], [("ps", bank)])
                cp("act", KcS[:, n0:n0 + nn], psn[bank][:, 0:nn], [("ps", bank)], ["KcS"])
            for ch in range(8):
                nn = 128 if ch < 7 else 127
                mm(psn[5][0:nn, (ch % 4) * 128:(ch % 4 + 1) * 128], hidS[:, ch * 128:ch * 128 + nn], W2c[:], ch % 4 == 0, ch % 4 == 3,
                   ["W2c", "hidS"], [("ps", 5)])
                if ch % 4 == 3:
                    c4 = ch - 3
                    for q in range(4):
                        nq = 128 if (c4 + q) < 7 else 127
                        cp("act", VcS[0:nq, c4 + q, 0:64], psn[5][0:nq, q * 128 + 64:(q + 1) * 128], [("ps", 5)], ["VcS"])
            yield
            Qb = QsT[:, :, b]
            for ch in range(8):
                nn = 128 if ch < 7 else 127
                mm(psn[6][0:nn, ch * 4:(ch + 1) * 4], KcS[0:64, ch * 128:ch * 128 + nn], Qb, ch == 0, ch == 7, ["KcS", "QsT"], [("ps", 6)])
            memset("dve", PcS[:], 0.0, ["PcS"])
            act(PcS[:, 0:7, :], psn[6][:, 0:28].rearrange("p (c g) -> p c g", g=4), AF.Exp, [("ps", 6), "PcS"], ["PcS"], scale=SCALE)
            act(PcS[0:127, 7, :], psn[6][0:127, 28:32], AF.Exp, [("ps", 6), "PcS"], ["PcS"], scale=SCALE)
            for ch in range(8):
                mm(psn[7][0:4, 0:65], PcS[:, ch, :], VcS[:, ch, :], ch == 0, ch == 7, ["PcS", "VcS"], [("ps", 7)])
            cp("act", o3[:, 0, :], psn[7][0:4, 0:65], [("ps", 7)], ["o3"])
            for ch in range(8):
                mm(psn[7][0:4, 128:128 + 257], PcS[:, ch, :], mems[:, ch, :], ch == 0, ch == 7, ["PcS", "mems"], [("ps", 7)])
            ts("dve", o3[:, 0, 64:65], o3[:, 0, 64:65], 1e-30, None, ALU.max, None, ["o3"], ["o3"])
            S.op("dve", lambda e: e.reciprocal(sm[4][0:4, 0:1], o3[:, 0, 64:65]), ["o3"], [("sm", 4)])
            ts("dve", impn[:], psn[7][0:4, 128:128 + 257], sm[4][0:4, 0:1], None, ALU.mult, None, [("ps", 7), ("sm", 4)], ["impn"])
            mm(psn[5][0:1, 0:257], ones4[:], impn[:], True, True, ["impn", "ones4"], [("ps", 5)])
            cp("act", scr_s[:], psn[5][0:1, 0:257], [("ps", 5)], ["scr_s"])
            memset("dve", scr_s[:, 0:1], 3e9, ["scr_s"])
            memset("dve", scr_s[:, 256:257], 2e9, ["scr_s"])
            memset("dve", scr_s[:, 255:256], 1e9, ["scr_s"])
            S.op("dve", lambda e: e.max(m8s[:, 0:8], scr_s[:]), ["scr_s"], ["m8s"])
            S.op("dve", lambda e: e.max_index(i8s[:, 0:8], m8s[:, 0:8], scr_s[:]), ["scr_s", "m8s"], ["i8s"])
            S.op("dve", lambda e: e.match_replace(scr_s2[:], m8s[:, 0:8], scr_s[:], -3e9), ["scr_s", "m8s"], ["scr_s2"])
            S.op("dve", lambda e: e.max(m8s[:, 8:16], scr_s2[:]), ["scr_s2"], ["m8s"])
            S.op("dve", lambda e: e.max_index(i8s[:, 8:16], m8s[:, 8:16], scr_s2[:]), ["scr_s2", "m8s"], ["i8s"])
            yield
            cp("dve", jf[:], i8s[:], ["i8s"], ["jf"])
            ts("dve", jf[:], jf[:], 255.0, None, ALU.min, None, ["jf"], ["jf"])
            cp("dve", i8s[:].bitcast(I32), jf[:], ["jf"], ["i8s"])
            S.op("dve", lambda e: e.tensor_single_scalar(i8s[:].bitcast(I32), i8s[:].bitcast(I32), 1, ALU.arith_shift_right), ["i8s"], ["i8s"])
            cp("dve", jh[:], i8s[:].bitcast(I32), ["i8s"], ["jh"])
            stt("dve", gbf[:], jh[:], -2.0, jf[:], ALU.mult, ALU.add, ["jh", "jf"], ["gbf"])
            dma("sp", pt1[:, 0:128], ptab_in[b:b + 1, :], [], ["pt1"], "s_pt1")
            cp("dve", pt1f[:, 0:128], pt1[:, 0:128], ["pt1"], ["pt1f"])
            for q in range(16):
                ts("dve", onehot[:], iota_row[:], jh[:, q:q + 1], None, ALU.is_equal, None, ["jh", "iota_row", "pgf"], ["oh"])
                tt("dve", onehot[:], onehot[:], pt1f[:, 0:128], ALU.mult, ["oh", "pt1f"], ["oh"])
                S.op("dve", lambda e, q=q: e.reduce_sum(pgf[:, q:q + 1], onehot[:], mybir.AxisListType.X), ["oh"], ["pgf"])
            stt("dve", gbf[:], pgf[:], 2.0, gbf[:], ALU.mult, ALU.add, ["pgf", "gbf"], ["gbf"])
            cp("dve", gbi[:], gbf[:], ["gbf"], ["gbi"])
            dma("sp", s_gbi[b:b + 1, :], gbi[:], ["gbi"], [("s_gbi", b)], "s_gbist")
            dma("sp", graw[0:64, :], s_gbi[b, 0:16:2].partition_broadcast(64), [("s_gbi", b)], ["graw"], "s_graw", allow_slow_non_contiguous=True)
            dma("sp", graw[64:128, :], s_gbi[b, 1:16:2].partition_broadcast(64), [("s_gbi", b), "graw"], ["graw"], "s_graw", allow_slow_non_contiguous=True)
            cp("dve", grawf[:], graw[:], ["graw"], ["grawf"])
            ts("dve", grawf[:], grawf[:], 64.0, mcol[:, 2:3], ALU.mult, ALU.add, ["grawf", "mcol"], ["grawf"])
            cp("dve", gidx[:], grawf[:], ["grawf"], ["gidx"])
            for q in range(8):
                S.op("pool", lambda e, q=q: e.indirect_dma_start(
                    out=selt[:, q, :], out_offset=None, in_=pool_sel,
                    in_offset=bass.IndirectOffsetOnAxis(ap=gidx[:, q:q + 1], axis=0)),
                    ["gidx"], [("selt", q)], dma="s_selt%d" % q)
            dma("pool", selt[64:65, 0, :], kvs_sb[b:b + 1, 128:256], ["kvs", ("selt", 0)], [("selt", 0)], "s_selt0")
            cp("dve", selb[:, :, 0:128], selt[:], [("selt", q) for q in range(8)] + ["selb"], ["selb"])
            for q in range(8):
                tr(ps_tr[0:64, q * 128:(q + 1) * 128], selb[:, q, 0:64], ["selb"], ["ps_tr"])
            cp("act", KsT[:], ps_tr[0:64, :].rearrange("p (q k) -> p q k", q=8), ["ps_tr"], ["KsT"])
            for q in range(8):
                mm(psn[6][:, q * 4:(q + 1) * 4], KsT[:, q, :], Qb, q == 0, q == 7, ["KsT", "QsT"], [("ps", 6)])
            act(PsS[:], psn[6][:, 0:32].rearrange("p (c g) -> p c g", g=4), AF.Exp, [("ps", 6)], ["PsS"], scale=SCALE)
            ts("dve", PsS[:, 0, :], PsS[:, 0, :], mcol[:, 0:1], None, ALU.mult, None, ["PsS", "mcol"], ["PsS"])
            for q in range(8):
                mm(psn[7][0:4, 0:65], PsS[:, q, :], selb[:, q, 64:129], q == 0, q == 7, ["PsS", "selb"], [("ps", 7)])
            cp("act", o3[:, 1, :], psn[7][0:4, 0:65], [("ps", 7)], ["o3"])
            yield
            dma("sp", wint[:, 0:4, :], win_s[b].rearrange("(t p) c -> p t c", p=128), ["wint"], ["wint"], "s_wint")
            dma("sp", wint[0:1, 4, :], kvs_sb[b:b + 1, 256:384], ["kvs", "wint"], ["wint"], "s_wint")
            cp("dve", winb[:, :, 0:128], wint[:], ["wint", "winb"], ["winb"])
            for q in range(5):
                tr(ps_tr[0:64, q * 128:(q + 1) * 128], winb[:, q, 0:64], ["winb"], ["ps_tr"])
            cp("act", KwT[:], ps_tr[0:64, 0:640].rearrange("p (q k) -> p q k", q=5), ["ps_tr"], ["KwT"])
            for q in range(5):
                mm(psn[6][:, q * 4:(q + 1) * 4], KwT[:, q, :], Qb, q == 0, q == 4, ["KwT", "QsT"], [("ps", 6)])
            act(PwS[:], psn[6][:, 0:20].rearrange("p (c g) -> p c g", g=4), AF.Exp, [("ps", 6)], ["PwS"], scale=SCALE)
            ts("dve", PwS[:, 4, :], PwS[:, 4, :], mcol[:, 1:2], None, ALU.mult, None, ["PwS", "mcol"], ["PwS"])
            for q in range(5):
                mm(psn[7][0:4, 0:65], PwS[:, q, :], winb[:, q, 64:129], q == 0, q == 4, ["PwS", "winb"], [("ps", 7)])
            cp("act", o3[:, 2, :], psn[7][0:4, 0:65], [("ps", 7)], ["o3"])
            dma("pool", s_o3[b], o3[:], ["o3"], [("s_o3", b)], "s_o3st")

        NBR = NB if "nosattn" not in DBG_S else 0
        if NBR:
            for m in range(8):
                gather_chunk(0, m)
        for b in range(NBR):
            pending = list(range(8)) if b + 1 < NBR else []
            for _ in attn_body(b):
                for _k in range(2):
                    if pending:
                        gather_chunk(b + 1, pending.pop(0))
            while pending:
                gather_chunk(b + 1, pending.pop(0))
        if "nosattn" in DBG_S:
            memset("dve", Oraw[:], 1.0, ["Oraw"])
            for b_ in range(NB):
                dma("pool", s_o3[b_], Oraw[0:4, 0, :, :], ["Oraw"], [("s_o3", b_)], "s_o3st")
        dma("sp", Oraw[:].rearrange("p g r e -> p (g r e)"), s_o3.rearrange("b g r e -> b (g r e)"), [("s_o3", b) for b in range(NB)], ["Oraw"], "s_orawld")
        ts("dve", rzs_[:], Oraw[:, :, :, 64], 1e-30, None, ALU.max, None, ["Oraw"], ["rzs"])
        S.op("dve", lambda e: e.reciprocal(rzs_[:], rzs_[:]), ["rzs"], ["rzs"])
        tt("dve", coefs[:], rzs_[:], gz_sb[:, 0:12].rearrange("p (g r) -> p g r", r=3), ALU.mult, ["rzs", "gz"], ["coefs"])
        for br in range(3):
            cbb = coefs[:, :, br:br + 1].to_broadcast([NB, 4, 64])
            dst = (Os if br == 0 else Otm)[:].rearrange("p (g d) -> p g d", d=64)
            tt("dve", dst, Oraw[:, :, br, 0:64], cbb, ALU.mult, ["Oraw", "coefs"], ["Os" if br == 0 else "Otm"])
            if br:
                tt("dve", Os[:], Os[:], Otm[:], ALU.add, ["Os", "Otm"], ["Os"])
        tt("dve", ozs[:], Os[:], gz_sb[:, 12:268], ALU.mult, ["Os", "gz"], ["ozs"])
        for kc in range(2):
            tr(ps_tr[:, kc * NB:(kc + 1) * NB], ozs[:, kc * 128:(kc + 1) * 128], ["ozs"], ["ps_tr"])
        cp("act", ozsT[:], ps_tr[:, 0:2 * NB].rearrange("p (k t) -> p k t", k=2), ["ps_tr"], ["ozsT"])
        for half in range(2):
            for kc in range(2):
                mm(psn[half][0:NB, :], ozsT[:, kc, :], wos_sb[:, kc, half * 512:(half + 1) * 512], kc == 0, kc == 1, ["ozsT", "wos"], [("ps", half)])
            cp("act", part_sb[:, half * 512:(half + 1) * 512], psn[half][0:NB, :], [("ps", half)], ["part"])
        dma("pool", cc_in, part_sb[:], ["part"], ["cc_in"], "s_ccin")
        if "ccdbg" in DBG_S:
            dma("pool", o_dpre, part_sb[:], ["part"], ["o_dpre"], "s_dpre", is_out=True)
        if use_cc:
            S.op("pool", lambda e: e.collective_compute("AllReduce", ALU.add, replica_groups=[[0, 1, 2, 3], [4, 5, 6, 7]],
                                                        ins=[cc_in.opt()], outs=[cc_out.opt()]), ["cc_in"], ["cc_out"], dma="s_cc_ar", inc=1)
            dma("sp", part_sb[:], cc_out, ["cc_out", "part"], ["part"], "s_ccout")
            if "ccdbg" in DBG_S:
                dma("pool", o_dpost, part_sb[:], ["part"], ["o_dpost"], "s_dpost", is_out=True)
        else:
            dma("pool", o_part, part_sb[:], ["part"], ["o_part"], "s_opart", is_out=True)
        tt("dve", hs_sb[:], hs_sb[:], part_sb[:], ALU.add, ["hs", "part"], ["hs"])
        ssq = stat[0:NB, 24:25]
        rstd = stat[0:NB, 25:26]
        act(ys_sb[:], hs_sb[:], AF.Square, ["hs"], ["ys", "ssqs"], accum_out=ssq)
        act(ssq, ssq, AF.Sqrt, ["ssqs", "eps"], ["ssqs"], scale=1.0 / D, bias=eps_sb[0:NB, 0:1])
        S.op("dve", lambda e, rstd=rstd, ssq=ssq: e.reciprocal(rstd, ssq), ["ssqs"], ["rstds"])
        stt("dve", ys_sb[:], hs_sb[:], rstd, nf_s[:], ALU.mult, ALU.mult, ["hs", "rstds", "nf_s", "ys"], ["ys"])
        dma("pool", o_y_s, ys_sb[:], ["ys"], ["o_y_s"], "s_oys", is_out=True)
        S.barrier()

    KT = [None, sbp("KT_sel", [128, 2, SEQ], BF16), sbp("KT_win", [128, 2, SEQ], BF16)]
    VV = [None, sbp("V_sel", [128, 32, 4, 72], BF16), sbp("V_win", [128, 32, 4, 72], BF16)]
    KcT = sbp("KcT", [128, 2, 256], BF16)
    Vc = sbp("Vc", [128, 2, 4, 65], BF16)
    memset("pool", VV[1][:], 1.0, ["V1init"])
    memset("pool", VV[2][:], 1.0, ["V2init"])
    memset("pool", Vc[:], 1.0, ["Vcinit"])

    with ExitStack() as es:
        def sb(name, shape, dt=F32):
            return es.enter_context(nc.sbuf_tensor(name, list(shape), dt))
        X = sb("X", [128, 4, D])
        xnb = [sb("xnb%d" % i, [128, D], BF16) for i in range(2)]
        aT = sb("aT", [128, 8, ST], BF16)
        NRING = 2
        wring = [sb("wring%d" % i, [128, 8, 512], BF16) for i in range(NRING)]
        wout_sb = sb("wout_sb", [128, NCC, D], BF16)
        G = sb("G", [128, NCC, ST], BF16)
        U = [sb("U%d" % i, [128, ST + 2]) for i in range(2)]
        UC = sb("UC", [128, NCC, 2])
        Vb = [sb("Vb%d" % i, [128, ST]) for i in range(2)]
        c_sb1 = sb("c_sb", [128, ST])
        c_sb = [c_sb1, c_sb1]
        sz_sb = [sb("sz_sb%d" % i, [128, ST]) for i in range(2)]
        t1_sb1 = sb("t1_sb", [128, ST])
        t1_sb = [t1_sb1, t1_sb1]
        kvst = [sb("kvst%d" % i, [128, 512]) for i in range(2)]
        cw_sb = sb("cw_sb", [128, NCC, 3])
        W1blk = sb("W1blk", [128, 2, 32, 128], BF16)
        W2f = sb("W2f", [128, 2, 64])
        W2blk = sb("W2blk", [128, 2, 128], BF16)
        peTf = sb("peTf", [128, 2, 32])
        peT = sb("peT", [128, 2, 32], BF16)
        pebias = sb("pebias", [128, 2])
        CK = sb("CK", [128, 4, 16 + ST], BF16)
        hidT = sb("hidT", [128, 2, 32], BF16)
        vcst = sb("vcst", [32, 4, 64], BF16)

        for j in range(3):
            dma("sp", cw_sb[:, :, j], conv_w[j, :].rearrange("(c p) -> p c", p=128), [], ["cw"], "c_cw", allow_slow_non_contiguous=True)
        memset("pool", UC[:], 0.0, [("UC", cc) for cc in range(NCC)])
        memset("pool", CK[:], 0.0, ["CK"])
        memset("pool", W1blk[:], 0.0, ["W1blk"])
        memset("pool", W2blk[:], 0.0, ["W2blk"])
        dma("sp", wout_sb[:], s_wout, [], ["wout"], "wres0")
        for c in range(2):
            for hf in range(2):
                dma("sp", W1blk[hf * 64:(hf + 1) * 64, c, :, hf * 64:(hf + 1) * 64],
                    s_w1[c][hf * 64:(hf + 1) * 64, :].rearrange("p (j e) -> p j e", e=64), [], ["W1blk"], "c_w1")
        for hf in range(2):
            dma("sp", W2f[hf * 64:(hf + 1) * 64, :, :], cmp_w2.rearrange("c e d -> e c d"), [], ["W2f"], "c_w2")
            for c in range(2):
                dma("sp", peTf[hf * 64:(hf + 1) * 64, c, :], cmp_pe[c].rearrange("j d -> d j"), [], ["peTf"], "c_pe", allow_slow_non_contiguous=True)
        for c in range(2):
            for hf in range(2):
                cp("dve", W2blk[hf * 64:(hf + 1) * 64, c, hf * 64:(hf + 1) * 64], W2f[hf * 64:(hf + 1) * 64, c, :], ["W2f", "W2blk"], ["W2blk"])
        cp("dve", peT[:], peTf[:], ["peTf"], ["peT"])
        import os
        DBG = os.environ.get("KDBG", "")
        for c in range(2):
            for j in range(32):
                mm(psn[7][:, c:c + 1], W1blk[:, c, j, :], peT[:, c, j:j + 1], j == 0, j == 31, ["W1blk", "peT"], [("ps", 7)])
        cp("act", pebias[:], psn[7][:, 0:2], [("ps", 7)], ["pebias"])

        def norm_transpose(src_ap, src_tok, i, dst_tok, slot):
            ssq = stat[:, slot:slot + 1]
            rstd = stat[:, 8 + slot:9 + slot]
            xb = xnb[slot % 2]
            xbt = ("xnb", slot % 2)
            act(xb[:], src_ap, AF.Square, [src_tok], [xbt, ("ssq", slot)], accum_out=ssq)
            act(ssq, ssq, AF.Sqrt, [("ssq", slot), "eps"], [("ssq", slot)], scale=1.0 / D, bias=eps_sb[:, 0:1])
            S.op("dve", lambda e, rstd=rstd, ssq=ssq: e.reciprocal(rstd, ssq), [("ssq", slot)], [("rstd", slot)])
            xb = xnb[slot % 2]
            xbt = ("xnb", slot % 2)
            act(xb[:], src_ap, AF.Copy, [src_tok, ("rstd", slot)], [xbt], scale=rstd)
            for kc in range(8):
                tr(ps_tr[:, kc * 128:(kc + 1) * 128], xb[:, kc * 128:(kc + 1) * 128], [xbt], ["ps_tr"])
            cp("dve", aT[:, :, i * 128:(i + 1) * 128], ps_tr[:].rearrange("p (k t) -> p k t", k=8), ["ps_tr"], [dst_tok])

        AT = [("aT", i) for i in range(4)]
        ring_n = [0]
        hs_n = [0]
        gb_n = [0]
        kv_n = [0]

        def gbank():
            b = (5, 6, 7)[gb_n[0] % 3]
            gb_n[0] += 1
            return b

        def ring_load(src):
            slot = ring_n[0] % NRING
            ring_n[0] += 1
            dma("sp", wring[slot][:], src, [], [("wr", slot)], "wr%d" % slot)
            return slot

        for t in range(nst):
            r0 = t * ST
            XT = [("X", i) for i in range(4)]
            dma("sp", X[:], x_p[r0:r0 + ST, :].rearrange("(i p) d -> p i d", p=128), [], XT, "xld")
            for i in range(4):
                norm_transpose(X[:, i, :], ("X", i), i, ("aT", i), i)
            for cc in range(NCC):
                slot = ring_load(s_win[cc])
                wt = ("wr", slot)
                k = cc % 2
                b0 = 2 * (hs_n[0] % 2)
                hs_n[0] += 1
                for part, bank in ((1, b0), (2, b0 + 1)):
                    for kc in range(8):
                        mm(psn[bank][:], wring[slot][:, kc, part * 128:(part + 1) * 128], aT[:, kc, :], kc == 0, kc == 7,
                           [wt] + AT, [("ps", bank)])
                cp("act", c_sb[k][:], psn[b0][:], [("ps", b0)], ["c_sb"])
                cp("pool", U[k][:, 0:2], UC[:, cc, :], [("UC", cc)], [("U", k)])
                tt("dve", U[k][:, 2:ST + 2], c_sb[k][:], psn[b0 + 1][:], ALU.mult, ["c_sb", ("ps", b0 + 1), ("U", k)], [("U", k)])
                cp("pool", UC[:, cc, :], U[k][:, ST:ST + 2], [("U", k)], [("UC", cc)])
                ts("dve", Vb[k][:], U[k][:, 2:ST + 2], cw_sb[:, cc, 2:3], None, ALU.mult, None, [("U", k), "cw"], [("Vb", k)])
                stt("dve", Vb[k][:], U[k][:, 1:ST + 1], cw_sb[:, cc, 1:2], Vb[k][:], ALU.mult, ALU.add, [("U", k), "cw", ("Vb", k)], [("Vb", k)])
                stt("dve", Vb[k][:], U[k][:, 0:ST], cw_sb[:, cc, 0:1], Vb[k][:], ALU.mult, ALU.add, [("U", k), "cw", ("Vb", k)], [("Vb", k)])
                b1 = 2 * (hs_n[0] % 2)
                hs_n[0] += 1
                for part, bank in ((0, b1), (3, b1 + 1)):
                    for kc in range(8):
                        mm(psn[bank][:], wring[slot][:, kc, part * 128:(part + 1) * 128], aT[:, kc, :], kc == 0, kc == 7,
                           [wt] + AT, [("ps", bank)])
                act(sz_sb[k][:], psn[b1 + 1][:], AF.Silu, [("ps", b1 + 1)], [("sz", k)])
                tt("dve", t1_sb[k][:], psn[b1][:], sz_sb[k][:], ALU.mult, [("ps", b1), ("sz", k)], ["t1"])
                tt("pool", G[:, cc, :], t1_sb[k][:], Vb[k][:], ALU.mult, ["t1", ("Vb", k)], [("G", cc)])
            for i in range(4):
                for half in range(2):
                    bank = gbank()
                    for cc in range(NCC):
                        mm(psn[bank][:], G[:, cc, i * 128:(i + 1) * 128], wout_sb[:, cc, half * 512:(half + 1) * 512], cc == 0, cc == NCC - 1,
                           [("G", cc), "wout"], [("ps", bank)])
                    xs_ = X[:, i, half * 512:(half + 1) * 512]
                    tt("dve", xs_, psn[bank][:], xs_, ALU.add, [("ps", bank), ("X", i)], [("X", i)])
                if "nosh" not in DBG:
                    dma("pool", s_h[r0 + i * 128:r0 + (i + 1) * 128, :], X[:, i, :], [("X", i)], [("s_h", t, i)], "hst%d" % i)
                norm_transpose(X[:, i, :], ("X", i), i, ("aT", i), 4 + i)
            if "nosh" not in DBG:
                dma("pool", s_hnT[t], aT[:].rearrange("p k t -> p (k t)"), AT, [("s_hnT", t)], "hnst")
            for br in range(3):
                if "nokv" in DBG:
                    break
                slot = ring_load(s_wkv[br])
                wt = ("wr", slot)
                wr = wring[slot]
                for i in range(4):
                    tile_idx = t * 4 + i
                    bank = gbank()
                    for kc in range(8):
                        mm(psn[bank][:], aT[:, kc, i * 128:(i + 1) * 128], wr[:, kc, :], kc == 0, kc == 7, [("aT", i), wt], [("ps", bank)])
                    need_out = (br < 2) or (t == NST - 1)
                    if need_out:
                        k = kv_n[0] % 2
                        kv_n[0] += 1
                        cp("act", kvst[k][:], psn[bank][:], [("ps", bank)], [("kvst", k)])
                        if br == 0:
                            dst = o_cmp_p[r0 + i * 128:r0 + (i + 1) * 128, :]
                        elif br == 1:
                            dst = o_sel_p[r0 + i * 128:r0 + (i + 1) * 128, :]
                        else:
                            dst = o_win_p[i * 128:(i + 1) * 128, :]
                        dma("pool", dst, kvst[k][:], [("kvst", k)], [("okv", t, i, br)], "kvst%d" % k, is_out=True)
                    if br >= 1 and "novv" not in DBG:
                        cp("act", VV[br][:, tile_idx, :, 0:64], psn[bank][:, 256:512].rearrange("p (h d) -> p h d", d=64),
                           [("ps", bank), "V%dinit" % br], [("V", br, tile_idx)])
                if "nock" in DBG:
                    continue
                if br == 0 and "nock0" in DBG:
                    continue
                if br >= 1 and "nokt" in DBG:
                    continue
                if br == 0:
                    for ch in range(4):
                        cp("pool", CK[:, ch, 0:16], CK[:, ch, ST:ST + 16], ["CK"], ["CK"])
                    for ch in range(4):
                        bank = gbank()
                        for kc in range(8):
                            mm(psn[bank][:], wr[:, kc, ch * 128:(ch + 1) * 128], aT[:, kc, :], kc == 0, kc == 7, AT + [wt], [("ps", bank)])
                        cp("act" if ch % 2 else "dve", CK[:, ch, 16:16 + ST], psn[bank][:], [("ps", bank), "CK"], ["CK"])
                else:
                    for hp in range(2):
                        bank = gbank()
                        for kc in range(8):
                            mm(psn[bank][:], wr[:, kc, hp * 128:(hp + 1) * 128], aT[:, kc, :], kc == 0, kc == 7, AT + [wt], [("ps", bank)])
                        cp("act" if hp else "dve", KT[br][:, hp, r0:r0 + ST], psn[bank][:], [("ps", bank)], [("KT", br, t)])
            m0 = 1 if t == 0 else 0
            nm = 32 - m0
            n_first = 0 if t == 0 else 32 * t - 1
            for c in range(2):
                if "nocmp" in DBG:
                    break
                bank = gbank()
                for hp in range(2):
                    ch = c * 2 + hp
                    for j in range(32):
                        mm(psn[bank][:, hp * 32:(hp + 1) * 32], W1blk[:, c, j, :], CK[:, ch, j:j + 16 * 31 + 1:16], j == 0, j == 31,
                           ["W1blk", "CK"], [("ps", bank)])
                act(hidT[:], psn[bank][:, 0:64].rearrange("p (h m) -> p h m", m=32), AF.Silu, [("ps", bank), "pebias"], ["hidT"],
                    bias=pebias[:, c:c + 1])
                bank2 = gbank()
                if c == 0:
                    for hp in range(2):
                        mm(psn[bank2][:, hp * 32:(hp + 1) * 32], W2blk[:, 0, :], hidT[:, hp, :], True, True, ["W2blk", "hidT"], [("ps", bank2)])
                    for hp in range(2):
                        cp("act", KcT[:, hp, n_first:n_first + nm], psn[bank2][:, hp * 32 + m0:(hp + 1) * 32], [("ps", bank2)], ["KcT"])
                else:
                    for hp in range(2):
                        mm(psn[bank2][0:32, hp * 128:(hp + 1) * 128], hidT[:, hp, :], W2blk[:, 1, :], True, True, ["W2blk", "hidT"], [("ps", bank2)])
                    cp("act", vcst[:], psn[bank2][0:32, 0:256].rearrange("p (h d) -> p h d", d=64), [("ps", bank2)], ["vcst"])
                    done = 0
                    while done < nm:
                        if "novc" in DBG:
                            break
                        n = n_first + done
                        chn, pr = n // 128, n % 128
                        cnt = min(nm - done, 128 - pr)
                        dma("pool", Vc[pr:pr + cnt, chn, :, 0:64], vcst[m0 + done:m0 + done + cnt, :, :], ["vcst", "Vcinit"], ["Vc"], "vcsc")
                        done += cnt
        for r in range(2):
            dma("pool", o_conv_p[r, :].rearrange("(c p) -> p c", p=128), UC[:, :, r], [("UC", cc) for cc in range(NCC)], [("o_conv", r)],
                "oconv%d" % r, is_out=True, allow_slow_non_contiguous=True)
        S.barrier()

    if with_b:
      with ExitStack() as es:
        def sb(name, shape, dt=F32):
            return es.enter_context(nc.sbuf_tensor(name, list(shape), dt))
        wq_sb = sb("wq_sb", [128, 8, 1024], BF16)
        wgz_sb = sb("wgz_sb", [128, 8, 1072], BF16)
        wo_sb = sb("wo_sb", [128, 8, 1024], BF16)
        Eb = sb("Eb", [128, 4096], BF16)
        cf = sb("cf", [128, 384])
        nf_sb = sb("nf_sb", [128, D])
        aT2 = sb("aT2", [128, 8, ST], BF16)
        Hq = sb("Hq", [128, D])
        QT = sb("QT", [128, 8, ST], BF16)
        SZ = sb("SZ", [128, 4, D], BF16)
        Gt = sb("Gt", [128, 4, 48])
        PT = [sb("PT%d" % i, [128, 512], BF16) for i in range(3)]
        Pc = [sb("Pc%d" % i, [128, 512], BF16) for i in range(2)]
        O = sb("O", [128, D])
        Otmp = sb("Otmp", [128, 256])
        ozb = sb("ozb", [128, D], BF16)
        ozT = sb("ozT", [128, 8, 128], BF16)
        Yst = sb("Yst", [128, D])
        imp = sb("imp", [128, 4, 64])
        scr = sb("scr", [128, 4, 64])
        scr2 = sb("scr2", [128, 4, 64])
        mb = sb("mb", [128, 4, 64])
        mbb = sb("mbb", [128, 256], BF16)
        mT = sb("mT", [128, 2, 128], BF16)
        m8 = sb("m8", [128, 4, 16])
        rz = sb("rz", [128, 16])
        coef = sb("coef", [128, 3, 4])

        dma("sp", wq_sb[:], s_wq, [], ["wq"], "b_wq")
        dma("sp", wgz_sb[:], s_wgz, [], ["wgz"], "b_wgz")
        dma("sp", wo_sb[:], s_wo, [], ["wo"], "b_wo")
        dma("sp", Eb[:], s_eb, [], ["Eb"], "b_eb")
        dma("sp", cf[:], consts[:, C_MUL:C_MUL + 384], [], ["cf"], "b_cf")
        dma("sp", nf_sb[:], norm_f.partition_broadcast(128), [], ["nf"], "b_nf")

        sb_n = [0]

        def sbank():
            b = (0, 1, 2)[sb_n[0] % 3]
            sb_n[0] += 1
            return b
        pt_n = [0]

        def bc4(ap2d, p):
            return ap2d.rearrange("p (o q) -> p o q", o=1).to_broadcast([p, 4, 128])
        TRIB = bc4(cb[:, C_TRIB:C_TRIB + 128], 128)
        TRIU = bc4(cb[:, C_TRIU:C_TRIU + 128], 128)
        STAIR = bc4(cb[0:8, C_STAIR:C_STAIR + 128], 8)
        QTT = [("QT", s) for s in range(8)]
        PSO = {0: 5, 1: 6, 2: 7}

        for t in range(nst):
            r0 = t * ST
            dma("sp", aT2[:].rearrange("p k t -> p (k t)"), s_hnT[t], [], ["aT2"], "b_aT2")
            for s in range(8):
                bank = sbank()
                for kc in range(8):
                    mm(psn[bank][:], wq_sb[:, kc, s * 128:(s + 1) * 128], aT2[:, kc, :], kc == 0, kc == 7, ["wq", "aT2"], [("ps", bank)])
                cp("act" if s % 2 else "dve", QT[:, s, :], psn[bank][:], [("ps", bank)], [("QT", s)])
            for i in range(4):
                for piece, (c0, cw_) in enumerate(((0, 48), (48, 512), (560, 512))):
                    bank = sbank()
                    for kc in range(8):
                        mm(psn[bank][:, 0:cw_], aT2[:, kc, i * 128:(i + 1) * 128], wgz_sb[:, kc, c0:c0 + cw_], kc == 0, kc == 7,
                           ["wgz", "aT2"], [("ps", bank)])
                    if piece == 0:
                        act(Gt[:, i, :], psn[bank][:, 0:48], AF.Sigmoid, [("ps", bank)], [("Gt", i)])
                    else:
                        act(SZ[:, i, (piece - 1) * 512:piece * 512], psn[bank][:, 0:512], AF.Silu, [("ps", bank)], [("SZ", i)])

            for i in range(4):
                qt = 4 * t + i
                q0 = 128 * qt
                dma("sp", Hq[:], s_h[q0:q0 + 128, :], [], ["Hq"], "b_hq")
                ncm = min(8 * qt + 7, 255)
                chunks = [(0, min(128, ncm))]
                if ncm > 128:
                    chunks.append((128, ncm - 128))
                for kvh in range(4):
                    pb = (kvh % 2) * 64
                    hp = kvh // 2
                    Qr = QT[pb:pb + 64, hp * 4:hp * 4 + 4, i * 128:(i + 1) * 128]
                    for ci, (n0, nn) in enumerate(chunks):
                        bank = sbank()
                        off = 129 + n0 - 8 * qt
                        need_stair = (8 * qt + 6 >= n0) and (8 * qt - 1 < n0 + nn) and (0 <= off) and (off + nn <= 264)
                        mm(psn[bank][0:nn, :], KcT[pb:pb + 64, hp, n0:n0 + nn], Qr, True, not need_stair, ["KcT"] + QTT, [("ps", bank)])
                        if need_stair:
                            mm(psn[bank][0:nn, :], cb[0:8, C_ZSEL + off:C_ZSEL + off + nn], STAIR, False, True, ["cb"], [("ps", bank)])
                        act(Pc[ci][0:nn, :], psn[bank][0:nn, :], AF.Exp, [("ps", bank)], [("Pc", ci)], scale=SCALE)
                    for g in range(4):
                        for ci, (n0, nn) in enumerate(chunks):
                            mm(psn[5][:, g * 65:(g + 1) * 65], Pc[ci][0:nn, g * 128:(g + 1) * 128], Vc[0:nn, ci, kvh, :],
                               g == 0 and ci == 0, g == 3 and ci == len(chunks) - 1, [("Pc", ci), "Vc"], [("ps", 5)])
                    for g in range(4 if qt >= 8 else 0):
                        for ci, (n0, nn) in enumerate(chunks):
                            mm(psn[3][:, g * 64:(g + 1) * 64], Pc[ci][0:nn, g * 128:(g + 1) * 128], cb[0:nn, C_MEM + ci * 64:C_MEM + (ci + 1) * 64],
                               g == 0 and ci == 0, g == 3 and ci == len(chunks) - 1, [("Pc", ci), "cb"], [("ps", 3)])
                    zc = psn[5][:, 0:260].rearrange("p (g e) -> p g e", e=65)[:, :, 64]
                    ts("dve", rz[:, 0:4], zc, 1e-30, None, ALU.max, None, [("ps", 5)], [("rz", kvh, 0)])
                    S.op("dve", lambda e: e.reciprocal(rz[:, 0:4], rz[:, 0:4]), [("rz", kvh, 0)], [("rz", kvh, 0)])
                    if qt >= 8:
                        ts("dve", imp[:, kvh, :], psn[3][:, 0:64], rz[:, 0:1], None, ALU.mult, None, [("ps", 3), ("rz", kvh, 0)], [("imp", kvh)])
                    for g in range(1, 4 if qt >= 8 else 1):
                        stt("dve", imp[:, kvh, :], psn[3][:, g * 64:(g + 1) * 64], rz[:, g:g + 1], imp[:, kvh, :], ALU.mult, ALU.add,
                            [("ps", 3), ("rz", kvh, 0), ("imp", kvh)], [("imp", kvh)])
                    gview = Gt[:, i, kvh * 12:(kvh + 1) * 12].rearrange("p (g b) -> p b g", b=3)
                    tt("dve", coef[:, 0, :], rz[:, 0:4], gview[:, 0, :], ALU.mult, [("rz", kvh, 0), ("Gt", i)], [("coef", 0)])
                    ocv = psn[5][:, 0:260].rearrange("p (g e) -> p g e", e=65)[:, :, 0:64]
                    Ov = O[:, kvh * 256:(kvh + 1) * 256].rearrange("p (g d) -> p g d", d=64)
                    c0b = coef[:, 0, :].rearrange("p (g o) -> p g o", o=1).to_broadcast([128, 4, 64])
                    tt("dve", Ov, ocv, c0b, ALU.mult, [("ps", 5), ("coef", 0)], [("O", kvh)])
                if qt >= 8:
                    IMP = [("imp", k) for k in range(4)]
                    sl0 = 64 - 2 * qt
                    mulb = cf[:, sl0:sl0 + 64].rearrange("p (o j) -> p o j", o=1).to_broadcast([128, 4, 64])
                    addb = cf[:, 128 + sl0:128 + sl0 + 64].rearrange("p (o j) -> p o j", o=1).to_broadcast([128, 4, 64])
                    valb = cf[:, 256 + sl0:256 + sl0 + 64].rearrange("p (o j) -> p o j", o=1).to_broadcast([128, 4, 64])
                    tt("dve", scr[:], imp[:], mulb, ALU.mult, IMP + ["cf"], ["scr"])
                    tt("dve", scr[:], scr[:], addb, ALU.add, ["scr", "cf"], ["scr"])
                    if qt >= 1:
                        memset("dve", scr[:, :, 0:1], 3e9, ["scr"])
                    for kvh in range(4):
                        S.op("dve", lambda e, kvh=kvh: e.max(m8[:, kvh, 0:8], scr[:, kvh, :]), ["scr"], [("m8", kvh)])
                        S.op("dve", lambda e, kvh=kvh: e.match_replace(scr2[:, kvh, :], m8[:, kvh, 0:8], scr[:, kvh, :], -3e9),
                             ["scr", ("m8", kvh)], [("scr2", kvh)])
                        S.op("dve", lambda e, kvh=kvh: e.max(m8[:, kvh, 8:16], scr2[:, kvh, :]), [("scr2", kvh)], [("m8", kvh)])
                        ts("dve", mb[:, kvh, :], scr[:, kvh, :], m8[:, kvh, 15:16], None, ALU.is_ge, None, ["scr", ("m8", kvh)], [("mb", kvh)])
                    MB = [("mb", k) for k in range(4)]
                    tt("dve", mb[:], mb[:], valb, ALU.mult, MB + ["cf"], MB)
                    ts("dve", mbb[:], mb[:].rearrange("p k j -> p (k j)"), -1.0, BIG, ALU.add, ALU.mult, MB, ["mbb"])
                    for pr in range(2):
                        tr(ps_tr[:, pr * 128:(pr + 1) * 128], mbb[:, pr * 128:(pr + 1) * 128], ["mbb"], ["ps_tr"])
                    cp("dve", mT[:], ps_tr[:, 0:256].rearrange("p (a q) -> p a q", a=2), ["ps_tr"], ["mT"])
                for kvh in range(4):
                    pb = (kvh % 2) * 64
                    hp = kvh // 2
                    Qr = QT[pb:pb + 64, hp * 4:hp * 4 + 4, i * 128:(i + 1) * 128]
                    mTb = mT[pb:pb + 64, hp, :].rearrange("p (o q) -> p o q", o=1).to_broadcast([64, 4, 128])
                    for br in (1, 2):
                        kcs = list(range(0, qt + 1)) if br == 1 else list(range(max(0, qt - 4), qt + 1))
                        ob = PSO[br]
                        for kc in kcs:
                            bank = sbank()
                            diag = (kc == qt)
                            low = (br == 2 and kc == qt - 4)
                            use_mask = (br == 1 and qt >= 8)
                            last_simple = not (diag or low or use_mask)
                            mm(psn[bank][:], KT[br][pb:pb + 64, hp, kc * 128:(kc + 1) * 128], Qr, True, last_simple,
                               [("KT", br, kc // 4)] + QTT, [("ps", bank)])
                            if use_mask:
                                mm(psn[bank][:], Eb[pb:pb + 64, kc * 128:(kc + 1) * 128], mTb, False, not diag, ["Eb", "mT"], [("ps", bank)])
                            if diag:
                                mm(psn[bank][:], ident, TRIB, False, True, ["cb"], [("ps", bank)])
                            elif low:
                                mm(psn[bank][:], ident, TRIU, False, True, ["cb"], [("ps", bank)])
                            x = pt_n[0] % 3
                            pt_n[0] += 1
                            act(PT[x][:], psn[bank][:], AF.Exp, [("ps", bank)], [("PT", x)], scale=SCALE)
                            for g in range(4):
                                mm(psn[ob][:, g * 65:(g + 1) * 65], PT[x][:, g * 128:(g + 1) * 128], VV[br][:, kc, kvh, 0:65],
                                   g == 0 and kc == kcs[0], g == 3 and kc == kcs[-1], [("PT", x), ("V", br, kc)], [("ps", ob)])
                        zc = psn[ob][:, 0:260].rearrange("p (g e) -> p g e", e=65)[:, :, 64]
                        rzs = rz[:, 4 * br:4 * br + 4]
                        ts("dve", rzs, zc, 1e-30, None, ALU.max, None, [("ps", ob)], [("rz", kvh, br)])
                        S.op("dve", lambda e, rzs=rzs: e.reciprocal(rzs, rzs), [("rz", kvh, br)], [("rz", kvh, br)])
                        gview = Gt[:, i, kvh * 12:(kvh + 1) * 12].rearrange("p (g b) -> p b g", b=3)
                        tt("dve", coef[:, br, :], rzs, gview[:, br, :], ALU.mult, [("rz", kvh, br), ("Gt", i)], [("coef", br)])
                        ocv = psn[ob][:, 0:260].rearrange("p (g e) -> p g e", e=65)[:, :, 0:64]
                        cbb = coef[:, br, :].rearrange("p (g o) -> p g o", o=1).to_broadcast([128, 4, 64])
                        Ov = O[:, kvh * 256:(kvh + 1) * 256].rearrange("p (g d) -> p g d", d=64)
                        tt("dve", Otmp[:].rearrange("p (g d) -> p g d", d=64), ocv, cbb, ALU.mult, [("ps", ob), ("coef", br)], ["Otmp"])
                        tt("pool", O[:, kvh * 256:(kvh + 1) * 256], O[:, kvh * 256:(kvh + 1) * 256], Otmp[:], ALU.add, ["Otmp", ("O", kvh)], [("O", kvh)])
                OT = [("O", k) for k in range(4)]
                tt("pool", ozb[:], O[:], SZ[:, i, :], ALU.mult, OT + [("SZ", i)], ["ozb"])
                for kc in range(8):
                    tr(ps_tr[:, kc * 128:(kc + 1) * 128], ozb[:, kc * 128:(kc + 1) * 128], ["ozb"], ["ps_tr"])
                cp("dve", ozT[:], ps_tr[:].rearrange("p (k t) -> p k t", k=8), ["ps_tr"], ["ozT"])
                for half in range(2):
                    bank = sbank()
                    for kc in range(8):
                        mm(psn[bank][:], ozT[:, kc, :], wo_sb[:, kc, half * 512:(half + 1) * 512], kc == 0, kc == 7, ["ozT", "wo"], [("ps", bank)])
                    hv = Hq[:, half * 512:(half + 1) * 512]
                    tt("dve", hv, psn[bank][:], hv, ALU.add, [("ps", bank), "Hq"], ["Hq"])
                ssq = stat[:, 16:17]
                rstd = stat[:, 17:18]
                act(ozb[:], Hq[:], AF.Square, ["Hq"], ["ozb", "ssqf"], accum_out=ssq)
                act(ssq, ssq, AF.Sqrt, ["ssqf", "eps"], ["ssqf"], scale=1.0 / D, bias=eps_sb[:, 0:1])
                S.op("dve", lambda e, rstd=rstd, ssq=ssq: e.reciprocal(rstd, ssq), ["ssqf"], ["rstdf"])
                stt("dve", Yst[:], Hq[:], rstd, nf_sb[:], ALU.mult, ALU.mult, ["Hq", "rstdf", "nf"], ["Yst"])
                dma("pool", o_y_p[q0:q0 + 128, :], Yst[:], ["Yst"], [("o_y", qt)], "b_yst", is_out=True)
    S.emit()
    return nc


_CACHE = {}


def shared_inputs(inputs):
    f = lambda a: np.ascontiguousarray(np.asarray(a, dtype=np.float32))
    return {
        "norm_a": f(inputs["norm_a"][0]), "w_in": f(inputs["conv_w_in"][0]), "conv_w": f(inputs["conv_w"][0]),
        "w_out": f(inputs["conv_w_out"][0]), "norm_kv": f(inputs["norm_kv"]), "w_kv": f(inputs["w_kv"]),
        "cmp_pe": f(inputs["cmp_pe"]), "cmp_w1": f(inputs["cmp_w1"]), "cmp_w2": f(inputs["cmp_w2"]),
        "norm_b": f(inputs["norm_b"][0]), "w_qg": f(inputs["w_qg"][0]), "w_o": f(inputs["w_o"][0]),
        "norm_f": f(inputs["norm_f"]), "consts": make_consts(), "mems_in": make_mems(),
    }


def core_inputs(inputs, c, shared):
    f = lambda a: np.ascontiguousarray(np.asarray(a, dtype=np.float32))
    kvh, half = c % 4, c // 4
    bs = slice(SBC * half, SBC * half + SBC)
    m = dict(shared)
    m["x_p"] = f(inputs["x_prompt"][c])
    m["x_s"] = f(inputs["x_sample"][bs, 0, :])
    m["cconv_s"] = f(inputs["cache_conv"][0, bs])
    m["pool_cmp"] = f(inputs["cache_cmp_kv"][:, :, :, kvh, :]).reshape(5120 * 8, 2048)
    m["pool_sel"] = f(inputs["cache_sel_kv"][:, :, :, kvh, :]).reshape(5120 * 128, 128)
    m["win_s"] = f(inputs["cache_win_kv"][bs, :, :, kvh, :]).reshape(SBC, 512, 128)
    m["ptab_in"] = np.ascontiguousarray(np.asarray(inputs["page_table"][bs], dtype=np.int32))
    wqg = np.asarray(inputs["w_qg"][0], dtype=np.float32)
    hs = slice(kvh * 256, (kvh + 1) * 256)
    m["wqs_in"] = np.ascontiguousarray(np.concatenate([wqg[:, hs], wqg[:, 1024 + kvh * 12:1024 + (kvh + 1) * 12], wqg[:, 1072 + kvh * 256:1072 + (kvh + 1) * 256]], axis=1))
    wkv = np.asarray(inputs["w_kv"], dtype=np.float32).reshape(D, 3, 2, 4, 64)
    m["wkvs_in"] = np.ascontiguousarray(wkv[:, :, :, kvh, :].reshape(D, 384))
    m["wos_in"] = f(inputs["w_o"][0][hs, :])
    return m


def kernel(**inputs):
    if "nc" not in _CACHE:
        _CACHE["nc"] = build_program()
    nc = _CACHE["nc"]
    shared = shared_inputs(inputs)
    in_maps = [core_inputs(inputs, c, shared) for c in range(NCORES)]
    res = run_bass_kernel_spmd(nc, in_maps, core_ids=list(range(NCORES)))
    R = res.results
    B = NCORES
    y_prompt = np.stack([R[c]["o_y_p"] for c in range(B)])
    conv_p = np.stack([R[c]["o_conv_p"] for c in range(B)])[None]
    cmp_p = np.stack([R[c]["o_cmp_p"] for c in range(B)]).reshape(B, SEQ, 2, 4, 64)
    sel_p = np.stack([R[c]["o_sel_p"] for c in range(B)]).reshape(B, SEQ, 2, 4, 64)
    win_p = np.stack([R[c]["o_win_p"] for c in range(B)]).reshape(B, 512, 2, 4, 64)
    y_sample = np.zeros((DEC_B, 1, D), np.float32)
    conv_s = np.zeros((1, DEC_B, 2, DC), np.float32)
    cmp_s = np.zeros((DEC_B, 1, 2, 4, 64), np.float32)
    sel_s = np.zeros((DEC_B, 1, 2, 4, 64), np.float32)
    win_s = np.zeros((DEC_B, 512, 2, 4, 64), np.float32)
    for c in range(B):
        kvh, half = c % 4, c // 4
        bs = slice(SBC * half, SBC * half + SBC)
        kv = np.asarray(R[c]["o_kv_s"]).reshape(SBC, 3, 2, 64)
        cmp_s[bs, 0, :, kvh, :] = kv[:, 0]
        sel_s[bs, 0, :, kvh, :] = kv[:, 1]
        win_s[bs, :, :, kvh, :] = np.asarray(R[c]["o_win_s"]).reshape(SBC, 512, 2, 64)
        if kvh == 0:
            y_sample[bs, 0, :] = np.asarray(R[c]["o_y_s"])
            conv_s[0, bs] = np.asarray(R[c]["o_conv_s"])
    return (y_prompt, y_sample, conv_p, conv_s, cmp_p, cmp_s, sel_p, sel_s, win_p, win_s)
```
